# Optimizing a Trainium2 kernel written in Bass

```python
import math
import jax, jax.numpy as jnp
from jax import lax
import numpy as np

D_MODEL = 1024
BATCH = 8
SEQ = 2048
DEPTH = 1

GLA_WIDTH = D_MODEL // 2
DIFF_WIDTH = D_MODEL - GLA_WIDTH
GLA_HEADS = 4
GLA_DV = GLA_WIDTH // GLA_HEADS
GLA_DK = GLA_DV // 2
GLA_QK = GLA_HEADS * GLA_DK
GLA_GATE_RANK = 16
GLA_GATE_NORM = 16.0
GLA_CHUNK = 64
DIFF_HEADS = 4
DIFF_DV = DIFF_WIDTH // DIFF_HEADS
DIFF_DQK = DIFF_DV // 2
DIFF_QK = DIFF_HEADS * 2 * DIFF_DQK
Q_BLOCK = 128
ROPE_THETA = 10000.0
D_FF = 4 * D_MODEL
EPS = 1e-6
N_ADA = 6
PROJ_SIZES = (GLA_QK, GLA_QK, GLA_WIDTH, GLA_GATE_RANK, GLA_WIDTH, DIFF_QK, DIFF_QK, DIFF_WIDTH)
PROJ_WIDTH = sum(PROJ_SIZES)

kernel_name = "hybrid_gla_diffattn_parallel_block"


def rms_norm(t, g):
    tf = t.astype(jnp.float32)
    y = tf * lax.rsqrt(jnp.mean(tf * tf, axis=-1, keepdims=True) + EPS)
    return (y * g.astype(jnp.float32)).astype(t.dtype)


def rope_tables(positions, dim):
    inv_freq = 1.0 / (ROPE_THETA ** (jnp.arange(0, dim, 2, dtype=jnp.float32) / dim))
    ang = positions.astype(jnp.float32)[..., None] * inv_freq
    return jnp.cos(ang), jnp.sin(ang)


def apply_rope(t, cos, sin):
    half = t.shape[-1] // 2
    t1 = t[..., :half].astype(jnp.float32)
    t2 = t[..., half:].astype(jnp.float32)
    return jnp.concatenate([t1 * cos - t2 * sin, t2 * cos + t1 * sin], axis=-1).astype(t.dtype)


def to_heads(t, n_heads):
    b, s, _ = t.shape
    return t.reshape(b, s, n_heads, -1).transpose(0, 2, 1, 3)


def from_heads(t):
    b, h, s, d = t.shape
    return t.transpose(0, 2, 1, 3).reshape(b, s, h * d)


def gla_chunked(q, k, v, g_log):
    bsz, nh, seq, dk = q.shape
    dv = v.shape[-1]
    n_chunks = seq // GLA_CHUNK

    def chunked(t):
        return t.astype(jnp.float32).reshape(bsz, nh, n_chunks, GLA_CHUNK, t.shape[-1]).transpose(2, 0, 1, 3, 4)

    causal = jnp.tril(jnp.ones((GLA_CHUNK, GLA_CHUNK), dtype=bool))[:, :, None]

    def step(state, inp):
        qc, kc, vc, gc = inp
        b = jnp.cumsum(gc, axis=-2)
        o_inter = jnp.einsum('bhcd,bhde->bhce', qc * jnp.exp(b), state)
        rel = b[:, :, :, None, :] - b[:, :, None, :, :]
        decay = jnp.exp(jnp.where(causal, rel, -jnp.inf))
        scores = jnp.einsum('bhid,bhjd,bhijd->bhij', qc, kc, decay)
        o_intra = jnp.einsum('bhij,bhje->bhie', scores, vc)
        b_last = b[:, :, -1:, :]
        new_state = jnp.exp(b_last[:, :, 0, :])[..., None] * state + \
            jnp.einsum('bhcd,bhce->bhde', kc * jnp.exp(b_last - b), vc)
        return new_state, o_inter + o_intra

    state0 = jnp.zeros((bsz, nh, dk, dv), jnp.float32)
    _, o = lax.scan(step, state0, (chunked(q), chunked(k), chunked(v), chunked(g_log)))
    return o.transpose(1, 2, 0, 3, 4).reshape(bsz, nh, seq, dv).astype(v.dtype)


def diff_attention(q, k, v, lam):
    seq = q.shape[3]
    scale = DIFF_DQK ** -0.5
    vf = v.astype(jnp.float32)
    outs = []
    for blk in range(seq // Q_BLOCK):
        q0, q1 = blk * Q_BLOCK, (blk + 1) * Q_BLOCK
        qb = q[:, :, :, q0:q1]
        kb = k[:, :, :, :q1]
        s = jnp.einsum('bhmqd,bhmkd->bhmqk', qb, kb).astype(jnp.float32) * scale
        mask = (q0 + jnp.arange(Q_BLOCK))[:, None] >= jnp.arange(q1)[None, :]
        p = jax.nn.softmax(jnp.where(mask, s, -jnp.inf), axis=-1)
        p_diff = p[:, :, 0] - lam * p[:, :, 1]
        outs.append(jnp.einsum('bhqk,bhkd->bhqd', p_diff, vf[:, :, :q1]))
    return jnp.concatenate(outs, axis=2).astype(v.dtype)


def hybrid_mixer(h, cos5, sin5, w_in, gate_w, gate_b, gla_norm, lq1, lk1, lq2, lk2,
                 diff_norm, w_out, lambda_init):
    bsz, seq, _ = h.shape
    proj = h @ w_in
    offsets = np.cumsum(PROJ_SIZES)[:-1].tolist()
    g_q, g_k, g_v, g_lr, g_og, d_q, d_k, d_v = jnp.split(proj, offsets, axis=-1)

    gq = to_heads(g_q, GLA_HEADS) * (GLA_DK ** -0.5)
    gk = to_heads(g_k, GLA_HEADS)
    gv = to_heads(g_v, GLA_HEADS)
    g_log = jax.nn.log_sigmoid((g_lr @ gate_w + gate_b).astype(jnp.float32)) / GLA_GATE_NORM
    g_log = to_heads(g_log, GLA_HEADS)
    go = gla_chunked(gq, gk, gv, g_log)
    go = from_heads(rms_norm(go, gla_norm)) * jax.nn.silu(g_og)

    def qk_heads(t):
        return t.reshape(bsz, seq, DIFF_HEADS, 2, DIFF_DQK).transpose(0, 2, 3, 1, 4)
    dq = apply_rope(qk_heads(d_q), cos5, sin5)
    dk = apply_rope(qk_heads(d_k), cos5, sin5)
    dv = to_heads(d_v, DIFF_HEADS)
    lam = jnp.exp(jnp.sum(lq1.astype(jnp.float32) * lk1.astype(jnp.float32))) \
        - jnp.exp(jnp.sum(lq2.astype(jnp.float32) * lk2.astype(jnp.float32))) + lambda_init
    do = diff_attention(dq, dk, dv, lam)
    do = from_heads(rms_norm(do, diff_norm) * (1.0 - lambda_init))

    return jnp.concatenate([go, do], axis=-1) @ w_out


def setup_inputs(seed: int = 0) -> dict:
    key = jax.random.key(seed)
    ks = jax.random.split(key, 24)
    f32 = jnp.float32

    def nrm(k, shape, scale):
        return jax.random.normal(k, shape, f32) * scale

    def gain(k, shape):
        return 1.0 + 0.02 * jax.random.normal(k, shape, f32)

    offsets = jax.random.randint(ks[2], (BATCH, 1), 0, 1024, dtype=jnp.int32)
    positions = offsets + jnp.arange(SEQ, dtype=jnp.int32)[None, :]
    return {
        "x": nrm(ks[0], (BATCH, SEQ, D_MODEL), 1.0),
        "c": nrm(ks[1], (BATCH, D_MODEL), 1.0),
        "positions": positions,
        "ada_w": nrm(ks[3], (DEPTH, D_MODEL, N_ADA * D_MODEL), D_MODEL ** -0.5),
        "ada_b": nrm(ks[4], (DEPTH, N_ADA * D_MODEL), 0.02),
        "pre_norm_mix": gain(ks[5], (DEPTH, D_MODEL)),
        "post_norm_mix": gain(ks[6], (DEPTH, D_MODEL)),
        "w_in": nrm(ks[7], (DEPTH, D_MODEL, PROJ_WIDTH), D_MODEL ** -0.5),
        "gla_gate_w": nrm(ks[8], (DEPTH, GLA_GATE_RANK, GLA_QK), GLA_GATE_RANK ** -0.5),
        "gla_gate_b": nrm(ks[9], (DEPTH, GLA_QK), 0.1),
        "gla_norm": gain(ks[10], (DEPTH, GLA_DV)),
        "lambda_q1": nrm(ks[11], (DEPTH, DIFF_DQK), 0.1),
        "lambda_k1": nrm(ks[12], (DEPTH, DIFF_DQK), 0.1),
        "lambda_q2": nrm(ks[13], (DEPTH, DIFF_DQK), 0.1),
        "lambda_k2": nrm(ks[14], (DEPTH, DIFF_DQK), 0.1),
        "diff_norm": gain(ks[15], (DEPTH, DIFF_DV)),
        "w_out": nrm(ks[16], (DEPTH, D_MODEL, D_MODEL), D_MODEL ** -0.5),
        "pre_norm_mlp": gain(ks[17], (DEPTH, D_MODEL)),
        "post_norm_mlp": gain(ks[18], (DEPTH, D_MODEL)),
        "w_up": nrm(ks[19], (DEPTH, D_MODEL, D_FF), D_MODEL ** -0.5),
        "w_down": nrm(ks[20], (DEPTH, D_FF, D_MODEL), D_FF ** -0.5),
    }


def reference(x, c, positions, ada_w, ada_b, pre_norm_mix, post_norm_mix, w_in, gla_gate_w,
              gla_gate_b, gla_norm, lambda_q1, lambda_k1, lambda_q2, lambda_k2, diff_norm,
              w_out, pre_norm_mlp, post_norm_mlp, w_up, w_down):
    cos, sin = rope_tables(positions, DIFF_DQK)
    cos5, sin5 = cos[:, None, None], sin[:, None, None]
    c_act = jax.nn.silu(c)
    for l in range(DEPTH):
        lambda_init = 0.8 - 0.6 * math.exp(-0.3 * l)
        ada = c_act @ ada_w[l] + ada_b[l]
        sh_a, sc_a, gt_a, sh_m, sc_m, gt_m = [a[:, None, :] for a in jnp.split(ada, N_ADA, axis=-1)]

        h = rms_norm(x, pre_norm_mix[l]) * (1.0 + sc_a) + sh_a
        y = hybrid_mixer(h, cos5, sin5, w_in[l], gla_gate_w[l], gla_gate_b[l], gla_norm[l],
                         lambda_q1[l], lambda_k1[l], lambda_q2[l], lambda_k2[l], diff_norm[l],
                         w_out[l], lambda_init)
        x = x + gt_a * rms_norm(y, post_norm_mix[l])

        h = rms_norm(x, pre_norm_mlp[l]) * (1.0 + sc_m) + sh_m
        y = jnp.square(jax.nn.relu(h @ w_up[l])) @ w_down[l]
        x = x + gt_m * rms_norm(y, post_norm_mlp[l])
    return x
```

```python
import math
import numpy as np
from contextlib import ExitStack
import concourse.bass as bass
import concourse.mybir as mybir
from concourse.bass_utils import run_bass_kernel_spmd

F32 = mybir.dt.float32
BF16 = mybir.dt.bfloat16
I32 = mybir.dt.int32
ALU = mybir.AluOpType
AF = mybir.ActivationFunctionType
AX = mybir.AxisListType

ENGS = ("pe", "act", "dve", "pool", "sp")

S_LEN = 2048
D = 1024
NT = 16
DFF = 4096
PW = 3088
EPS = 1e-6
LAMBDA_INIT = 0.8 - 0.6 * math.exp(0.0)


class Buf:
    __slots__ = ("name", "w", "r")

    def __init__(self, name):
        self.name = name
        self.w = None
        self.r = {}


class Sched:
    def __init__(self, same_engine_sync=True):
        self.ops = {e: [] for e in ENGS}
        self.dma_vals = []
        self.same_engine_sync = same_engine_sync

    def new_dma_sem(self):
        self.dma_vals.append(0)
        return len(self.dma_vals) - 1

    def fence(self):
        evs = set()
        for e in ENGS:
            for i in range(len(self.ops[e]) - 1, -1, -1):
                o = self.ops[e][i]
                if o["fn"] is not None and o["dma"] is None:
                    evs.add(("e", e, i))
                    break
        for i, v in enumerate(self.dma_vals):
            if v:
                evs.add(("d", i, v))
        return evs

    def op(self, eng, fn, reads=(), writes=(), dma_sem=None, extra=None):
        deps = set()
        for b in reads:
            if b.w is not None:
                deps.add(b.w)
        for b in writes:
            if b.w is not None:
                deps.add(b.w)
            for ev in b.r.values():
                if not (ev[0] == "e" and ev[1] == eng and dma_sem is None and (eng == "pe" or not self.same_engine_sync)):
                    deps.add(ev)
        idx = len(self.ops[eng])
        if dma_sem is None:
            ev = ("e", eng, idx)
        else:
            self.dma_vals[dma_sem] += 16
            ev = ("d", dma_sem, self.dma_vals[dma_sem])
        for b in reads:
            b.r[(ev[0], ev[1])] = ev
        for b in writes:
            b.w = ev
            b.r = {}
        if dma_sem is None:
            if eng == "pe" or not self.same_engine_sync:
                deps = {d for d in deps if not (d[0] == "e" and d[1] == eng)}
        if extra:
            for d in extra:
                if d[0] == "e" and d[1] == eng and (eng == "pe" or d[2] >= idx):
                    continue
                deps.add(d)
        self.ops[eng].append(dict(fn=fn, deps=deps, dma=dma_sem))
        return ev

    def wait_all(self, eng, events):
        self.ops[eng].append(dict(fn=None, deps=set(e for e in events if e is not None), dma=None))

    def emit(self, nc, es):
        esems = {e: es.enter_context(nc.semaphore("s_" + e)) for e in ENGS}
        dsems = [es.enter_context(nc.semaphore("d%d" % i)) for i in range(len(self.dma_vals))]
        sig = {e: set() for e in ENGS}
        for e in ENGS:
            for o in self.ops[e]:
                for d in o["deps"]:
                    if d[0] == "e":
                        sig[d[1]].add(d[2])
        cnt = {}
        for e in ENGS:
            c = 0
            arr = []
            for i in range(len(self.ops[e])):
                if i in sig[e]:
                    c += 1
                arr.append(c)
            cnt[e] = arr
        stats = {e: [len(self.ops[e]), len(sig[e]), 0] for e in ENGS}
        block = es.enter_context(nc.Block())

        def run(ename, h):
            waited = {}
            for i, o in enumerate(self.ops[ename]):
                for d in sorted(o["deps"]):
                    if d[0] == "e":
                        key, val, sem = ("e", d[1]), cnt[d[1]][d[2]], esems[d[1]]
                    else:
                        key, val, sem = ("d", d[1]), d[2], dsems[d[1]]
                    if waited.get(key, 0) >= val:
                        continue
                    waited[key] = val
                    h.wait_ge(sem, val)
                    stats[ename][2] += 1
                if o["fn"] is None:
                    continue
                ins = o["fn"](h)
                if o["dma"] is not None:
                    ins.then_inc(dsems[o["dma"]], 16)
                elif i in sig[ename]:
                    ins.then_inc(esems[ename], 1)

        @block.tensor
        def _(h):
            run("pe", h)

        @block.scalar
        def _(h):
            run("act", h)

        @block.vector
        def _(h):
            run("dve", h)

        @block.gpsimd
        def _(h):
            run("pool", h)

        @block.sync
        def _(h):
            run("sp", h)

        return stats


class _Stop(Exception):
    pass


def build_program(debug_phase=99, n_p1=NT, p1_stop=999):
    nc = bass.Bass("TRN2", target_bir_lowering=False)

    def din(name, shape, dt=F32):
        return nc.dram_tensor(name, list(shape), dt, kind="ExternalInput").ap()

    x_d = din("x", [S_LEN, D])
    c8_d = din("c8", [128, 8])
    pos_d = din("pos", [128, NT], I32)
    adaw_d = din("ada_w", [D, 6 * D])
    adab_d = din("ada_b", [1, 6 * D])
    gpre_mix_d = din("pre_norm_mix", [1, D])
    gpost_mix_d = din("post_norm_mix", [1, D])
    gpre_mlp_d = din("pre_norm_mlp", [1, D])
    gpost_mlp_d = din("post_norm_mlp", [1, D])
    win_d = din("w_in", [D, PW])
    gatew_d = din("gla_gate_w", [16, 256])
    gateb_d = din("gla_gate_b", [1, 256])
    glanorm_d = din("gla_norm", [1, 128])
    lq1_d = din("lambda_q1", [1, 64])
    lk1_d = din("lambda_k1", [1, 64])
    lq2_d = din("lambda_q2", [1, 64])
    lk2_d = din("lambda_k2", [1, 64])
    dnorm_d = din("diff_norm", [1, 128])
    wout_d = din("w_out", [D, D])
    wup_d = din("w_up", [D, DFF])
    wdown_d = din("w_down", [DFF, D])
    ident_d = din("k_ident", [128, 128])
    tri_d = din("k_tri", [128, 128])
    negm_d = din("k_negmask", [128, 128])
    invf_d = din("k_invf", [1, 64])
    out_d = nc.dram_tensor("out", [S_LEN, D], F32, kind="ExternalOutput").ap()

    S = Sched()
    es = ExitStack()
    with es:
        AW = 53200
        arena = es.enter_context(nc.sbuf_tensor("arena", [128, AW], F32))
        A = arena[:]
        psum2 = [es.enter_context(nc.psum_tensor("ps%d" % i, [128, 1024], F32)) for i in range(4)]
        psum = [psum2[i // 2][:, (i % 2) * 512:(i % 2 + 1) * 512] for i in range(8)]
        PB = [Buf("pb%d" % i) for i in range(8)]

        def f32v(off, cols, rows=None):
            assert off % 4 == 0 and off + cols * 4 <= AW * 4, (off, cols)
            v = A[:, off // 4: off // 4 + cols]
            return v if rows is None else v[rows[0]:rows[1]]

        def bf16v(off, cols):
            assert off % 4 == 0 and cols % 2 == 0 and off + cols * 2 <= AW * 4, (off, cols)
            return A[:, off // 4: off // 4 + cols // 2].bitcast(BF16)

        def pbf(i):
            return psum[i][:].bitcast(BF16)

        o = 0
        identb = bf16v(0, 128)
        identf = f32v(256, 128)
        tri = f32v(768, 128)
        onesf = f32v(1280, 128)
        negmb = bf16v(1792, 128)
        gnorm4 = f32v(2048, 512)
        dnormbc = f32v(4096, 128)
        sincos = f32v(4608, 1024).rearrange("p (t j) -> p t j", t=NT)
        gw = f32v(8704, 256)
        SM = 9728
        c8 = f32v(SM, 8)
        cact = f32v(SM + 32, 8)
        neglam = f32v(SM + 64, 1)
        posf = f32v(SM + 128, 16)
        posi = f32v(SM + 192, 16).bitcast(I32)
        lamt = f32v(SM + 256, 8)
        epsb = f32v(SM + 288, 1)
        gncol = f32v(SM + 296, 1)
        R_MIX = 10240
        mixT = bf16v(R_MIX, 8 * S_LEN).rearrange("p (k n) -> p k n", k=8)
        R_BC = 43008
        gmod_a = f32v(R_BC, D)
        sh_a = f32v(R_BC + 4096, D)
        gpost_a = f32v(R_BC + 8192, D)
        gmod_m = f32v(R_BC + 12288, D)
        sh_m = f32v(R_BC + 16384, D)
        gpost_m = f32v(R_BC + 20480, D)
        R_A = 67584
        w_in = bf16v(R_A, 8 * PW).rearrange("p (k n) -> p k n", k=8)
        PT = [bf16v(R_A + i * 16384, 16 * 512).rearrange("p (j n) -> p j n", j=16) for i in range(2)]
        R_B = 116992
        qT = bf16v(R_B, 4 * S_LEN).rearrange("p (h n) -> p h n", h=4)
        kT = bf16v(R_B + 16384, 4 * S_LEN).rearrange("p (h n) -> p h n", h=4)
        vaug = bf16v(R_B + 32768, NT * 4 * 130).rearrange("p (t h e) -> p t h e", t=NT, h=4)
        R_C = R_B + 32768 + NT * 4 * 130 * 2
        assert R_C == 166400
        ada_ring = [f32v(R_B + i * 16384, 8 * 512).rearrange("p (k n) -> p k n", k=8) for i in range(2)]
        cb = f32v(R_B + 32768, 8 * 128).rearrange("p (k n) -> p k n", k=8)
        adab = f32v(R_MIX, 6 * D)
        angs = f32v(R_MIX + 24576, 1024).rearrange("p (t j) -> p t j", t=NT)
        angk = f32v(R_MIX + 28672, 1024).rearrange("p (t j) -> p t j", t=NT)
        angi = f32v(R_C, 1024).bitcast(I32).rearrange("p (t j) -> p t j", t=NT)
        invfbc = f32v(R_C + 4096, 64)
        lamv = f32v(R_C + 4608, 256)
        o = R_BC + 8192
        xt = [f32v(o, D), f32v(o + 4096, D)]; o += 8192
        h1 = f32v(o, D); o += 4096
        hb2 = [bf16v(o, D), None]; o += 2048
        junk = bf16v(o, D); o += 2048
        assert o == R_BC + 24576
        o = R_C
        hTt2 = [bf16v(o + i * 2048, D).rearrange("p (k n) -> p k n", k=8) for i in range(2)]; o += 4096
        glrT = f32v(o, 128); o += 512
        ez = f32v(o, 256); o += 1024
        spz = f32v(o, 256); o += 1024
        eb2 = [f32v(o + i * 1024, 256) for i in range(2)]; o += 2048
        enb2 = [f32v(o + i * 1024, 256) for i in range(2)]; o += 2048
        gqk2 = [f32v(o + i * 2048, 512) for i in range(2)]; o += 4096
        qg = bf16v(o, 256); o += 512
        kg = bf16v(o, 256); o += 512
        qkT = bf16v(o, 512).rearrange("p (a n) -> p a n", a=4); o += 1024
        sT = bf16v(o, 512).rearrange("p (a n) -> p a n", a=4); o += 1024
        vg2 = [bf16v(o + i * 1024, 512) for i in range(2)]; o += 2048
        gate2 = [f32v(o + i * 2048, 512) for i in range(2)]; o += 4096
        mixg = bf16v(o, 512); o += 1024
        Sf = f32v(o, 256).rearrange("p (a n) -> p a n", a=2); o += 1024
        Sb = bf16v(o, 256).rearrange("p (a n) -> p a n", a=2); o += 512
        Stmp = f32v(o, 256).rearrange("p (a n) -> p a n", a=2); o += 1024
        ropeA = f32v(o, 512); o += 2048
        ropeB = f32v(o, 512); o += 2048
        ropeAk = f32v(o, 512); o += 2048
        ropeBk = f32v(o, 512); o += 2048
        qr2 = [bf16v(o + i * 1024, 512) for i in range(2)]; o += 2048
        kr2 = [bf16v(o + i * 1024, 512) for i in range(2)]; o += 2048
        hb2[1] = bf16v(o, D); o += 2048
        assert o <= AW * 4, o
        dec = f32v(SM + 320, 2)
        ssq4 = f32v(SM + 336, 4)
        rstd4 = f32v(SM + 352, 4)
        ssum = f32v(SM + 368, 2)
        rstd = f32v(SM + 376, 2)
        rden = f32v(SM + 384, 4)
        rden2 = f32v(SM + 400, 4)
        ssum2 = f32v(SM + 416, 4)
        rstd2 = f32v(SM + 432, 4)
        ssumA = f32v(SM + 448, 2)
        rstdA = f32v(SM + 456, 2)
        dec2 = [f32v(SM + 464 + i * 8, 2) for i in range(2)]
        ada_ring2 = [bf16v(100352 + i * 4096, 8 * 256).rearrange("p (k n) -> p k n", k=8) for i in range(2)]
        adab2 = [bf16v(R_C + i * 512, 256) for i in range(2)]
        cb2 = bf16v(R_C + 2048, 8 * 128).rearrange("p (k n) -> p k n", k=8)
        onesb = bf16v(R_C + 1024, 128)
        W_OUT = 182272
        w_out = bf16v(W_OUT, 8 * D).rearrange("p (k n) -> p k n", k=8)
        o = 198656
        o1 = f32v(o, 512).rearrange("p (r n) -> p r n", r=4); o += 2048
        od = f32v(o, 128); o += 512
        mixd = bf16v(o, 128); o += 256
        p2junk = bf16v(o, 128); o += 256
        p2sq = f32v(o, 128); o += 512
        od4 = [od] + [f32v(o + i * 512, 128) for i in range(3)]; o += 1536
        mixd4 = [mixd] + [bf16v(o + i * 256, 128) for i in range(3)]; o += 768
        p3x = [f32v(o, D), f32v(o + 4096, D)]; o += 8192
        assert o <= AW * 4, o
        p3u = f32v(166400, D)
        p3h1 = f32v(166400 + 4096, D)
        p3hb = bf16v(166400 + 8192, D)
        p3junk = bf16v(166400 + 10240, D)
        p3hb2 = [p3hb, bf16v(166400 + 12288, D)]
        W_UP = 67584
        w_up = bf16v(W_UP, 8 * DFF).rearrange("p (k n) -> p k n", k=8)
        W_DN = W_UP + 65536
        w_dn = bf16v(W_DN, 32 * D).rearrange("p (f n) -> p f n", f=32)
        assert W_DN + 65536 == 198656
        hidT = bf16v(R_BC, 32 * 256).rearrange("p (f n) -> p f n", f=32)
        xg = f32v(198656, 2 * D).rearrange("p (a n) -> p a n", a=2)
        p4u = f32v(198656 + 8192, D)
        assert 198656 + 8192 + 4096 <= AW * 4
        RT = [f32v(R_BC + 16384 + i * 1024, 256) for i in range(2)]
        p4junk = bf16v(R_BC + 16384 + 2048, 512)

        def op(eng, fn, r=(), w=(), **kw):
            return S.op(eng, fn, reads=r, writes=w, **kw)

        dsem = {k: S.new_dma_sem() for k in ["const", "consta", "constp"] + ["win%d" % i for i in range(7)] + [ "ada0", "ada1", "x0", "x1", "wout", "ffnu", "o0", "o1",
                                             "xm0", "xm1", "xg", "g2", "ada2_0", "ada2_1", "adb2_0", "adb2_1", "xm2", "os0", "os1", "os2"] + ["ffnd%d" % g for g in range(8)]}
        B = {}

        def bf(name):
            if name not in B:
                B[name] = Buf(name)
            return B[name]

        def rstd_from(sum_ap, out_ap, n, sbuf, wbuf, scale):
            op("dve", lambda e: e.tensor_scalar(out=out_ap, in0=sum_ap, scalar1=scale, scalar2=EPS, op0=ALU.mult, op1=ALU.add),
               r=[sbuf], w=[wbuf])
            op("act", lambda e: e.activation(out=out_ap, in_=out_ap, func=AF.Ln), r=[wbuf], w=[wbuf])
            op("act", lambda e: e.activation(out=out_ap, in_=out_ap, func=AF.Exp, scale=-0.5), r=[wbuf], w=[wbuf])

        CB = bf("consts")
        CBa = bf("consts_a")
        for dst, src in [(identf, ident_d), (tri, tri_d)]:
            op("sp", lambda e, dst=dst, src=src: e.dma_start(out=dst, in_=src), w=[CB], dma_sem=dsem["const"])
        op("pool", lambda e: e.dma_start(out=negmb, in_=negm_d), w=[bf("negmb")], dma_sem=dsem["constp"])
        op("sp", lambda e: e.dma_start(out=c8, in_=c8_d), w=[CB], dma_sem=dsem["const"])
        op("act", lambda e: e.dma_start(out=posi, in_=pos_d), w=[CBa], dma_sem=dsem["consta"])
        op("sp", lambda e: e.dma_start(out=adab[0:1, :], in_=adab_d), w=[CB], dma_sem=dsem["const"])
        op("sp", lambda e: e.dma_start(out=gmod_a, in_=gpre_mix_d.partition_broadcast(128)), w=[CB], dma_sem=dsem["const"])
        op("act", lambda e: e.dma_start(out=dnormbc, in_=dnorm_d.partition_broadcast(128)), w=[CBa], dma_sem=dsem["consta"])
        op("act", lambda e: e.dma_start(out=gncol, in_=glanorm_d.rearrange("o d -> d o")), w=[CBa], dma_sem=dsem["consta"])
        op("act", lambda e: e.dma_start(out=invfbc, in_=invf_d.partition_broadcast(128)), w=[CBa], dma_sem=dsem["consta"])
        op("sp", lambda e: e.dma_start(out=gw[0:16, :], in_=gatew_d), w=[CB], dma_sem=dsem["const"])
        op("sp", lambda e: e.dma_start(out=gw[16:17, :], in_=gateb_d), w=[CB], dma_sem=dsem["const"])
        for i, src in enumerate([lq1_d, lk1_d, lq2_d, lk2_d]):
            op("act", lambda e, i=i, src=src: e.dma_start(out=lamv[0:1, i * 64:(i + 1) * 64], in_=src), w=[CBa], dma_sem=dsem["consta"])
        WIN_BLOCKS = [(1024, 16), (0, 512), (1552, 512), (2064, 512), (512, 512), (1040, 512), (2576, 512)]
        WINB = {}
        for i, (c0, ncols) in enumerate(WIN_BLOCKS):
            WINB[c0] = bf("w_in_%d" % c0)
            op("pool", lambda e, c0=c0, ncols=ncols: e.dma_start(out=w_in[:, :, c0:c0 + ncols],
                                                                  in_=win_d[:, c0:c0 + ncols].rearrange("(k p) n -> p k n", p=128)),
               r=[CB, CBa], w=[WINB[c0]], dma_sem=dsem["win%d" % i])

        op("dve", lambda e: e.memset(onesf, 1.0), w=[bf("ones")])
        op("dve", lambda e: e.memset(epsb, EPS), w=[bf("epsb")])
        op("dve", lambda e: e.tensor_copy(out=identb, in_=identf), r=[CB, CBa], w=[bf("identb")])
        CA = bf("cact")
        op("act", lambda e: e.activation(out=cact, in_=c8, func=AF.Exp, scale=-1.0), r=[CB, CBa], w=[CA])
        op("dve", lambda e: e.tensor_scalar(out=cact, in0=cact, scalar1=1.0, scalar2=None, op0=ALU.add), r=[CA], w=[CA])
        op("dve", lambda e: e.reciprocal(out=cact, in_=cact), r=[CA], w=[CA])
        op("dve", lambda e: e.tensor_tensor(out=cact, in0=cact, in1=c8, op=ALU.mult), r=[CA, CB, CBa], w=[CA])
        CBB = bf("cb")
        for k in range(8):
            op("dve", lambda e, k=k: e.tensor_scalar(out=cb[:, k, :], in0=onesf, scalar1=cact[:, k:k + 1], scalar2=None, op0=ALU.mult),
               r=[CA, bf("ones")], w=[CBB])
        LB = bf("lam")
        op("dve", lambda e: e.tensor_tensor(out=lamv[0:1, 0:64], in0=lamv[0:1, 0:64], in1=lamv[0:1, 64:128], op=ALU.mult), r=[CB, CBa], w=[LB])
        op("dve", lambda e: e.tensor_tensor(out=lamv[0:1, 128:192], in0=lamv[0:1, 128:192], in1=lamv[0:1, 192:256], op=ALU.mult), r=[LB], w=[LB])
        op("dve", lambda e: e.reduce_sum(out=lamt[0:1, 0:1], in_=lamv[0:1, 0:64], axis=AX.X), r=[LB], w=[LB])
        op("dve", lambda e: e.reduce_sum(out=lamt[0:1, 1:2], in_=lamv[0:1, 128:192], axis=AX.X), r=[LB], w=[LB])
        op("act", lambda e: e.activation(out=lamt[0:1, 0:2], in_=lamt[0:1, 0:2], func=AF.Exp), r=[LB], w=[LB])
        op("dve", lambda e: e.scalar_tensor_tensor(out=lamt[0:1, 2:3], in0=lamt[0:1, 1:2], scalar=-LAMBDA_INIT, in1=lamt[0:1, 0:1],
                                                   op0=ALU.add, op1=ALU.subtract), r=[LB], w=[LB])
        op("pe", lambda e: e.matmul(psum[7][:, 0:1], lhsT=onesf[0:1, :], rhs=lamt[0:1, 2:3], start=True, stop=True),
           r=[LB, bf("ones")], w=[PB[7]])
        op("dve", lambda e: e.tensor_copy(out=neglam, in_=psum[7][:, 0:1]), w=[PB[7], bf("neglam")])
        op("dve", lambda e: e.tensor_scalar(out=dnormbc, in0=dnormbc, scalar1=1.0 - LAMBDA_INIT, scalar2=None, op0=ALU.mult), r=[CB, CBa], w=[bf("dnorm")])

        RB = bf("rope")
        op("dve", lambda e: e.tensor_copy(out=posf, in_=posi), r=[CB, CBa], w=[RB])
        for t in range(NT):
            op("dve", lambda e, t=t: e.tensor_scalar(out=angs[:, t, :], in0=invfbc, scalar1=posf[:, t:t + 1], scalar2=None, op0=ALU.mult),
               r=[RB, CB, CBa], w=[RB])
        op("dve", lambda e: e.tensor_scalar(out=angs[:, :, 32:64], in0=angs[:, :, 32:64], scalar1=math.pi / 2, scalar2=None, op0=ALU.add), r=[RB], w=[RB])
        op("dve", lambda e: e.tensor_scalar(out=angk, in0=angs, scalar1=1.0 / (2 * math.pi), scalar2=None, op0=ALU.mult), r=[RB], w=[RB])
        op("dve", lambda e: e.tensor_copy(out=angi, in_=angk), r=[RB], w=[RB])
        op("dve", lambda e: e.tensor_copy(out=angk, in_=angi), r=[RB], w=[RB])
        C1 = 6.28125
        C2 = 2 * math.pi - C1
        op("dve", lambda e: e.scalar_tensor_tensor(out=angs, in0=angk, scalar=-C1, in1=angs, op0=ALU.mult, op1=ALU.add), r=[RB], w=[RB])
        op("dve", lambda e: e.scalar_tensor_tensor(out=angs, in0=angk, scalar=-C2, in1=angs, op0=ALU.mult, op1=ALU.add), r=[RB], w=[RB])
        op("dve", lambda e: e.tensor_scalar(out=angk, in0=angs, scalar1=math.pi, scalar2=-2 * math.pi, op0=ALU.is_gt, op1=ALU.mult), r=[RB], w=[RB])
        op("dve", lambda e: e.tensor_tensor(out=angs, in0=angs, in1=angk, op=ALU.add), r=[RB], w=[RB])
        op("dve", lambda e: e.tensor_scalar(out=angk, in0=angs, scalar1=-math.pi, scalar2=2 * math.pi, op0=ALU.is_lt, op1=ALU.mult), r=[RB], w=[RB])
        op("dve", lambda e: e.tensor_tensor(out=angs, in0=angs, in1=angk, op=ALU.add), r=[RB], w=[RB])
        op("dve", lambda e: e.tensor_scalar(out=angs, in0=angs, scalar1=3.1415925, scalar2=-3.1415925, op0=ALU.min, op1=ALU.max), r=[RB], w=[RB])
        op("act", lambda e: e.activation(out=sincos, in_=angs, func=AF.Sin), r=[RB], w=[bf("sincos")])

        F0a = S.fence()
        ADS = [bf("adaslot0"), bf("adaslot1")]
        BCB = {n: bf("bc_" + n) for n in ["gmod_a", "sh_a", "gpost_a", "gmod_m", "sh_m", "gpost_m"]}
        targets = [("sh_a", sh_a, "copy"), ("gmod_a", gmod_a, "mod"), ("gpost_a", gpost_a, "mul"),
                   ("sh_m", sh_m, "copy"), ("gmod_m", gmod_m, "mod"), ("gpost_m", gpost_m, "mul")]
        for n in range(4):
            slot = n % 2
            op("sp", lambda e, n=n, slot=slot: e.dma_start(out=ada_ring[slot],
                                                            in_=adaw_d[:, n * 512:(n + 1) * 512].rearrange("(k p) n -> p k n", p=128)),
               w=[ADS[slot]], dma_sem=dsem["ada%d" % slot])
            bank = 5 + slot
            for k in range(8):
                op("pe", lambda e, k=k, slot=slot, bank=bank: e.matmul(psum[bank][:], lhsT=cb[:, k, :], rhs=ada_ring[slot][:, k, :],
                                                                        start=(k == 0), stop=False),
                   r=[CBB, ADS[slot]], w=[PB[bank]])
            op("pe", lambda e, n=n, bank=bank: e.matmul(psum[bank][:], lhsT=onesf[0:1, :], rhs=adab[0:1, n * 512:(n + 1) * 512],
                                                        start=False, stop=True), r=[CB, CBa, bf("ones")], w=[PB[bank]])
            name, tile, kind = targets[n // 2]
            dst = tile[:, (n % 2) * 512:(n % 2 + 1) * 512]
            if kind == "copy":
                op("dve", lambda e, dst=dst, bank=bank: e.tensor_copy(out=dst, in_=psum[bank][:]), w=[PB[bank], BCB[name]])
            elif kind == "mod":
                op("dve", lambda e, dst=dst, bank=bank: e.scalar_tensor_tensor(out=dst, in0=psum[bank][:], scalar=1.0, in1=dst,
                                                                                op0=ALU.add, op1=ALU.mult), r=[CB, CBa], w=[PB[bank], BCB[name]])
            else:
                op("dve", lambda e, dst=dst, bank=bank: e.tensor_tensor(out=dst, in0=psum[bank][:], in1=dst, op=ALU.mult),
                   r=[CB, CBa], w=[PB[bank], BCB[name]])
        F0 = S.fence()

        XB = [bf("xt0"), bf("xt1")]
        QTB = bf("qT"); KTB = bf("kT"); VAB = bf("vaug")
        MIXB = [bf("mix%d" % t) for t in range(NT)]
        STB = bf("state")
        op("pool", lambda e: e.memset(Sf, 0.0), w=[STB], extra=F0a)
        op("pool", lambda e: e.memset(Sb, 0.0), w=[STB])
        op("pool", lambda e: e.memset(glrT[0:32, :], 1.0), w=[bf("glrT")], extra=F0a)

        NP1 = n_p1 if debug_phase >= 1 else 0

        def p1_tile(t):
            par = t % 2
            xst = {"x": F0a if t < 2 else None}

            def o_(eng, fn, r=(), w=()):
                return S.op(eng, fn, reads=r, writes=w, extra=xst["x"])

            P = lambda n: bf("%s_%d" % (n, par))
            hTt = hTt2[par]; eb = eb2[par]; enb = enb2[par]; gqk = gqk2[par]; vg = vg2[par]; gate = gate2[par]; dec = dec2[par]
            hb = hb2[par]; qr = qr2[par]; kr = kr2[par]
            S.op("sp", lambda e: e.dma_start(out=xt[par], in_=x_d[t * 128:(t + 1) * 128, :]), writes=[XB[par]],
                 dma_sem=dsem["x%d" % par])
            o_("act", lambda e: e.activation(out=junk, in_=xt[par], func=AF.Square, accum_out=ssumA[:, par:par + 1]),
               r=[XB[par]], w=[bf("junk"), P("ssumA")])
            o_("act", lambda e: e.activation(out=rstdA[:, par:par + 1], in_=ssumA[:, par:par + 1], func=AF.Ln, scale=1.0 / D, bias=epsb),
               r=[P("ssumA"), bf("epsb")], w=[P("rstdA")])
            o_("act", lambda e: e.activation(out=rstdA[:, par:par + 1], in_=rstdA[:, par:par + 1], func=AF.Exp, scale=-0.5),
               r=[P("rstdA")], w=[P("rstdA")])
            o_("dve", lambda e: e.scalar_tensor_tensor(out=h1, in0=xt[par], scalar=rstdA[:, par:par + 1], in1=gmod_a, op0=ALU.mult, op1=ALU.mult),
               r=[XB[par], P("rstdA"), BCB["gmod_a"]], w=[bf("h1")])
            o_("pool", lambda e: e.tensor_tensor(out=hb, in0=h1, in1=sh_a, op=ALU.add), r=[bf("h1"), BCB["sh_a"]], w=[P("hb")])
            yield 1
            for k in range(8):
                o_("pe", lambda e, k=k: e.transpose(out=pbf(0)[:, k * 128:(k + 1) * 128], in_=hb[:, k * 128:(k + 1) * 128], identity=identb),
                   r=[P("hb"), bf("identb")], w=[PB[0]])
            o_("act", lambda e: e.copy(out=hTt.rearrange("p k n -> p (k n)"), in_=pbf(0)), w=[PB[0], P("hTt")])
            yield 1

            xst["x"] = F0 if t < 2 else None
            if t == 0:
                o_("pool", lambda e: e.memset(vaug[:, :, :, 128:130], 1.0), w=[VAB])

            def inproj(bank, c0, ncols):
                for k in range(8):
                    o_("pe", lambda e, k=k: e.matmul(psum[bank][:, 0:ncols], lhsT=hTt[:, k, :], rhs=w_in[:, k, c0:c0 + ncols],
                                                     start=(k == 0), stop=(k == 7)), r=[P("hTt"), WINB[c0]], w=[PB[bank]])

            def rope(bank, dst, dname, RA, RBf, aname, bname):
                cos2 = sincos[:, t, 32:64].unsqueeze(1).unsqueeze(1).to_broadcast([128, 8, 2, 32])
                sinb = sincos[:, t, 0:32].unsqueeze(1).to_broadcast([128, 8, 32])
                SC = bf("sincos")
                src4 = psum[bank][:].rearrange("p (g two j) -> p g two j", g=8, two=2)
                A4 = RA.rearrange("p (g two j) -> p g two j", g=8, two=2)
                B4 = RBf.rearrange("p (g two j) -> p g two j", g=8, two=2)
                D4 = dst.rearrange("p (g two j) -> p g two j", g=8, two=2)
                o_("dve", lambda e: e.tensor_tensor(out=A4, in0=src4, in1=cos2, op=ALU.mult), r=[SC], w=[PB[bank], bf(aname)])
                o_("dve", lambda e: e.tensor_tensor(out=B4[:, :, 0, :], in0=src4[:, :, 1, :], in1=sinb, op=ALU.mult),
                   r=[SC], w=[PB[bank], bf(bname)])
                o_("dve", lambda e: e.tensor_tensor(out=B4[:, :, 1, :], in0=src4[:, :, 0, :], in1=sinb, op=ALU.mult),
                   r=[SC], w=[PB[bank], bf(bname)])
                o_("pool", lambda e: e.tensor_tensor(out=D4[:, :, 0, :], in0=A4[:, :, 0, :], in1=B4[:, :, 0, :], op=ALU.subtract),
                   r=[bf(aname), bf(bname)], w=[bf(dname)])
                o_("pool", lambda e: e.tensor_tensor(out=D4[:, :, 1, :], in0=A4[:, :, 1, :], in1=B4[:, :, 1, :], op=ALU.add),
                   r=[bf(aname), bf(bname)], w=[bf(dname)])

            for k in range(8):
                o_("pe", lambda e, k=k: e.matmul(psum[5][0:16, 0:128], lhsT=w_in[:, k, 1024:1040], rhs=hTt[:, k, :],
                                                 start=(k == 0), stop=(k == 7)), r=[P("hTt"), WINB[1024]], w=[PB[5]])
            o_("act", lambda e: e.copy(out=glrT[0:16, :], in_=psum[5][0:16, 0:128]), w=[PB[5], bf("glrT")])
            inproj(1, 0, 512)
            yield 0
            o_("pe", lambda e: e.matmul(psum[4][:, 0:256], lhsT=glrT[0:17, :], rhs=gw[0:17, :], start=True, stop=True),
               r=[bf("glrT"), CB, CBa], w=[PB[4]])
            o_("act", lambda e: e.activation(out=ez, in_=psum[4][:, 0:256], func=AF.Exp, scale=-1.0), w=[PB[4], bf("ez")])
            o_("act", lambda e: e.activation(out=spz, in_=ez, func=AF.Ln, bias=1.0), r=[bf("ez")], w=[bf("spz")])
            inproj(2, 1552, 512)
            yield 0
            o_("act", lambda e: e.copy(out=gqk, in_=psum[1][:]), w=[PB[1], P("gqk")])
            rope(2, qr, "qr_%d" % par, ropeA, ropeB, "ropeAq", "ropeBq")
            inproj(5, 2064, 512)
            yield 0
            rope(5, kr, "kr_%d" % par, ropeAk, ropeBk, "ropeAk", "ropeBk")
            o_("pe", lambda e: e.matmul(psum[4][:, 256:512], lhsT=tri, rhs=spz, start=True, stop=True), r=[CB, CBa, bf("spz")], w=[PB[4]])
            o_("act", lambda e: e.activation(out=eb, in_=psum[4][:, 256:512], func=AF.Exp, scale=-1.0 / 16), w=[PB[4], P("eb")])
            o_("act", lambda e: e.activation(out=enb, in_=psum[4][:, 256:512], func=AF.Exp, scale=1.0 / 16), w=[PB[4], P("enb")])
            for pr in range(2):
                o_("pe", lambda e, pr=pr: e.matmul(psum[4][:, pr:pr + 1], lhsT=spz[:, pr * 128:(pr + 1) * 128], rhs=onesf[:, 0:1],
                                                   start=True, stop=True), r=[bf("spz"), bf("ones")], w=[PB[4]])
            o_("act", lambda e: e.activation(out=dec, in_=psum[4][:, 0:2], func=AF.Exp, scale=-1.0 / 16), w=[PB[4], P("dec")])
            inproj(1, 512, 512)
            yield 0
            o_("act", lambda e: e.copy(out=vg, in_=psum[1][:]), w=[PB[1], P("vg")])
            inproj(2, 1040, 512)
            yield 0
            o_("act", lambda e: e.activation(out=gate, in_=psum[2][:], func=AF.Exp, scale=-1.0), w=[PB[2], P("gate")])
            o_("act", lambda e: e.activation(out=gate, in_=gate, func=AF.Ln, bias=1.0), r=[P("gate")], w=[P("gate")])
            o_("act", lambda e: e.activation(out=gate, in_=gate, func=AF.Exp, scale=-1.0), r=[P("gate")], w=[P("gate")])
            o_("dve", lambda e: e.tensor_tensor(out=gate, in0=psum[2][:], in1=gate, op=ALU.mult), r=[P("gate")], w=[PB[2], P("gate")])
            inproj(5, 2576, 512)
            yield 0
            o_("act", lambda e: e.copy(out=vaug[:, t, :, 0:128], in_=psum[5][:].rearrange("p (h n) -> p h n", h=4)), w=[PB[5], VAB])
            yield 1

            o_("dve", lambda e: e.scalar_tensor_tensor(out=qg, in0=gqk[:, 0:256], scalar=0.125, in1=eb, op0=ALU.mult, op1=ALU.mult),
               r=[P("gqk"), P("eb")], w=[bf("qg")])
            o_("dve", lambda e: e.tensor_tensor(out=kg, in0=gqk[:, 256:512], in1=enb, op=ALU.mult), r=[P("gqk"), P("enb")], w=[bf("kg")])
            for a in range(4):
                src = qg if a < 2 else kg
                pr = a % 2
                o_("pe", lambda e, a=a, src=src, pr=pr: e.transpose(out=pbf(6)[:, a * 128:(a + 1) * 128], in_=src[:, pr * 128:(pr + 1) * 128],
                                                                      identity=identb), r=[bf("qg"), bf("kg"), bf("identb")], w=[PB[6]])
            o_("act", lambda e: e.copy(out=qkT.rearrange("p a n -> p (a n)"), in_=pbf(6)[:, 0:512]), w=[PB[6], bf("qkT")])
            yield 0
            for hh in range(4):
                pr, hf = hh // 2, hh % 2
                sbk = 6 if hf == 0 else 7
                o_("pe", lambda e, pr=pr, hf=hf, sbk=sbk: e.matmul(psum[sbk][:, pr * 128:(pr + 1) * 128],
                                                                   lhsT=qkT[hf * 64:(hf + 1) * 64, 2 + pr, :], rhs=qkT[hf * 64:(hf + 1) * 64, pr, :],
                                                                   start=True, stop=True), r=[bf("qkT")], w=[PB[sbk]])
            for hh in range(4):
                pr, hf = hh // 2, hh % 2
                sbk = 6 if hf == 0 else 7
                o_("dve", lambda e, hh=hh, pr=pr, sbk=sbk: e.tensor_tensor(out=sT[:, hh, :], in0=psum[sbk][:, pr * 128:(pr + 1) * 128], in1=tri, op=ALU.mult),
                   r=[CB, CBa], w=[PB[sbk], bf("sT")])
            yield 0
            for hh in range(4):
                pr, hf = hh // 2, hh % 2
                o_("pe", lambda e, hh=hh: e.matmul(psum[7][:, hh * 128:(hh + 1) * 128], lhsT=sT[:, hh, :], rhs=vg[:, hh * 128:(hh + 1) * 128],
                                                   start=True, stop=False), r=[bf("sT"), P("vg")], w=[PB[7]])
                o_("pe", lambda e, hh=hh, pr=pr, hf=hf: e.matmul(psum[7][:, hh * 128:(hh + 1) * 128], lhsT=qkT[hf * 64:(hf + 1) * 64, pr, :],
                                                                   rhs=Sb[hf * 64:(hf + 1) * 64, pr, :], start=False, stop=True),
                   r=[bf("qkT"), STB], w=[PB[7]])
            for pr in range(2):
                o_("pe", lambda e, pr=pr: e.matmul(psum[6][:, pr * 256:(pr + 1) * 256], lhsT=kg[:, pr * 128:(pr + 1) * 128],
                                                   rhs=vg[:, pr * 256:(pr + 1) * 256], start=True, stop=True), r=[bf("kg"), P("vg")], w=[PB[6]])
            for pr in range(2):
                for hf in range(2):
                    rows = slice(hf * 64, (hf + 1) * 64)
                    o_("dve", lambda e, pr=pr, hf=hf, rows=rows: e.tensor_scalar(
                        out=Stmp[rows, pr, :], in0=psum[6][rows, pr * 256 + hf * 128: pr * 256 + (hf + 1) * 128],
                        scalar1=dec[rows, pr:pr + 1], scalar2=None, op0=ALU.mult), r=[P("dec")], w=[PB[6], bf("Stmp")])
            for pr in range(2):
                o_("dve", lambda e, pr=pr: e.scalar_tensor_tensor(out=Sf[:, pr, :], in0=Sf[:, pr, :], scalar=dec[:, pr:pr + 1], in1=Stmp[:, pr, :],
                                                                   op0=ALU.mult, op1=ALU.add), r=[P("dec"), bf("Stmp")], w=[STB])
                o_("pool", lambda e, pr=pr: e.tensor_copy(out=Sb[:, pr, :], in_=Sf[:, pr, :]), w=[STB])
            yield 0
            for hh in range(4):
                o_("act", lambda e, hh=hh: e.activation(out=junk[:, 0:128], in_=psum[7][:, hh * 128:(hh + 1) * 128], func=AF.Square,
                                                        accum_out=ssq4[:, hh:hh + 1]), w=[PB[7], bf("junk"), bf("ssq4")])
            o_("act", lambda e: e.activation(out=rstd4, in_=ssq4, func=AF.Ln, scale=1.0 / 128, bias=epsb), r=[bf("ssq4"), bf("epsb")], w=[bf("rstd4")])
            o_("act", lambda e: e.activation(out=rstd4, in_=rstd4, func=AF.Exp, scale=-0.5), r=[bf("rstd4")], w=[bf("rstd4")])
            for hh in range(4):
                o_("dve", lambda e, hh=hh: e.scalar_tensor_tensor(out=mixg[:, hh * 128:(hh + 1) * 128], in0=psum[7][:, hh * 128:(hh + 1) * 128],
                                                                   scalar=rstd4[:, hh:hh + 1], in1=gate[:, hh * 128:(hh + 1) * 128],
                                                                   op0=ALU.mult, op1=ALU.mult), r=[bf("rstd4"), P("gate")], w=[PB[7], bf("mixg")])
            yield 0
            for hh in range(4):
                o_("pe", lambda e, hh=hh: e.transpose(out=pbf(6)[:, hh * 128:(hh + 1) * 128], in_=mixg[:, hh * 128:(hh + 1) * 128], identity=identb),
                   r=[bf("mixg"), bf("identb")], w=[PB[6]])
            o_("act", lambda e: e.activation(out=mixT[:, 0:4, t * 128:(t + 1) * 128], in_=pbf(6)[:, 0:512].rearrange("p (a n) -> p a n", a=4),
                                             func=AF.Copy, scale=gncol[:, 0:1]), r=[CB, CBa], w=[PB[6], MIXB[t]])
            yield 0
            for a in range(8):
                src = qr if a < 4 else kr
                hh = a % 4
                o_("pe", lambda e, a=a, src=src, hh=hh: e.transpose(out=pbf(3)[:, a * 128:(a + 1) * 128], in_=src[:, hh * 128:(hh + 1) * 128],
                                                                      identity=identb), r=[P("qr"), P("kr"), bf("identb")], w=[PB[3]])
            o_("dve", lambda e: e.tensor_copy(out=qT[:, :, t * 128:(t + 1) * 128], in_=pbf(3)[:, 0:512].rearrange("p (a n) -> p a n", a=4)),
               w=[PB[3], QTB])
            o_("dve", lambda e: e.tensor_copy(out=kT[:, :, t * 128:(t + 1) * 128], in_=pbf(3)[:, 512:1024].rearrange("p (a n) -> p a n", a=4)),
               w=[PB[3], KTB])
            yield 1

        def run_pipeline(gens, nstages, newest_first=False):
            n = len(gens)
            done = [False] * n
            for step in range(n + nstages - 1):
                act = [t for t in range(n) if t <= step < t + nstages and not done[t]]
                if newest_first:
                    act = act[::-1]
                fin = {t: False for t in act}
                while not all(fin.values()):
                    for t in act:
                        if fin[t]:
                            continue
                        try:
                            v = next(gens[t])
                        except StopIteration:
                            done[t] = True
                            v = 1
                        if v == 1:
                            fin[t] = True

        run_pipeline([p1_tile(t) for t in range(NP1)], 4, newest_first=True)
        F1 = S.fence()

        WOB = bf("w_out")

        def load_w_out():
            for k in range(8):
                op("pool", lambda e, k=k: e.dma_start(out=w_out[:, k, :], in_=wout_d[k * 128:(k + 1) * 128, :]), w=[WOB], dma_sem=dsem["wout"],
                   extra=(F1 if k == 0 else None))
        PTB = [[bf("pt%d_%d" % (m, j)) for j in range(16)] for m in range(2)]
        GB = bf("gains2")
        op("sp", lambda e: e.dma_start(out=gpost_a, in_=gpost_mix_d.partition_broadcast(128)), w=[GB], dma_sem=dsem["g2"], extra=F1)
        op("sp", lambda e: e.dma_start(out=gmod_m, in_=gpre_mlp_d.partition_broadcast(128)), w=[GB], dma_sem=dsem["g2"])
        op("sp", lambda e: e.dma_start(out=gpost_m, in_=gpost_mlp_d.partition_broadcast(128)), w=[GB], dma_sem=dsem["g2"])
        CB2 = bf("cb2")
        op("dve", lambda e: e.tensor_copy(out=onesb, in_=onesf), r=[bf("ones")], w=[bf("onesb")], extra=F1)
        for k in range(8):
            op("dve", lambda e, k=k: e.tensor_scalar(out=cb2[:, k, :], in0=onesf, scalar1=cact[:, k:k + 1], scalar2=None, op0=ALU.mult),
               r=[CA, bf("ones")], w=[CB2], extra=(F1 if k == 0 else None))
        ADS2 = [bf("ada2slot0"), bf("ada2slot1")]
        ADB2 = [bf("adab2_0"), bf("adab2_1")]

        def ada_chunk2(n2):
            slot = n2 % 2
            c0 = 2048 + n2 * 256
            op("pool", lambda e: e.dma_start(out=ada_ring2[slot], in_=adaw_d[:, c0:c0 + 256].rearrange("(k p) n -> p k n", p=128)),
               w=[ADS2[slot]], dma_sem=dsem["ada2_%d" % slot], extra=(F1 if n2 < 2 else None))
            op("pool", lambda e: e.dma_start(out=adab2[slot][0:1, :], in_=adab_d[:, c0:c0 + 256]),
               w=[ADB2[slot]], dma_sem=dsem["adb2_%d" % slot], extra=(F1 if n2 < 2 else None))
            for k in range(8):
                op("pe", lambda e, k=k: e.matmul(psum[7][:, 0:256], lhsT=cb2[:, k, :], rhs=ada_ring2[slot][:, k, :],
                                                 start=(k == 0), stop=False), r=[CB2, ADS2[slot]], w=[PB[7]])
            op("pe", lambda e: e.matmul(psum[7][:, 0:256], lhsT=onesb[0:1, :], rhs=adab2[slot][0:1, :], start=False, stop=True),
               r=[ADB2[slot], bf("onesb")], w=[PB[7]])
            name, tile, kind = targets[c0 // 1024]
            dst = tile[:, c0 % 1024: c0 % 1024 + 256]
            if kind == "copy":
                op("dve", lambda e: e.tensor_copy(out=dst, in_=psum[7][:, 0:256]), w=[PB[7], BCB[name]], extra=(F1 if n2 < 2 else None))
            elif kind == "mod":
                op("dve", lambda e: e.scalar_tensor_tensor(out=dst, in0=psum[7][:, 0:256], scalar=1.0, in1=dst, op0=ALU.add, op1=ALU.mult),
                   r=[GB], w=[PB[7], BCB[name]])
            else:
                op("dve", lambda e: e.tensor_tensor(out=dst, in0=psum[7][:, 0:256], in1=dst, op=ALU.mult), r=[GB], w=[PB[7], BCB[name]])

        st2 = dict(sbank=0, abank=0, first=True)

        def p2_qk(hh, c, m):
            rows = slice(m * 64, (m + 1) * 64)
            j = 0
            while j < 4 * c + 4:
                r_ = j - 4 * c
                first_w = (list(WINB.values()) if st2["first"] else [])
                if r_ < -1:
                    if st2["sbank"] % 2:
                        st2["sbank"] += 1
                    b0 = st2["sbank"] % 4
                    st2["sbank"] += 2
                    for d in range(2):
                        op("pe", lambda e, b0=b0, d=d, j=j: e.matmul(
                            psum[b0 + d][:, 0:512], lhsT=kT[rows, hh, (j + d) * 128:(j + d + 1) * 128], rhs=qT[rows, hh, c * 512:(c + 1) * 512],
                            start=True, stop=True), r=[KTB, QTB], w=[PB[b0 + d]])
                    op("act", lambda e, b0=b0, j=j: e.activation(out=PT[m][:, j:j + 2, :].rearrange("p j n -> p (j n)"),
                                                                 in_=psum2[b0 // 2][:, 0:1024], func=AF.Exp, scale=0.125),
                       w=[PB[b0], PB[b0 + 1], PTB[m][j], PTB[m][j + 1]] + first_w)
                    st2["first"] = False
                    j += 2
                    continue
                off = max(r_, 0) * 128
                bank = st2["sbank"] % 4
                st2["sbank"] += 1
                op("pe", lambda e, bank=bank, off=off, j=j, r_=r_: e.matmul(
                    psum[bank][:, off:512], lhsT=kT[rows, hh, j * 128:(j + 1) * 128], rhs=qT[rows, hh, c * 512 + off:(c + 1) * 512],
                    start=True, stop=(r_ < 0)), r=[KTB, QTB], w=[PB[bank]])
                if r_ >= 0:
                    op("pe", lambda e, bank=bank, off=off: e.matmul(psum[bank][:, off:off + 128], lhsT=identb, rhs=negmb,
                                                                     start=False, stop=True), r=[bf("identb"), bf("negmb")], w=[PB[bank]])
                op("act", lambda e, bank=bank, off=off, j=j: e.activation(out=PT[m][:, j, off:512], in_=psum[bank][:, off:512],
                                                                          func=AF.Exp, scale=0.125),
                   w=[PB[bank], PTB[m][j]] + first_w)
                st2["first"] = False
                j += 1

        def p2_pv(hh, c, m):
            for r_ in range(4):
                i = 4 * c + r_
                bank = 4 + st2["abank"] % 2
                st2["abank"] += 1
                for j in range(i + 1):
                    op("pe", lambda e, bank=bank, j=j, r_=r_, i=i: e.matmul(
                        psum[bank][:, 0:129], lhsT=PT[m][:, j, r_ * 128:(r_ + 1) * 128], rhs=vaug[:, j, hh, 0:129],
                        start=(j == 0), stop=(j == i)), r=[PTB[m][j], VAB], w=[PB[bank]])
                if m == 0:
                    op("dve", lambda e, bank=bank, r_=r_: e.reciprocal(out=rden[:, r_:r_ + 1], in_=psum[bank][:, 128:129]),
                       w=[PB[bank], bf("rden%d" % r_)])
                    op("dve", lambda e, bank=bank, r_=r_: e.tensor_scalar(out=o1[:, r_, :], in0=psum[bank][:, 0:128],
                                                                           scalar1=rden[:, r_:r_ + 1], scalar2=None, op0=ALU.mult),
                       r=[bf("rden%d" % r_)], w=[PB[bank], bf("o1_%d" % r_)])
                else:
                    q = r_
                    odq, mixdq = od4[q], mixd4[q]
                    ODB, MXB, RDB, SSB, RSB = bf("od%d" % q), bf("mixd%d" % q), bf("rdenb%d" % q), bf("ssumb%d" % q), bf("rstdb%d" % q)
                    rd = rden2[:, q:q + 1]
                    ss = ssum2[:, q:q + 1]
                    rs = rstd2[:, q:q + 1]
                    op("dve", lambda e, bank=bank, rd=rd: e.reciprocal(out=rd, in_=psum[bank][:, 128:129]), w=[PB[bank], RDB])
                    op("dve", lambda e, rd=rd: e.tensor_tensor(out=rd, in0=rd, in1=neglam, op=ALU.mult), r=[bf("neglam")], w=[RDB])
                    op("dve", lambda e, bank=bank, r_=r_, rd=rd, odq=odq: e.scalar_tensor_tensor(out=odq, in0=psum[bank][:, 0:128], scalar=rd,
                                                                                              in1=o1[:, r_, :], op0=ALU.mult, op1=ALU.add),
                       r=[RDB, bf("o1_%d" % r_)], w=[PB[bank], ODB])
                    op("dve", lambda e, odq=odq: e.tensor_tensor(out=p2sq, in0=odq, in1=odq, op=ALU.mult), r=[ODB], w=[bf("p2sq")])
                    op("dve", lambda e, ss=ss: e.reduce_sum(out=ss, in_=p2sq, axis=AX.X), r=[bf("p2sq")], w=[bf("ssum2all")])

        def p2_c2(hh, c, m):
            if m == 0:
                return
            op("act", lambda e: e.activation(out=rstd2, in_=ssum2, func=AF.Ln, scale=1.0 / 128, bias=epsb), r=[bf("ssum2all"), bf("epsb")], w=[bf("rstd2all")])
            op("act", lambda e: e.activation(out=rstd2, in_=rstd2, func=AF.Exp, scale=-0.5), r=[bf("rstd2all")], w=[bf("rstd2all")])
            for r_ in range(4):
                op("dve", lambda e, r_=r_: e.scalar_tensor_tensor(out=mixd4[r_], in0=od4[r_], scalar=rstd2[:, r_:r_ + 1], in1=dnormbc,
                                                                   op0=ALU.mult, op1=ALU.mult),
                   r=[bf("od%d" % r_), bf("rstd2all"), bf("dnorm")], w=[bf("mixd%d" % r_)])

        def p2_tr(hh, c, m):
            if m == 0:
                return
            for r_ in range(4):
                op("pe", lambda e, r_=r_: e.transpose(out=pbf(6)[:, r_ * 128:(r_ + 1) * 128], in_=mixd4[r_], identity=identb),
                   r=[bf("mixd%d" % r_), bf("identb")], w=[PB[6]])
            op("dve", lambda e: e.tensor_copy(out=mixT[:, 4 + hh, c * 512:(c + 1) * 512], in_=pbf(6)[:, 0:512]),
               w=[PB[6]] + [MIXB[4 * c + r_] for r_ in range(4)])

        NH = 4 if debug_phase >= 2 else 0
        iters = [(hh, c, m) for hh in range(NH) for c in range(4) for m in range(2)]
        if iters:
            p2_qk(*iters[0])
        for n, it in enumerate(iters):
            if it[2] == 0 and n >= 4:
                ada_chunk2(n // 2 - 2)
            if n in (27, 29):
                ada_chunk2(14 + (n - 27) // 2)
            if n == 8:
                load_w_out()
                op("pool", lambda e: e.dma_start(out=w_up[:, 5, :], in_=wup_d[5 * 128:6 * 128, :]), w=[bf("w_up")], dma_sem=dsem["ffnu"], extra=F1)
                st2["pref"] = {5}
            if n >= 1:
                p2_c2(*iters[n - 1])
            if n + 1 < len(iters):
                p2_qk(*iters[n + 1])
            if n >= 1:
                p2_tr(*iters[n - 1])
            p2_pv(*it)
        if iters:
            p2_c2(*iters[-1])
            p2_tr(*iters[-1])
            op("pool", lambda e: e.dma_start(out=w_up[:, 4, :], in_=wup_d[4 * 128:5 * 128, :]), w=[bf("w_up"), ADS2[0], ADS2[1]],
               dma_sem=dsem["ffnu"])
            st2["pref"] = st2.get("pref", set()) | {4}
        else:
            load_w_out()
        F2 = S.fence()

        WUB = bf("w_up"); WDBS = [bf("w_dn%d" % g) for g in range(8)]
        for k in range(8):
            if k in st2.get("pref", set()):
                continue
            op("pool", lambda e, k=k: e.dma_start(out=w_up[:, k, :], in_=wup_d[k * 128:(k + 1) * 128, :]), w=[WUB], dma_sem=dsem["ffnu"],
               extra=(F2 if k == 0 else None))
        for g in range(4):
            op("pool", lambda e, g=g: e.dma_start(out=w_dn[:, g * 4:(g + 1) * 4, :],
                                                   in_=wdown_d[g * 512:(g + 1) * 512, :].rearrange("(f p) n -> p f n", p=128)),
               w=[WDBS[g]], dma_sem=dsem["ffnd%d" % g])
        P3X = [bf("p3x0"), bf("p3x1"), bf("p3x2")]
        OUTD = [bf("outd%d" % t) for t in range(NT)]
        NP3 = NT if debug_phase >= 3 else 0
        p3x3 = [f32v(198656 + i * 4096, D) for i in range(3)]
        ssq_y = [f32v(SM + 480, 2), f32v(SM + 488, 2)]
        sm_y = f32v(SM + 496, 4)

        def p3_tile(t):
            par = t % 2
            slot = t % 3
            X2 = F2 if t < 3 else None
            xx = p3x3[slot]
            XS = P3X[slot]
            yb = (0, 1) if par == 0 else (2, 3)
            P = lambda n: bf("p3%s_%d" % (n, par))

            def o_(eng, fn, r=(), w=(), **kw):
                return S.op(eng, fn, reads=r, writes=w, extra=X2, **kw)

            o_("sp", lambda e: e.dma_start(out=xx, in_=x_d[t * 128:(t + 1) * 128, :]), w=[XS], dma_sem=dsem["xm%d" % slot])
            for half in range(2):
                for k in range(8):
                    o_("pe", lambda e, half=half, k=k: e.matmul(psum[yb[half]][:], lhsT=mixT[:, k, t * 128:(t + 1) * 128],
                                                                  rhs=w_out[:, k, half * 512:(half + 1) * 512], start=(k == 0), stop=(k == 7)),
                       r=[MIXB[t], WOB], w=[PB[yb[half]]])
                yield 0
            for half in range(2):
                o_("act", lambda e, half=half: e.activation(out=p3junk[:, 0:512], in_=psum[yb[half]][:], func=AF.Square,
                                                            accum_out=ssq_y[par][:, half:half + 1]),
                   w=[PB[yb[half]], bf("p3junk"), P("ssqy")])
            yield 1
            o_("dve", lambda e: e.tensor_tensor(out=sm_y[:, par:par + 1], in0=ssq_y[par][:, 0:1], in1=ssq_y[par][:, 1:2], op=ALU.add),
               r=[P("ssqy")], w=[P("smy")])
            o_("act", lambda e: e.activation(out=sm_y[:, par:par + 1], in_=sm_y[:, par:par + 1], func=AF.Ln, scale=1.0 / D, bias=epsb),
               r=[P("smy"), bf("epsb")], w=[P("smy")])
            o_("act", lambda e: e.activation(out=sm_y[:, par:par + 1], in_=sm_y[:, par:par + 1], func=AF.Exp, scale=-0.5), r=[P("smy")], w=[P("smy")])
            yield 0
            for half in range(2):
                cs = slice(half * 512, (half + 1) * 512)
                o_("dve", lambda e, half=half, cs=cs: e.scalar_tensor_tensor(out=p3u[:, cs], in0=psum[yb[half]][:], scalar=sm_y[:, par:par + 1],
                                                                              in1=gpost_a[:, cs], op0=ALU.mult, op1=ALU.mult),
                   r=[P("smy"), BCB["gpost_a"]], w=[PB[yb[half]], bf("p3u")])
            o_("dve", lambda e: e.tensor_tensor(out=xx, in0=xx, in1=p3u, op=ALU.add), r=[bf("p3u")], w=[XS])
            yield 0
            o_("sp", lambda e: e.dma_start(out=out_d[t * 128:(t + 1) * 128, :], in_=xx), r=[XS], w=[OUTD[t]], dma_sem=dsem["os%d" % slot])
            o_("act", lambda e: e.activation(out=p3junk, in_=xx, func=AF.Square, accum_out=sm_y[:, 2 + par:3 + par]),
               r=[XS], w=[bf("p3junk"), P("ssh")])
            yield 1
            o_("act", lambda e: e.activation(out=sm_y[:, 2 + par:3 + par], in_=sm_y[:, 2 + par:3 + par], func=AF.Ln, scale=1.0 / D, bias=epsb),
               r=[P("ssh"), bf("epsb")], w=[P("ssh")])
            o_("act", lambda e: e.activation(out=sm_y[:, 2 + par:3 + par], in_=sm_y[:, 2 + par:3 + par], func=AF.Exp, scale=-0.5), r=[P("ssh")], w=[P("ssh")])
            yield 0
            o_("dve", lambda e: e.scalar_tensor_tensor(out=p3h1, in0=xx, scalar=sm_y[:, 2 + par:3 + par], in1=gmod_m, op0=ALU.mult, op1=ALU.mult),
               r=[XS, P("ssh"), BCB["gmod_m"]], w=[bf("p3h1")])
            hbp = p3hb2[par]
            o_("dve", lambda e: e.tensor_tensor(out=hbp, in0=p3h1, in1=sh_m, op=ALU.add), r=[bf("p3h1"), BCB["sh_m"]], w=[P("hb")])
            yield 1
            tb = 4 + par
            for k in range(8):
                o_("pe", lambda e, k=k: e.transpose(out=pbf(tb)[:, k * 128:(k + 1) * 128], in_=hbp[:, k * 128:(k + 1) * 128], identity=identb),
                   r=[P("hb"), bf("identb")], w=[PB[tb]])
            o_("act", lambda e: e.copy(out=mixT[:, :, t * 128:(t + 1) * 128], in_=pbf(tb).rearrange("p (k n) -> p k n", k=8)),
               w=[PB[tb], MIXB[t]])
            yield 1

        run_pipeline([p3_tile(t) for t in range(NP3)], 4, newest_first=True)
        F3 = S.fence()

        for g in range(4, 8):
            op("pool", lambda e, g=g: e.dma_start(out=w_dn[:, g * 4:(g + 1) * 4, :],
                                                   in_=wdown_d[g * 512:(g + 1) * 512, :].rearrange("(f p) n -> p f n", p=128)),
               w=[WDBS[g]], dma_sem=dsem["ffnd%d" % g], extra=(F3 if g == 4 else None))
        HB = [bf("hid%d" % f) for f in range(32)]
        XGB = bf("xg")
        RTB = [bf("rt0"), bf("rt1")]
        fin = []
        NG = 8 if debug_phase >= 4 else 0
        ub = 0
        for g in range(NG):
            for tt in range(2):
                t = 2 * g + tt
                op("sp", lambda e, t=t, tt=tt: e.dma_start(out=xg[:, tt, :], in_=out_d[t * 128:(t + 1) * 128, :]), r=[OUTD[t]], w=[XGB],
                   dma_sem=dsem["xg"], extra=(F3 if g == 0 else None))
            for f in range(32):
                bank = 4 + ub % 4
                rt = ub % 2
                ub += 1
                for k in range(8):
                    op("pe", lambda e, bank=bank, f=f, k=k, g=g: e.matmul(psum[bank][:, 0:256], lhsT=w_up[:, k, f * 128:(f + 1) * 128],
                                                                            rhs=mixT[:, k, g * 256:(g + 1) * 256], start=(k == 0), stop=(k == 7)),
                       r=[WUB, MIXB[2 * g], MIXB[2 * g + 1]], w=[PB[bank]])
                op("act", lambda e, bank=bank, rt=rt: e.activation(out=RT[rt], in_=psum[bank][:, 0:256], func=AF.Relu),
                   w=[PB[bank], RTB[rt]], extra=(F3 if (g == 0 and f < 2) else None))
                eng = "pool" if (f % 2 == 0 and g > 0) else "dve"
                op(eng, lambda e, rt=rt, f=f: e.tensor_tensor(out=hidT[:, f, :], in0=RT[rt], in1=RT[rt], op=ALU.mult),
                   r=[RTB[rt]], w=[HB[f]], extra=(F3 if g == 0 else None))
            for tt in range(2):
                t = 2 * g + tt
                for half in range(2):
                    bank = tt * 2 + half
                    for f in range(32):
                        op("pe", lambda e, bank=bank, f=f, tt=tt, half=half: e.matmul(
                            psum[bank][:], lhsT=hidT[:, f, tt * 128:(tt + 1) * 128], rhs=w_dn[:, f, half * 512:(half + 1) * 512],
                            start=(f == 0), stop=(f == 31)), r=[HB[f], WDBS[f // 4]], w=[PB[bank]])
                for half in range(2):
                    bank = tt * 2 + half
                    op("act", lambda e, bank=bank, half=half: e.activation(out=p4junk, in_=psum[bank][:], func=AF.Square,
                                                                           accum_out=ssq4[:, half:half + 1]),
                       w=[PB[bank], bf("p4junk"), bf("ssq4")], extra=(F3 if g == 0 else None))
                op("dve", lambda e: e.tensor_tensor(out=ssum[:, 0:1], in0=ssq4[:, 0:1], in1=ssq4[:, 1:2], op=ALU.add), r=[bf("ssq4")], w=[bf("ssum")])
                rstd_from(ssum[:, 0:1], rstd[:, 0:1], 1, bf("ssum"), bf("rstd"), 1.0 / D)
                for half in range(2):
                    bank = tt * 2 + half
                    cs = slice(half * 512, (half + 1) * 512)
                    op("dve", lambda e, bank=bank, cs=cs: e.scalar_tensor_tensor(out=p4u[:, cs], in0=psum[bank][:], scalar=rstd[:, 0:1],
                                                                                  in1=gpost_m[:, cs], op0=ALU.mult, op1=ALU.mult),
                       r=[bf("rstd"), BCB["gpost_m"]], w=[PB[bank], bf("p4u")], extra=(F3 if g == 0 else None))
                op("pool", lambda e, tt=tt: e.tensor_tensor(out=xg[:, tt, :], in0=xg[:, tt, :], in1=p4u, op=ALU.add), r=[bf("p4u")], w=[XGB])
                fin.append(op("sp", lambda e, t=t, tt=tt: e.dma_start(out=out_d[t * 128:(t + 1) * 128, :], in_=xg[:, tt, :]), r=[XGB], w=[OUTD[t]],
                              dma_sem=dsem["o%d" % tt]))
        S.wait_all("sp", list(S.fence()))
        stats = S.emit(nc, es)
        build_program.stats = stats
    return nc


_CACHE = {}


def _consts():
    ident = np.eye(128, dtype=np.float32)
    j = np.arange(128)[:, None]
    i = np.arange(128)[None, :]
    tri = (j <= i).astype(np.float32)
    negm = np.where(j > i, -30000.0, 0.0).astype(np.float32)
    inv = (np.float32(1.0) / (np.float32(10000.0) ** (np.arange(0, 64, 2, dtype=np.float32) / np.float32(64)))).astype(np.float32)
    invf = np.concatenate([inv, inv])[None, :].astype(np.float32)
    return ident, tri, negm, invf


def kernel(x, c, positions, ada_w, ada_b, pre_norm_mix, post_norm_mix, w_in, gla_gate_w, gla_gate_b, gla_norm,
           lambda_q1, lambda_k1, lambda_q2, lambda_k2, diff_norm, w_out, pre_norm_mlp, post_norm_mlp, w_up, w_down):
    f = lambda a: np.ascontiguousarray(np.asarray(a, dtype=np.float32))
    x = f(x); c = f(c)
    positions = np.asarray(positions).astype(np.int32)
    nb = x.shape[0]
    if "nc" not in _CACHE:
        _CACHE["nc"] = build_program()
    nc = _CACHE["nc"]
    ident, tri, negm, invf = _consts()
    shared = {
        "ada_w": f(ada_w[0]), "ada_b": f(ada_b[0])[None, :],
        "pre_norm_mix": f(pre_norm_mix[0])[None, :], "post_norm_mix": f(post_norm_mix[0])[None, :],
        "pre_norm_mlp": f(pre_norm_mlp[0])[None, :], "post_norm_mlp": f(post_norm_mlp[0])[None, :],
        "w_in": f(w_in[0]), "gla_gate_w": f(gla_gate_w[0]), "gla_gate_b": f(gla_gate_b[0])[None, :],
        "gla_norm": f(gla_norm[0])[None, :],
        "lambda_q1": f(lambda_q1[0])[None, :], "lambda_k1": f(lambda_k1[0])[None, :],
        "lambda_q2": f(lambda_q2[0])[None, :], "lambda_k2": f(lambda_k2[0])[None, :],
        "diff_norm": f(diff_norm[0])[None, :], "w_out": f(w_out[0]), "w_up": f(w_up[0]), "w_down": f(w_down[0]),
        "k_ident": ident, "k_tri": tri, "k_negmask": negm, "k_invf": invf,
    }
    in_maps = []
    for b in range(nb):
        m = dict(shared)
        m["x"] = x[b]
        m["c8"] = np.ascontiguousarray(c[b].reshape(8, 128).T)
        m["pos"] = np.ascontiguousarray(positions[b].reshape(NT, 128).T)
        in_maps.append(m)
    res = run_bass_kernel_spmd(nc, in_maps, core_ids=list(range(nb)))
    return np.stack([np.asarray(r["out"], dtype=np.float32) for r in res.results], axis=0)
```

```python
import math
import numpy as np
from contextlib import ExitStack
import concourse.bass as bass
import concourse.mybir as mybir
from concourse.bass_utils import run_bass_kernel_spmd

F32 = mybir.dt.float32
BF16 = mybir.dt.bfloat16
I32 = mybir.dt.int32
ALU = mybir.AluOpType
AF = mybir.ActivationFunctionType
AX = mybir.AxisListType

ENGS = ("pe", "act", "dve", "pool", "sp")

S_LEN = 2048
D = 1024
NT = 16
DFF = 4096
PW = 3088
EPS = 1e-6
LAMBDA_INIT = 0.8 - 0.6 * math.exp(0.0)


class Buf:
    __slots__ = ("name", "w", "r")

    def __init__(self, name):
        self.name = name
        self.w = None
        self.r = {}


class Sched:
    def __init__(self, same_engine_sync=True):
        self.ops = {e: [] for e in ENGS}
        self.dma_vals = []
        self.same_engine_sync = same_engine_sync

    def new_dma_sem(self):
        self.dma_vals.append(0)
        return len(self.dma_vals) - 1

    def fence(self):
        evs = set()
        for e in ENGS:
            for i in range(len(self.ops[e]) - 1, -1, -1):
                o = self.ops[e][i]
                if o["fn"] is not None and o["dma"] is None:
                    evs.add(("e", e, i))
                    break
        for i, v in enumerate(self.dma_vals):
            if v:
                evs.add(("d", i, v))
        return evs

    def op(self, eng, fn, reads=(), writes=(), dma_sem=None, extra=None):
        deps = set()
        for b in reads:
            if b.w is not None:
                deps.add(b.w)
        for b in writes:
            if b.w is not None:
                deps.add(b.w)
            for ev in b.r.values():
                if not (ev[0] == "e" and ev[1] == eng and dma_sem is None and (eng == "pe" or not self.same_engine_sync)):
                    deps.add(ev)
        idx = len(self.ops[eng])
        if dma_sem is None:
            ev = ("e", eng, idx)
        else:
            self.dma_vals[dma_sem] += 16
            ev = ("d", dma_sem, self.dma_vals[dma_sem])
        for b in reads:
            b.r[(ev[0], ev[1])] = ev
        for b in writes:
            b.w = ev
            b.r = {}
        if dma_sem is None:
            if eng == "pe" or not self.same_engine_sync:
                deps = {d for d in deps if not (d[0] == "e" and d[1] == eng)}
        if extra:
            for d in extra:
                if d[0] == "e" and d[1] == eng and (eng == "pe" or d[2] >= idx):
                    continue
                deps.add(d)
        self.ops[eng].append(dict(fn=fn, deps=deps, dma=dma_sem))
        return ev

    def wait_all(self, eng, events):
        self.ops[eng].append(dict(fn=None, deps=set(e for e in events if e is not None), dma=None))

    def emit(self, nc, es):
        esems = {e: es.enter_context(nc.semaphore("s_" + e)) for e in ENGS}
        dsems = [es.enter_context(nc.semaphore("d%d" % i)) for i in range(len(self.dma_vals))]
        sig = {e: set() for e in ENGS}
        for e in ENGS:
            for o in self.ops[e]:
                for d in o["deps"]:
                    if d[0] == "e":
                        sig[d[1]].add(d[2])
        cnt = {}
        for e in ENGS:
            c = 0
            arr = []
            for i in range(len(self.ops[e])):
                if i in sig[e]:
                    c += 1
                arr.append(c)
            cnt[e] = arr
        stats = {e: [len(self.ops[e]), len(sig[e]), 0] for e in ENGS}
        block = es.enter_context(nc.Block())

        def run(ename, h):
            waited = {}
            for i, o in enumerate(self.ops[ename]):
                for d in sorted(o["deps"]):
                    if d[0] == "e":
                        key, val, sem = ("e", d[1]), cnt[d[1]][d[2]], esems[d[1]]
                    else:
                        key, val, sem = ("d", d[1]), d[2], dsems[d[1]]
                    if waited.get(key, 0) >= val:
                        continue
                    waited[key] = val
                    h.wait_ge(sem, val)
                    stats[ename][2] += 1
                if o["fn"] is None:
                    continue
                ins = o["fn"](h)
                if o["dma"] is not None:
                    ins.then_inc(dsems[o["dma"]], 16)
                elif i in sig[ename]:
                    ins.then_inc(esems[ename], 1)

        @block.tensor
        def _(h):
            run("pe", h)

        @block.scalar
        def _(h):
            run("act", h)

        @block.vector
        def _(h):
            run("dve", h)

        @block.gpsimd
        def _(h):
            run("pool", h)

        @block.sync
        def _(h):
            run("sp", h)

        return stats


class _Stop(Exception):
    pass


def build_program(debug_phase=99, n_p1=NT, p1_stop=999):
    nc = bass.Bass("TRN2", target_bir_lowering=False)

    def din(name, shape, dt=F32):
        return nc.dram_tensor(name, list(shape), dt, kind="ExternalInput").ap()

    x_d = din("x", [S_LEN, D])
    c8_d = din("c8", [128, 8])
    pos_d = din("pos", [128, NT], I32)
    adaw_d = din("ada_w", [D, 6 * D])
    adab_d = din("ada_b", [1, 6 * D])
    gpre_mix_d = din("pre_norm_mix", [1, D])
    gpost_mix_d = din("post_norm_mix", [1, D])
    gpre_mlp_d = din("pre_norm_mlp", [1, D])
    gpost_mlp_d = din("post_norm_mlp", [1, D])
    win_d = din("w_in", [D, PW])
    gatew_d = din("gla_gate_w", [16, 256])
    gateb_d = din("gla_gate_b", [1, 256])
    glanorm_d = din("gla_norm", [1, 128])
    lq1_d = din("lambda_q1", [1, 64])
    lk1_d = din("lambda_k1", [1, 64])
    lq2_d = din("lambda_q2", [1, 64])
    lk2_d = din("lambda_k2", [1, 64])
    dnorm_d = din("diff_norm", [1, 128])
    wout_d = din("w_out", [D, D])
    wup_d = din("w_up", [D, DFF])
    wdown_d = din("w_down", [DFF, D])
    ident_d = din("k_ident", [128, 128])
    tri_d = din("k_tri", [128, 128])
    negm_d = din("k_negmask", [128, 128])
    invf_d = din("k_invf", [1, 64])
    out_d = nc.dram_tensor("out", [S_LEN, D], F32, kind="ExternalOutput").ap()

    S = Sched()
    es = ExitStack()
    with es:
        AW = 53200
        arena = es.enter_context(nc.sbuf_tensor("arena", [128, AW], F32))
        A = arena[:]
        psum = [es.enter_context(nc.psum_tensor("ps%d" % i, [128, 512], F32)) for i in range(8)]
        PB = [Buf("pb%d" % i) for i in range(8)]

        def f32v(off, cols, rows=None):
            assert off % 4 == 0 and off + cols * 4 <= AW * 4, (off, cols)
            v = A[:, off // 4: off // 4 + cols]
            return v if rows is None else v[rows[0]:rows[1]]

        def bf16v(off, cols):
            assert off % 4 == 0 and cols % 2 == 0 and off + cols * 2 <= AW * 4, (off, cols)
            return A[:, off // 4: off // 4 + cols // 2].bitcast(BF16)

        def pbf(i):
            return psum[i][:].bitcast(BF16)

        o = 0
        identb = bf16v(0, 128)
        identf = f32v(256, 128)
        tri = f32v(768, 128)
        onesf = f32v(1280, 128)
        negmb = bf16v(1792, 128)
        gnorm4 = f32v(2048, 512)
        dnormbc = f32v(4096, 128)
        sincos = f32v(4608, 1024).rearrange("p (t j) -> p t j", t=NT)
        gw = f32v(8704, 256)
        SM = 9728
        c8 = f32v(SM, 8)
        cact = f32v(SM + 32, 8)
        neglam = f32v(SM + 64, 1)
        posf = f32v(SM + 128, 16)
        posi = f32v(SM + 192, 16).bitcast(I32)
        lamt = f32v(SM + 256, 8)
        epsb = f32v(SM + 288, 1)
        gncol = f32v(SM + 296, 1)
        R_MIX = 10240
        mixT = bf16v(R_MIX, 8 * S_LEN).rearrange("p (k n) -> p k n", k=8)
        R_BC = 43008
        gmod_a = f32v(R_BC, D)
        sh_a = f32v(R_BC + 4096, D)
        gpost_a = f32v(R_BC + 8192, D)
        gmod_m = f32v(R_BC + 12288, D)
        sh_m = f32v(R_BC + 16384, D)
        gpost_m = f32v(R_BC + 20480, D)
        R_A = 67584
        w_in = bf16v(R_A, 8 * PW).rearrange("p (k n) -> p k n", k=8)
        PT = [bf16v(R_A + i * 16384, 16 * 512).rearrange("p (j n) -> p j n", j=16) for i in range(2)]
        R_B = 116992
        qT = bf16v(R_B, 4 * S_LEN).rearrange("p (h n) -> p h n", h=4)
        kT = bf16v(R_B + 16384, 4 * S_LEN).rearrange("p (h n) -> p h n", h=4)
        vaug = bf16v(R_B + 32768, NT * 4 * 130).rearrange("p (t h e) -> p t h e", t=NT, h=4)
        R_C = R_B + 32768 + NT * 4 * 130 * 2
        assert R_C == 166400
        ada_ring = [f32v(R_B + i * 16384, 8 * 512).rearrange("p (k n) -> p k n", k=8) for i in range(2)]
        cb = f32v(R_B + 32768, 8 * 128).rearrange("p (k n) -> p k n", k=8)
        adab = f32v(R_MIX, 6 * D)
        angs = f32v(R_MIX + 24576, 1024).rearrange("p (t j) -> p t j", t=NT)
        angk = f32v(R_MIX + 28672, 1024).rearrange("p (t j) -> p t j", t=NT)
        angi = f32v(R_C, 1024).bitcast(I32).rearrange("p (t j) -> p t j", t=NT)
        invfbc = f32v(R_C + 4096, 64)
        lamv = f32v(R_C + 4608, 256)
        o = R_BC + 8192
        xt = [f32v(o, D), f32v(o + 4096, D)]; o += 8192
        h1 = f32v(o, D); o += 4096
        hb2 = [bf16v(o, D), None]; o += 2048
        junk = bf16v(o, D); o += 2048
        assert o == R_BC + 24576
        o = R_C
        hTt2 = [bf16v(o + i * 2048, D).rearrange("p (k n) -> p k n", k=8) for i in range(2)]; o += 4096
        glrT = f32v(o, 128); o += 512
        ez = f32v(o, 256); o += 1024
        spz = f32v(o, 256); o += 1024
        eb2 = [f32v(o + i * 1024, 256) for i in range(2)]; o += 2048
        enb2 = [f32v(o + i * 1024, 256) for i in range(2)]; o += 2048
        gqk2 = [f32v(o + i * 2048, 512) for i in range(2)]; o += 4096
        qg = bf16v(o, 256); o += 512
        kg = bf16v(o, 256); o += 512
        qkT = bf16v(o, 512).rearrange("p (a n) -> p a n", a=4); o += 1024
        sT = bf16v(o, 512).rearrange("p (a n) -> p a n", a=4); o += 1024
        vg2 = [bf16v(o + i * 1024, 512) for i in range(2)]; o += 2048
        gate2 = [f32v(o + i * 2048, 512) for i in range(2)]; o += 4096
        mixg = bf16v(o, 512); o += 1024
        Sf = f32v(o, 256).rearrange("p (a n) -> p a n", a=2); o += 1024
        Sb = bf16v(o, 256).rearrange("p (a n) -> p a n", a=2); o += 512
        Stmp = f32v(o, 256).rearrange("p (a n) -> p a n", a=2); o += 1024
        ropeA = f32v(o, 512); o += 2048
        ropeB = f32v(o, 512); o += 2048
        ropeAk = f32v(o, 512); o += 2048
        ropeBk = f32v(o, 512); o += 2048
        qr2 = [bf16v(o + i * 1024, 512) for i in range(2)]; o += 2048
        kr2 = [bf16v(o + i * 1024, 512) for i in range(2)]; o += 2048
        hb2[1] = bf16v(o, D); o += 2048
        assert o <= AW * 4, o
        dec = f32v(SM + 320, 2)
        ssq4 = f32v(SM + 336, 4)
        rstd4 = f32v(SM + 352, 4)
        ssum = f32v(SM + 368, 2)
        rstd = f32v(SM + 376, 2)
        rden = f32v(SM + 384, 4)
        rden2 = f32v(SM + 400, 4)
        ssum2 = f32v(SM + 416, 4)
        rstd2 = f32v(SM + 432, 4)
        ssumA = f32v(SM + 448, 2)
        rstdA = f32v(SM + 456, 2)
        dec2 = [f32v(SM + 464 + i * 8, 2) for i in range(2)]
        ada_ring2 = [bf16v(100352 + i * 4096, 8 * 256).rearrange("p (k n) -> p k n", k=8) for i in range(2)]
        adab2 = [bf16v(R_C + i * 512, 256) for i in range(2)]
        cb2 = bf16v(R_C + 2048, 8 * 128).rearrange("p (k n) -> p k n", k=8)
        onesb = bf16v(R_C + 1024, 128)
        W_OUT = 182272
        w_out = bf16v(W_OUT, 8 * D).rearrange("p (k n) -> p k n", k=8)
        o = 198656
        o1 = f32v(o, 512).rearrange("p (r n) -> p r n", r=4); o += 2048
        od = f32v(o, 128); o += 512
        mixd = bf16v(o, 128); o += 256
        p2junk = bf16v(o, 128); o += 256
        p2sq = f32v(o, 128); o += 512
        od4 = [od] + [f32v(o + i * 512, 128) for i in range(3)]; o += 1536
        mixd4 = [mixd] + [bf16v(o + i * 256, 128) for i in range(3)]; o += 768
        p3x = [f32v(o, D), f32v(o + 4096, D)]; o += 8192
        assert o <= AW * 4, o
        p3u = f32v(166400, D)
        p3h1 = f32v(166400 + 4096, D)
        p3hb = bf16v(166400 + 8192, D)
        p3junk = bf16v(166400 + 10240, D)
        p3hb2 = [p3hb, bf16v(166400 + 12288, D)]
        W_UP = 67584
        w_up = bf16v(W_UP, 8 * DFF).rearrange("p (k n) -> p k n", k=8)
        W_DN = W_UP + 65536
        w_dn = bf16v(W_DN, 32 * D).rearrange("p (f n) -> p f n", f=32)
        assert W_DN + 65536 == 198656
        hidT = bf16v(R_BC, 32 * 256).rearrange("p (f n) -> p f n", f=32)
        xg = f32v(198656, 2 * D).rearrange("p (a n) -> p a n", a=2)
        p4u = f32v(198656 + 8192, D)
        assert 198656 + 8192 + 4096 <= AW * 4
        RT = [f32v(R_BC + 16384 + i * 1024, 256) for i in range(2)]
        p4junk = bf16v(R_BC + 16384 + 2048, 512)

        def op(eng, fn, r=(), w=(), **kw):
            return S.op(eng, fn, reads=r, writes=w, **kw)

        dsem = {k: S.new_dma_sem() for k in ["const", "consta", "constp"] + ["win%d" % i for i in range(7)] + [ "ada0", "ada1", "x0", "x1", "wout", "ffnu", "o0", "o1",
                                             "xm0", "xm1", "xg", "g2", "ada2_0", "ada2_1", "adb2_0", "adb2_1", "xm2", "os0", "os1", "os2"] + ["ffnd%d" % g for g in range(8)]}
        B = {}

        def bf(name):
            if name not in B:
                B[name] = Buf(name)
            return B[name]

        def rstd_from(sum_ap, out_ap, n, sbuf, wbuf, scale):
            op("dve", lambda e: e.tensor_scalar(out=out_ap, in0=sum_ap, scalar1=scale, scalar2=EPS, op0=ALU.mult, op1=ALU.add),
               r=[sbuf], w=[wbuf])
            op("act", lambda e: e.activation(out=out_ap, in_=out_ap, func=AF.Ln), r=[wbuf], w=[wbuf])
            op("act", lambda e: e.activation(out=out_ap, in_=out_ap, func=AF.Exp, scale=-0.5), r=[wbuf], w=[wbuf])

        CB = bf("consts")
        CBa = bf("consts_a")
        for dst, src in [(identf, ident_d), (tri, tri_d)]:
            op("sp", lambda e, dst=dst, src=src: e.dma_start(out=dst, in_=src), w=[CB], dma_sem=dsem["const"])
        op("pool", lambda e: e.dma_start(out=negmb, in_=negm_d), w=[bf("negmb")], dma_sem=dsem["constp"])
        op("sp", lambda e: e.dma_start(out=c8, in_=c8_d), w=[CB], dma_sem=dsem["const"])
        op("act", lambda e: e.dma_start(out=posi, in_=pos_d), w=[CBa], dma_sem=dsem["consta"])
        op("sp", lambda e: e.dma_start(out=adab[0:1, :], in_=adab_d), w=[CB], dma_sem=dsem["const"])
        op("sp", lambda e: e.dma_start(out=gmod_a, in_=gpre_mix_d.partition_broadcast(128)), w=[CB], dma_sem=dsem["const"])
        op("act", lambda e: e.dma_start(out=dnormbc, in_=dnorm_d.partition_broadcast(128)), w=[CBa], dma_sem=dsem["consta"])
        op("act", lambda e: e.dma_start(out=gncol, in_=glanorm_d.rearrange("o d -> d o")), w=[CBa], dma_sem=dsem["consta"])
        op("act", lambda e: e.dma_start(out=invfbc, in_=invf_d.partition_broadcast(128)), w=[CBa], dma_sem=dsem["consta"])
        op("sp", lambda e: e.dma_start(out=gw[0:16, :], in_=gatew_d), w=[CB], dma_sem=dsem["const"])
        op("sp", lambda e: e.dma_start(out=gw[16:17, :], in_=gateb_d), w=[CB], dma_sem=dsem["const"])
        for i, src in enumerate([lq1_d, lk1_d, lq2_d, lk2_d]):
            op("act", lambda e, i=i, src=src: e.dma_start(out=lamv[0:1, i * 64:(i + 1) * 64], in_=src), w=[CBa], dma_sem=dsem["consta"])
        WIN_BLOCKS = [(1024, 16), (0, 512), (1552, 512), (2064, 512), (512, 512), (1040, 512), (2576, 512)]
        WINB = {}
        for i, (c0, ncols) in enumerate(WIN_BLOCKS):
            WINB[c0] = bf("w_in_%d" % c0)
            op("pool", lambda e, c0=c0, ncols=ncols: e.dma_start(out=w_in[:, :, c0:c0 + ncols],
                                                                  in_=win_d[:, c0:c0 + ncols].rearrange("(k p) n -> p k n", p=128)),
               r=[CB, CBa], w=[WINB[c0]], dma_sem=dsem["win%d" % i])

        op("dve", lambda e: e.memset(onesf, 1.0), w=[bf("ones")])
        op("dve", lambda e: e.memset(epsb, EPS), w=[bf("epsb")])
        op("dve", lambda e: e.tensor_copy(out=identb, in_=identf), r=[CB, CBa], w=[bf("identb")])
        CA = bf("cact")
        op("act", lambda e: e.activation(out=cact, in_=c8, func=AF.Exp, scale=-1.0), r=[CB, CBa], w=[CA])
        op("dve", lambda e: e.tensor_scalar(out=cact, in0=cact, scalar1=1.0, scalar2=None, op0=ALU.add), r=[CA], w=[CA])
        op("dve", lambda e: e.reciprocal(out=cact, in_=cact), r=[CA], w=[CA])
        op("dve", lambda e: e.tensor_tensor(out=cact, in0=cact, in1=c8, op=ALU.mult), r=[CA, CB, CBa], w=[CA])
        CBB = bf("cb")
        for k in range(8):
            op("dve", lambda e, k=k: e.tensor_scalar(out=cb[:, k, :], in0=onesf, scalar1=cact[:, k:k + 1], scalar2=None, op0=ALU.mult),
               r=[CA, bf("ones")], w=[CBB])
        LB = bf("lam")
        op("dve", lambda e: e.tensor_tensor(out=lamv[0:1, 0:64], in0=lamv[0:1, 0:64], in1=lamv[0:1, 64:128], op=ALU.mult), r=[CB, CBa], w=[LB])
        op("dve", lambda e: e.tensor_tensor(out=lamv[0:1, 128:192], in0=lamv[0:1, 128:192], in1=lamv[0:1, 192:256], op=ALU.mult), r=[LB], w=[LB])
        op("dve", lambda e: e.reduce_sum(out=lamt[0:1, 0:1], in_=lamv[0:1, 0:64], axis=AX.X), r=[LB], w=[LB])
        op("dve", lambda e: e.reduce_sum(out=lamt[0:1, 1:2], in_=lamv[0:1, 128:192], axis=AX.X), r=[LB], w=[LB])
        op("act", lambda e: e.activation(out=lamt[0:1, 0:2], in_=lamt[0:1, 0:2], func=AF.Exp), r=[LB], w=[LB])
        op("dve", lambda e: e.scalar_tensor_tensor(out=lamt[0:1, 2:3], in0=lamt[0:1, 1:2], scalar=-LAMBDA_INIT, in1=lamt[0:1, 0:1],
                                                   op0=ALU.add, op1=ALU.subtract), r=[LB], w=[LB])
        op("pe", lambda e: e.matmul(psum[7][:, 0:1], lhsT=onesf[0:1, :], rhs=lamt[0:1, 2:3], start=True, stop=True),
           r=[LB, bf("ones")], w=[PB[7]])
        op("dve", lambda e: e.tensor_copy(out=neglam, in_=psum[7][:, 0:1]), w=[PB[7], bf("neglam")])
        op("dve", lambda e: e.tensor_scalar(out=dnormbc, in0=dnormbc, scalar1=1.0 - LAMBDA_INIT, scalar2=None, op0=ALU.mult), r=[CB, CBa], w=[bf("dnorm")])

        RB = bf("rope")
        op("dve", lambda e: e.tensor_copy(out=posf, in_=posi), r=[CB, CBa], w=[RB])
        for t in range(NT):
            op("dve", lambda e, t=t: e.tensor_scalar(out=angs[:, t, :], in0=invfbc, scalar1=posf[:, t:t + 1], scalar2=None, op0=ALU.mult),
               r=[RB, CB, CBa], w=[RB])
        op("dve", lambda e: e.tensor_scalar(out=angs[:, :, 32:64], in0=angs[:, :, 32:64], scalar1=math.pi / 2, scalar2=None, op0=ALU.add), r=[RB], w=[RB])
        op("dve", lambda e: e.tensor_scalar(out=angk, in0=angs, scalar1=1.0 / (2 * math.pi), scalar2=None, op0=ALU.mult), r=[RB], w=[RB])
        op("dve", lambda e: e.tensor_copy(out=angi, in_=angk), r=[RB], w=[RB])
        op("dve", lambda e: e.tensor_copy(out=angk, in_=angi), r=[RB], w=[RB])
        C1 = 6.28125
        C2 = 2 * math.pi - C1
        op("dve", lambda e: e.scalar_tensor_tensor(out=angs, in0=angk, scalar=-C1, in1=angs, op0=ALU.mult, op1=ALU.add), r=[RB], w=[RB])
        op("dve", lambda e: e.scalar_tensor_tensor(out=angs, in0=angk, scalar=-C2, in1=angs, op0=ALU.mult, op1=ALU.add), r=[RB], w=[RB])
        op("dve", lambda e: e.tensor_scalar(out=angk, in0=angs, scalar1=math.pi, scalar2=-2 * math.pi, op0=ALU.is_gt, op1=ALU.mult), r=[RB], w=[RB])
        op("dve", lambda e: e.tensor_tensor(out=angs, in0=angs, in1=angk, op=ALU.add), r=[RB], w=[RB])
        op("dve", lambda e: e.tensor_scalar(out=angk, in0=angs, scalar1=-math.pi, scalar2=2 * math.pi, op0=ALU.is_lt, op1=ALU.mult), r=[RB], w=[RB])
        op("dve", lambda e: e.tensor_tensor(out=angs, in0=angs, in1=angk, op=ALU.add), r=[RB], w=[RB])
        op("dve", lambda e: e.tensor_scalar(out=angs, in0=angs, scalar1=3.1415925, scalar2=-3.1415925, op0=ALU.min, op1=ALU.max), r=[RB], w=[RB])
        op("act", lambda e: e.activation(out=sincos, in_=angs, func=AF.Sin), r=[RB], w=[bf("sincos")])

        F0a = S.fence()
        ADS = [bf("adaslot0"), bf("adaslot1")]
        BCB = {n: bf("bc_" + n) for n in ["gmod_a", "sh_a", "gpost_a", "gmod_m", "sh_m", "gpost_m"]}
        targets = [("sh_a", sh_a, "copy"), ("gmod_a", gmod_a, "mod"), ("gpost_a", gpost_a, "mul"),
                   ("sh_m", sh_m, "copy"), ("gmod_m", gmod_m, "mod"), ("gpost_m", gpost_m, "mul")]
        for n in range(4):
            slot = n % 2
            op("sp", lambda e, n=n, slot=slot: e.dma_start(out=ada_ring[slot],
                                                            in_=adaw_d[:, n * 512:(n + 1) * 512].rearrange("(k p) n -> p k n", p=128)),
               w=[ADS[slot]], dma_sem=dsem["ada%d" % slot])
            bank = 5 + slot
            for k in range(8):
                op("pe", lambda e, k=k, slot=slot, bank=bank: e.matmul(psum[bank][:], lhsT=cb[:, k, :], rhs=ada_ring[slot][:, k, :],
                                                                        start=(k == 0), stop=False),
                   r=[CBB, ADS[slot]], w=[PB[bank]])
            op("pe", lambda e, n=n, bank=bank: e.matmul(psum[bank][:], lhsT=onesf[0:1, :], rhs=adab[0:1, n * 512:(n + 1) * 512],
                                                        start=False, stop=True), r=[CB, CBa, bf("ones")], w=[PB[bank]])
            name, tile, kind = targets[n // 2]
            dst = tile[:, (n % 2) * 512:(n % 2 + 1) * 512]
            if kind == "copy":
                op("dve", lambda e, dst=dst, bank=bank: e.tensor_copy(out=dst, in_=psum[bank][:]), w=[PB[bank], BCB[name]])
            elif kind == "mod":
                op("dve", lambda e, dst=dst, bank=bank: e.scalar_tensor_tensor(out=dst, in0=psum[bank][:], scalar=1.0, in1=dst,
                                                                                op0=ALU.add, op1=ALU.mult), r=[CB, CBa], w=[PB[bank], BCB[name]])
            else:
                op("dve", lambda e, dst=dst, bank=bank: e.tensor_tensor(out=dst, in0=psum[bank][:], in1=dst, op=ALU.mult),
                   r=[CB, CBa], w=[PB[bank], BCB[name]])
        F0 = S.fence()

        XB = [bf("xt0"), bf("xt1")]
        QTB = bf("qT"); KTB = bf("kT"); VAB = bf("vaug")
        MIXB = [bf("mix%d" % t) for t in range(NT)]
        STB = bf("state")
        op("pool", lambda e: e.memset(Sf, 0.0), w=[STB], extra=F0a)
        op("pool", lambda e: e.memset(Sb, 0.0), w=[STB])
        op("pool", lambda e: e.memset(glrT[0:32, :], 1.0), w=[bf("glrT")], extra=F0a)

        NP1 = n_p1 if debug_phase >= 1 else 0

        def p1_tile(t):
            par = t % 2
            xst = {"x": F0a if t < 2 else None}

            def o_(eng, fn, r=(), w=()):
                return S.op(eng, fn, reads=r, writes=w, extra=xst["x"])

            P = lambda n: bf("%s_%d" % (n, par))
            hTt = hTt2[par]; eb = eb2[par]; enb = enb2[par]; gqk = gqk2[par]; vg = vg2[par]; gate = gate2[par]; dec = dec2[par]
            hb = hb2[par]; qr = qr2[par]; kr = kr2[par]
            S.op("sp", lambda e: e.dma_start(out=xt[par], in_=x_d[t * 128:(t + 1) * 128, :]), writes=[XB[par]],
                 dma_sem=dsem["x%d" % par])
            o_("act", lambda e: e.activation(out=junk, in_=xt[par], func=AF.Square, accum_out=ssumA[:, par:par + 1]),
               r=[XB[par]], w=[bf("junk"), P("ssumA")])
            o_("act", lambda e: e.activation(out=rstdA[:, par:par + 1], in_=ssumA[:, par:par + 1], func=AF.Ln, scale=1.0 / D, bias=epsb),
               r=[P("ssumA"), bf("epsb")], w=[P("rstdA")])
            o_("act", lambda e: e.activation(out=rstdA[:, par:par + 1], in_=rstdA[:, par:par + 1], func=AF.Exp, scale=-0.5),
               r=[P("rstdA")], w=[P("rstdA")])
            o_("dve", lambda e: e.scalar_tensor_tensor(out=h1, in0=xt[par], scalar=rstdA[:, par:par + 1], in1=gmod_a, op0=ALU.mult, op1=ALU.mult),
               r=[XB[par], P("rstdA"), BCB["gmod_a"]], w=[bf("h1")])
            o_("pool", lambda e: e.tensor_tensor(out=hb, in0=h1, in1=sh_a, op=ALU.add), r=[bf("h1"), BCB["sh_a"]], w=[P("hb")])
            yield 1
            for k in range(8):
                o_("pe", lambda e, k=k: e.transpose(out=pbf(0)[:, k * 128:(k + 1) * 128], in_=hb[:, k * 128:(k + 1) * 128], identity=identb),
                   r=[P("hb"), bf("identb")], w=[PB[0]])
            o_("act", lambda e: e.copy(out=hTt.rearrange("p k n -> p (k n)"), in_=pbf(0)), w=[PB[0], P("hTt")])
            yield 1

            xst["x"] = F0 if t < 2 else None
            if t == 0:
                o_("pool", lambda e: e.memset(vaug[:, :, :, 128:130], 1.0), w=[VAB])

            def inproj(bank, c0, ncols):
                for k in range(8):
                    o_("pe", lambda e, k=k: e.matmul(psum[bank][:, 0:ncols], lhsT=hTt[:, k, :], rhs=w_in[:, k, c0:c0 + ncols],
                                                     start=(k == 0), stop=(k == 7)), r=[P("hTt"), WINB[c0]], w=[PB[bank]])

            def rope(bank, dst, dname, RA, RBf, aname, bname):
                cos2 = sincos[:, t, 32:64].unsqueeze(1).unsqueeze(1).to_broadcast([128, 8, 2, 32])
                sinb = sincos[:, t, 0:32].unsqueeze(1).to_broadcast([128, 8, 32])
                SC = bf("sincos")
                src4 = psum[bank][:].rearrange("p (g two j) -> p g two j", g=8, two=2)
                A4 = RA.rearrange("p (g two j) -> p g two j", g=8, two=2)
                B4 = RBf.rearrange("p (g two j) -> p g two j", g=8, two=2)
                D4 = dst.rearrange("p (g two j) -> p g two j", g=8, two=2)
                o_("dve", lambda e: e.tensor_tensor(out=A4, in0=src4, in1=cos2, op=ALU.mult), r=[SC], w=[PB[bank], bf(aname)])
                o_("dve", lambda e: e.tensor_tensor(out=B4[:, :, 0, :], in0=src4[:, :, 1, :], in1=sinb, op=ALU.mult),
                   r=[SC], w=[PB[bank], bf(bname)])
                o_("dve", lambda e: e.tensor_tensor(out=B4[:, :, 1, :], in0=src4[:, :, 0, :], in1=sinb, op=ALU.mult),
                   r=[SC], w=[PB[bank], bf(bname)])
                o_("pool", lambda e: e.tensor_tensor(out=D4[:, :, 0, :], in0=A4[:, :, 0, :], in1=B4[:, :, 0, :], op=ALU.subtract),
                   r=[bf(aname), bf(bname)], w=[bf(dname)])
                o_("pool", lambda e: e.tensor_tensor(out=D4[:, :, 1, :], in0=A4[:, :, 1, :], in1=B4[:, :, 1, :], op=ALU.add),
                   r=[bf(aname), bf(bname)], w=[bf(dname)])

            for k in range(8):
                o_("pe", lambda e, k=k: e.matmul(psum[5][0:16, 0:128], lhsT=w_in[:, k, 1024:1040], rhs=hTt[:, k, :],
                                                 start=(k == 0), stop=(k == 7)), r=[P("hTt"), WINB[1024]], w=[PB[5]])
            o_("act", lambda e: e.copy(out=glrT[0:16, :], in_=psum[5][0:16, 0:128]), w=[PB[5], bf("glrT")])
            inproj(1, 0, 512)
            yield 0
            o_("pe", lambda e: e.matmul(psum[4][:, 0:256], lhsT=glrT[0:17, :], rhs=gw[0:17, :], start=True, stop=True),
               r=[bf("glrT"), CB, CBa], w=[PB[4]])
            o_("act", lambda e: e.activation(out=ez, in_=psum[4][:, 0:256], func=AF.Exp, scale=-1.0), w=[PB[4], bf("ez")])
            o_("act", lambda e: e.activation(out=spz, in_=ez, func=AF.Ln, bias=1.0), r=[bf("ez")], w=[bf("spz")])
            inproj(2, 1552, 512)
            yield 0
            o_("act", lambda e: e.copy(out=gqk, in_=psum[1][:]), w=[PB[1], P("gqk")])
            rope(2, qr, "qr_%d" % par, ropeA, ropeB, "ropeAq", "ropeBq")
            inproj(5, 2064, 512)
            yield 0
            rope(5, kr, "kr_%d" % par, ropeAk, ropeBk, "ropeAk", "ropeBk")
            o_("pe", lambda e: e.matmul(psum[4][:, 256:512], lhsT=tri, rhs=spz, start=True, stop=True), r=[CB, CBa, bf("spz")], w=[PB[4]])
            o_("act", lambda e: e.activation(out=eb, in_=psum[4][:, 256:512], func=AF.Exp, scale=-1.0 / 16), w=[PB[4], P("eb")])
            o_("act", lambda e: e.activation(out=enb, in_=psum[4][:, 256:512], func=AF.Exp, scale=1.0 / 16), w=[PB[4], P("enb")])
            for pr in range(2):
                o_("pe", lambda e, pr=pr: e.matmul(psum[4][:, pr:pr + 1], lhsT=spz[:, pr * 128:(pr + 1) * 128], rhs=onesf[:, 0:1],
                                                   start=True, stop=True), r=[bf("spz"), bf("ones")], w=[PB[4]])
            o_("act", lambda e: e.activation(out=dec, in_=psum[4][:, 0:2], func=AF.Exp, scale=-1.0 / 16), w=[PB[4], P("dec")])
            inproj(1, 512, 512)
            yield 0
            o_("act", lambda e: e.copy(out=vg, in_=psum[1][:]), w=[PB[1], P("vg")])
            inproj(2, 1040, 512)
            yield 0
            o_("act", lambda e: e.activation(out=gate, in_=psum[2][:], func=AF.Exp, scale=-1.0), w=[PB[2], P("gate")])
            o_("act", lambda e: e.activation(out=gate, in_=gate, func=AF.Ln, bias=1.0), r=[P("gate")], w=[P("gate")])
            o_("act", lambda e: e.activation(out=gate, in_=gate, func=AF.Exp, scale=-1.0), r=[P("gate")], w=[P("gate")])
            o_("dve", lambda e: e.tensor_tensor(out=gate, in0=psum[2][:], in1=gate, op=ALU.mult), r=[P("gate")], w=[PB[2], P("gate")])
            inproj(5, 2576, 512)
            yield 0
            o_("act", lambda e: e.copy(out=vaug[:, t, :, 0:128], in_=psum[5][:].rearrange("p (h n) -> p h n", h=4)), w=[PB[5], VAB])
            yield 1

            o_("dve", lambda e: e.scalar_tensor_tensor(out=qg, in0=gqk[:, 0:256], scalar=0.125, in1=eb, op0=ALU.mult, op1=ALU.mult),
               r=[P("gqk"), P("eb")], w=[bf("qg")])
            o_("dve", lambda e: e.tensor_tensor(out=kg, in0=gqk[:, 256:512], in1=enb, op=ALU.mult), r=[P("gqk"), P("enb")], w=[bf("kg")])
            for a in range(4):
                src = qg if a < 2 else kg
                pr = a % 2
                o_("pe", lambda e, a=a, src=src, pr=pr: e.transpose(out=pbf(6)[:, a * 128:(a + 1) * 128], in_=src[:, pr * 128:(pr + 1) * 128],
                                                                      identity=identb), r=[bf("qg"), bf("kg"), bf("identb")], w=[PB[6]])
            o_("act", lambda e: e.copy(out=qkT.rearrange("p a n -> p (a n)"), in_=pbf(6)[:, 0:512]), w=[PB[6], bf("qkT")])
            yield 0
            for hh in range(4):
                pr, hf = hh // 2, hh % 2
                sbk = 6 if hf == 0 else 7
                o_("pe", lambda e, pr=pr, hf=hf, sbk=sbk: e.matmul(psum[sbk][:, pr * 128:(pr + 1) * 128],
                                                                   lhsT=qkT[hf * 64:(hf + 1) * 64, 2 + pr, :], rhs=qkT[hf * 64:(hf + 1) * 64, pr, :],
                                                                   start=True, stop=True), r=[bf("qkT")], w=[PB[sbk]])
            for hh in range(4):
                pr, hf = hh // 2, hh % 2
                sbk = 6 if hf == 0 else 7
                o_("dve", lambda e, hh=hh, pr=pr, sbk=sbk: e.tensor_tensor(out=sT[:, hh, :], in0=psum[sbk][:, pr * 128:(pr + 1) * 128], in1=tri, op=ALU.mult),
                   r=[CB, CBa], w=[PB[sbk], bf("sT")])
            yield 0
            for hh in range(4):
                pr, hf = hh // 2, hh % 2
                o_("pe", lambda e, hh=hh: e.matmul(psum[7][:, hh * 128:(hh + 1) * 128], lhsT=sT[:, hh, :], rhs=vg[:, hh * 128:(hh + 1) * 128],
                                                   start=True, stop=False), r=[bf("sT"), P("vg")], w=[PB[7]])
                o_("pe", lambda e, hh=hh, pr=pr, hf=hf: e.matmul(psum[7][:, hh * 128:(hh + 1) * 128], lhsT=qkT[hf * 64:(hf + 1) * 64, pr, :],
                                                                   rhs=Sb[hf * 64:(hf + 1) * 64, pr, :], start=False, stop=True),
                   r=[bf("qkT"), STB], w=[PB[7]])
            for pr in range(2):
                o_("pe", lambda e, pr=pr: e.matmul(psum[6][:, pr * 256:(pr + 1) * 256], lhsT=kg[:, pr * 128:(pr + 1) * 128],
                                                   rhs=vg[:, pr * 256:(pr + 1) * 256], start=True, stop=True), r=[bf("kg"), P("vg")], w=[PB[6]])
            for pr in range(2):
                for hf in range(2):
                    rows = slice(hf * 64, (hf + 1) * 64)
                    o_("dve", lambda e, pr=pr, hf=hf, rows=rows: e.tensor_scalar(
                        out=Stmp[rows, pr, :], in0=psum[6][rows, pr * 256 + hf * 128: pr * 256 + (hf + 1) * 128],
                        scalar1=dec[rows, pr:pr + 1], scalar2=None, op0=ALU.mult), r=[P("dec")], w=[PB[6], bf("Stmp")])
            for pr in range(2):
                o_("dve", lambda e, pr=pr: e.scalar_tensor_tensor(out=Sf[:, pr, :], in0=Sf[:, pr, :], scalar=dec[:, pr:pr + 1], in1=Stmp[:, pr, :],
                                                                   op0=ALU.mult, op1=ALU.add), r=[P("dec"), bf("Stmp")], w=[STB])
                o_("pool", lambda e, pr=pr: e.tensor_copy(out=Sb[:, pr, :], in_=Sf[:, pr, :]), w=[STB])
            yield 0
            for hh in range(4):
                o_("act", lambda e, hh=hh: e.activation(out=junk[:, 0:128], in_=psum[7][:, hh * 128:(hh + 1) * 128], func=AF.Square,
                                                        accum_out=ssq4[:, hh:hh + 1]), w=[PB[7], bf("junk"), bf("ssq4")])
            o_("act", lambda e: e.activation(out=rstd4, in_=ssq4, func=AF.Ln, scale=1.0 / 128, bias=epsb), r=[bf("ssq4"), bf("epsb")], w=[bf("rstd4")])
            o_("act", lambda e: e.activation(out=rstd4, in_=rstd4, func=AF.Exp, scale=-0.5), r=[bf("rstd4")], w=[bf("rstd4")])
            for hh in range(4):
                o_("dve", lambda e, hh=hh: e.scalar_tensor_tensor(out=mixg[:, hh * 128:(hh + 1) * 128], in0=psum[7][:, hh * 128:(hh + 1) * 128],
                                                                   scalar=rstd4[:, hh:hh + 1], in1=gate[:, hh * 128:(hh + 1) * 128],
                                                                   op0=ALU.mult, op1=ALU.mult), r=[bf("rstd4"), P("gate")], w=[PB[7], bf("mixg")])
            yield 0
            for hh in range(4):
                o_("pe", lambda e, hh=hh: e.transpose(out=pbf(6)[:, hh * 128:(hh + 1) * 128], in_=mixg[:, hh * 128:(hh + 1) * 128], identity=identb),
                   r=[bf("mixg"), bf("identb")], w=[PB[6]])
            o_("act", lambda e: e.activation(out=mixT[:, 0:4, t * 128:(t + 1) * 128], in_=pbf(6)[:, 0:512].rearrange("p (a n) -> p a n", a=4),
                                             func=AF.Copy, scale=gncol[:, 0:1]), r=[CB, CBa], w=[PB[6], MIXB[t]])
            yield 0
            for a in range(8):
                src = qr if a < 4 else kr
                hh = a % 4
                o_("pe", lambda e, a=a, src=src, hh=hh: e.transpose(out=pbf(3)[:, a * 128:(a + 1) * 128], in_=src[:, hh * 128:(hh + 1) * 128],
                                                                      identity=identb), r=[P("qr"), P("kr"), bf("identb")], w=[PB[3]])
            o_("dve", lambda e: e.tensor_copy(out=qT[:, :, t * 128:(t + 1) * 128], in_=pbf(3)[:, 0:512].rearrange("p (a n) -> p a n", a=4)),
               w=[PB[3], QTB])
            o_("dve", lambda e: e.tensor_copy(out=kT[:, :, t * 128:(t + 1) * 128], in_=pbf(3)[:, 512:1024].rearrange("p (a n) -> p a n", a=4)),
               w=[PB[3], KTB])
            yield 1

        def run_pipeline(gens, nstages, newest_first=False):
            n = len(gens)
            done = [False] * n
            for step in range(n + nstages - 1):
                act = [t for t in range(n) if t <= step < t + nstages and not done[t]]
                if newest_first:
                    act = act[::-1]
                fin = {t: False for t in act}
                while not all(fin.values()):
                    for t in act:
                        if fin[t]:
                            continue
                        try:
                            v = next(gens[t])
                        except StopIteration:
                            done[t] = True
                            v = 1
                        if v == 1:
                            fin[t] = True

        run_pipeline([p1_tile(t) for t in range(NP1)], 4, newest_first=True)
        F1 = S.fence()

        WOB = bf("w_out")

        def load_w_out():
            for k in range(8):
                op("pool", lambda e, k=k: e.dma_start(out=w_out[:, k, :], in_=wout_d[k * 128:(k + 1) * 128, :]), w=[WOB], dma_sem=dsem["wout"],
                   extra=(F1 if k == 0 else None))
        PTB = [[bf("pt%d_%d" % (m, j)) for j in range(16)] for m in range(2)]
        GB = bf("gains2")
        op("sp", lambda e: e.dma_start(out=gpost_a, in_=gpost_mix_d.partition_broadcast(128)), w=[GB], dma_sem=dsem["g2"], extra=F1)
        op("sp", lambda e: e.dma_start(out=gmod_m, in_=gpre_mlp_d.partition_broadcast(128)), w=[GB], dma_sem=dsem["g2"])
        op("sp", lambda e: e.dma_start(out=gpost_m, in_=gpost_mlp_d.partition_broadcast(128)), w=[GB], dma_sem=dsem["g2"])
        CB2 = bf("cb2")
        op("dve", lambda e: e.tensor_copy(out=onesb, in_=onesf), r=[bf("ones")], w=[bf("onesb")], extra=F1)
        for k in range(8):
            op("dve", lambda e, k=k: e.tensor_scalar(out=cb2[:, k, :], in0=onesf, scalar1=cact[:, k:k + 1], scalar2=None, op0=ALU.mult),
               r=[CA, bf("ones")], w=[CB2], extra=(F1 if k == 0 else None))
        ADS2 = [bf("ada2slot0"), bf("ada2slot1")]
        ADB2 = [bf("adab2_0"), bf("adab2_1")]

        def ada_chunk2(n2):
            slot = n2 % 2
            c0 = 2048 + n2 * 256
            op("pool", lambda e: e.dma_start(out=ada_ring2[slot], in_=adaw_d[:, c0:c0 + 256].rearrange("(k p) n -> p k n", p=128)),
               w=[ADS2[slot]], dma_sem=dsem["ada2_%d" % slot], extra=(F1 if n2 < 2 else None))
            op("pool", lambda e: e.dma_start(out=adab2[slot][0:1, :], in_=adab_d[:, c0:c0 + 256]),
               w=[ADB2[slot]], dma_sem=dsem["adb2_%d" % slot], extra=(F1 if n2 < 2 else None))
            for k in range(8):
                op("pe", lambda e, k=k: e.matmul(psum[6][:, 0:256], lhsT=cb2[:, k, :], rhs=ada_ring2[slot][:, k, :],
                                                 start=(k == 0), stop=False), r=[CB2, ADS2[slot]], w=[PB[6]])
            op("pe", lambda e: e.matmul(psum[6][:, 0:256], lhsT=onesb[0:1, :], rhs=adab2[slot][0:1, :], start=False, stop=True),
               r=[ADB2[slot], bf("onesb")], w=[PB[6]])
            name, tile, kind = targets[c0 // 1024]
            dst = tile[:, c0 % 1024: c0 % 1024 + 256]
            if kind == "copy":
                op("dve", lambda e: e.tensor_copy(out=dst, in_=psum[6][:, 0:256]), w=[PB[6], BCB[name]], extra=(F1 if n2 < 2 else None))
            elif kind == "mod":
                op("dve", lambda e: e.scalar_tensor_tensor(out=dst, in0=psum[6][:, 0:256], scalar=1.0, in1=dst, op0=ALU.add, op1=ALU.mult),
                   r=[GB], w=[PB[6], BCB[name]])
            else:
                op("dve", lambda e: e.tensor_tensor(out=dst, in0=psum[6][:, 0:256], in1=dst, op=ALU.mult), r=[GB], w=[PB[6], BCB[name]])

        st2 = dict(sbank=0, abank=0, first=True)

        def p2_qk(hh, c, m):
            rows = slice(m * 64, (m + 1) * 64)
            for j in range(4 * c + 4):
                r_ = j - 4 * c
                off = max(r_, 0) * 128
                bank = (0, 1, 2, 3, 7)[st2["sbank"] % 5]
                st2["sbank"] += 1
                op("pe", lambda e, bank=bank, off=off, j=j, r_=r_: e.matmul(
                    psum[bank][:, off:512], lhsT=kT[rows, hh, j * 128:(j + 1) * 128], rhs=qT[rows, hh, c * 512 + off:(c + 1) * 512],
                    start=True, stop=(r_ < 0)), r=[KTB, QTB], w=[PB[bank]])
                if r_ >= 0:
                    op("pe", lambda e, bank=bank, off=off: e.matmul(psum[bank][:, off:off + 128], lhsT=identb, rhs=negmb,
                                                                     start=False, stop=True), r=[bf("identb"), bf("negmb")], w=[PB[bank]])
                op("act", lambda e, bank=bank, off=off, j=j: e.activation(out=PT[m][:, j, off:512], in_=psum[bank][:, off:512],
                                                                          func=AF.Exp, scale=0.125),
                   w=[PB[bank], PTB[m][j]] + (list(WINB.values()) if st2["first"] else []))
                st2["first"] = False

        def p2_pv(hh, c, m):
            for r_ in range(4):
                i = 4 * c + r_
                bank = 4 + st2["abank"] % 2
                st2["abank"] += 1
                for j in range(i + 1):
                    op("pe", lambda e, bank=bank, j=j, r_=r_, i=i: e.matmul(
                        psum[bank][:, 0:129], lhsT=PT[m][:, j, r_ * 128:(r_ + 1) * 128], rhs=vaug[:, j, hh, 0:129],
                        start=(j == 0), stop=(j == i)), r=[PTB[m][j], VAB], w=[PB[bank]])
                if m == 0:
                    op("dve", lambda e, bank=bank, r_=r_: e.reciprocal(out=rden[:, r_:r_ + 1], in_=psum[bank][:, 128:129]),
                       w=[PB[bank], bf("rden%d" % r_)])
                    op("dve", lambda e, bank=bank, r_=r_: e.tensor_scalar(out=o1[:, r_, :], in0=psum[bank][:, 0:128],
                                                                           scalar1=rden[:, r_:r_ + 1], scalar2=None, op0=ALU.mult),
                       r=[bf("rden%d" % r_)], w=[PB[bank], bf("o1_%d" % r_)])
                else:
                    q = r_
                    odq, mixdq = od4[q], mixd4[q]
                    ODB, MXB, RDB, SSB, RSB = bf("od%d" % q), bf("mixd%d" % q), bf("rdenb%d" % q), bf("ssumb%d" % q), bf("rstdb%d" % q)
                    rd = rden2[:, q:q + 1]
                    ss = ssum2[:, q:q + 1]
                    rs = rstd2[:, q:q + 1]
                    op("dve", lambda e, bank=bank, rd=rd: e.reciprocal(out=rd, in_=psum[bank][:, 128:129]), w=[PB[bank], RDB])
                    op("dve", lambda e, rd=rd: e.tensor_tensor(out=rd, in0=rd, in1=neglam, op=ALU.mult), r=[bf("neglam")], w=[RDB])
                    op("dve", lambda e, bank=bank, r_=r_, rd=rd, odq=odq: e.scalar_tensor_tensor(out=odq, in0=psum[bank][:, 0:128], scalar=rd,
                                                                                              in1=o1[:, r_, :], op0=ALU.mult, op1=ALU.add),
                       r=[RDB, bf("o1_%d" % r_)], w=[PB[bank], ODB])
                    op("dve", lambda e, odq=odq: e.tensor_tensor(out=p2sq, in0=odq, in1=odq, op=ALU.mult), r=[ODB], w=[bf("p2sq")])
                    op("dve", lambda e, ss=ss: e.reduce_sum(out=ss, in_=p2sq, axis=AX.X), r=[bf("p2sq")], w=[bf("ssum2all")])

        def p2_c2(hh, c, m):
            if m == 0:
                return
            op("act", lambda e: e.activation(out=rstd2, in_=ssum2, func=AF.Ln, scale=1.0 / 128, bias=epsb), r=[bf("ssum2all"), bf("epsb")], w=[bf("rstd2all")])
            op("act", lambda e: e.activation(out=rstd2, in_=rstd2, func=AF.Exp, scale=-0.5), r=[bf("rstd2all")], w=[bf("rstd2all")])
            for r_ in range(4):
                op("dve", lambda e, r_=r_: e.scalar_tensor_tensor(out=mixd4[r_], in0=od4[r_], scalar=rstd2[:, r_:r_ + 1], in1=dnormbc,
                                                                   op0=ALU.mult, op1=ALU.mult),
                   r=[bf("od%d" % r_), bf("rstd2all"), bf("dnorm")], w=[bf("mixd%d" % r_)])

        def p2_tr(hh, c, m):
            if m == 0:
                return
            for r_ in range(4):
                op("pe", lambda e, r_=r_: e.transpose(out=pbf(6)[:, r_ * 128:(r_ + 1) * 128], in_=mixd4[r_], identity=identb),
                   r=[bf("mixd%d" % r_), bf("identb")], w=[PB[6]])
            op("dve", lambda e: e.tensor_copy(out=mixT[:, 4 + hh, c * 512:(c + 1) * 512], in_=pbf(6)[:, 0:512]),
               w=[PB[6]] + [MIXB[4 * c + r_] for r_ in range(4)])

        NH = 4 if debug_phase >= 2 else 0
        iters = [(hh, c, m) for hh in range(NH) for c in range(4) for m in range(2)]
        if iters:
            p2_qk(*iters[0])
        for n, it in enumerate(iters):
            if it[2] == 0 and n >= 4:
                ada_chunk2(n // 2 - 2)
            if n in (27, 29):
                ada_chunk2(14 + (n - 27) // 2)
            if n == 8:
                load_w_out()
                op("pool", lambda e: e.dma_start(out=w_up[:, 5, :], in_=wup_d[5 * 128:6 * 128, :]), w=[bf("w_up")], dma_sem=dsem["ffnu"], extra=F1)
                st2["pref"] = {5}
            if n >= 1:
                p2_c2(*iters[n - 1])
            if n + 1 < len(iters):
                p2_qk(*iters[n + 1])
            if n >= 1:
                p2_tr(*iters[n - 1])
            p2_pv(*it)
        if iters:
            p2_c2(*iters[-1])
            p2_tr(*iters[-1])
            op("pool", lambda e: e.dma_start(out=w_up[:, 4, :], in_=wup_d[4 * 128:5 * 128, :]), w=[bf("w_up"), ADS2[0], ADS2[1]],
               dma_sem=dsem["ffnu"])
            st2["pref"] = st2.get("pref", set()) | {4}
        else:
            load_w_out()
        F2 = S.fence()

        WUB = bf("w_up"); WDBS = [bf("w_dn%d" % g) for g in range(8)]
        for k in range(8):
            if k in st2.get("pref", set()):
                continue
            op("pool", lambda e, k=k: e.dma_start(out=w_up[:, k, :], in_=wup_d[k * 128:(k + 1) * 128, :]), w=[WUB], dma_sem=dsem["ffnu"],
               extra=(F2 if k == 0 else None))
        for g in range(4):
            op("pool", lambda e, g=g: e.dma_start(out=w_dn[:, g * 4:(g + 1) * 4, :],
                                                   in_=wdown_d[g * 512:(g + 1) * 512, :].rearrange("(f p) n -> p f n", p=128)),
               w=[WDBS[g]], dma_sem=dsem["ffnd%d" % g])
        P3X = [bf("p3x0"), bf("p3x1"), bf("p3x2")]
        OUTD = [bf("outd%d" % t) for t in range(NT)]
        NP3 = NT if debug_phase >= 3 else 0
        p3x3 = [f32v(198656 + i * 4096, D) for i in range(3)]
        ssq_y = [f32v(SM + 480, 2), f32v(SM + 488, 2)]
        sm_y = f32v(SM + 496, 4)

        def p3_tile(t):
            par = t % 2
            slot = t % 3
            X2 = F2 if t < 3 else None
            xx = p3x3[slot]
            XS = P3X[slot]
            yb = (0, 1) if par == 0 else (2, 3)
            P = lambda n: bf("p3%s_%d" % (n, par))

            def o_(eng, fn, r=(), w=(), **kw):
                return S.op(eng, fn, reads=r, writes=w, extra=X2, **kw)

            o_("sp", lambda e: e.dma_start(out=xx, in_=x_d[t * 128:(t + 1) * 128, :]), w=[XS], dma_sem=dsem["xm%d" % slot])
            for half in range(2):
                for k in range(8):
                    o_("pe", lambda e, half=half, k=k: e.matmul(psum[yb[half]][:], lhsT=mixT[:, k, t * 128:(t + 1) * 128],
                                                                  rhs=w_out[:, k, half * 512:(half + 1) * 512], start=(k == 0), stop=(k == 7)),
                       r=[MIXB[t], WOB], w=[PB[yb[half]]])
                yield 0
            for half in range(2):
                o_("act", lambda e, half=half: e.activation(out=p3junk[:, 0:512], in_=psum[yb[half]][:], func=AF.Square,
                                                            accum_out=ssq_y[par][:, half:half + 1]),
                   w=[PB[yb[half]], bf("p3junk"), P("ssqy")])
            yield 1
            o_("dve", lambda e: e.tensor_tensor(out=sm_y[:, par:par + 1], in0=ssq_y[par][:, 0:1], in1=ssq_y[par][:, 1:2], op=ALU.add),
               r=[P("ssqy")], w=[P("smy")])
            o_("act", lambda e: e.activation(out=sm_y[:, par:par + 1], in_=sm_y[:, par:par + 1], func=AF.Ln, scale=1.0 / D, bias=epsb),
               r=[P("smy"), bf("epsb")], w=[P("smy")])
            o_("act", lambda e: e.activation(out=sm_y[:, par:par + 1], in_=sm_y[:, par:par + 1], func=AF.Exp, scale=-0.5), r=[P("smy")], w=[P("smy")])
            yield 0
            for half in range(2):
                cs = slice(half * 512, (half + 1) * 512)
                o_("dve", lambda e, half=half, cs=cs: e.scalar_tensor_tensor(out=p3u[:, cs], in0=psum[yb[half]][:], scalar=sm_y[:, par:par + 1],
                                                                              in1=gpost_a[:, cs], op0=ALU.mult, op1=ALU.mult),
                   r=[P("smy"), BCB["gpost_a"]], w=[PB[yb[half]], bf("p3u")])
            o_("dve", lambda e: e.tensor_tensor(out=xx, in0=xx, in1=p3u, op=ALU.add), r=[bf("p3u")], w=[XS])
            yield 0
            o_("sp", lambda e: e.dma_start(out=out_d[t * 128:(t + 1) * 128, :], in_=xx), r=[XS], w=[OUTD[t]], dma_sem=dsem["os%d" % slot])
            o_("act", lambda e: e.activation(out=p3junk, in_=xx, func=AF.Square, accum_out=sm_y[:, 2 + par:3 + par]),
               r=[XS], w=[bf("p3junk"), P("ssh")])
            yield 1
            o_("act", lambda e: e.activation(out=sm_y[:, 2 + par:3 + par], in_=sm_y[:, 2 + par:3 + par], func=AF.Ln, scale=1.0 / D, bias=epsb),
               r=[P("ssh"), bf("epsb")], w=[P("ssh")])
            o_("act", lambda e: e.activation(out=sm_y[:, 2 + par:3 + par], in_=sm_y[:, 2 + par:3 + par], func=AF.Exp, scale=-0.5), r=[P("ssh")], w=[P("ssh")])
            yield 0
            o_("dve", lambda e: e.scalar_tensor_tensor(out=p3h1, in0=xx, scalar=sm_y[:, 2 + par:3 + par], in1=gmod_m, op0=ALU.mult, op1=ALU.mult),
               r=[XS, P("ssh"), BCB["gmod_m"]], w=[bf("p3h1")])
            hbp = p3hb2[par]
            o_("dve", lambda e: e.tensor_tensor(out=hbp, in0=p3h1, in1=sh_m, op=ALU.add), r=[bf("p3h1"), BCB["sh_m"]], w=[P("hb")])
            yield 1
            tb = 4 + par
            for k in range(8):
                o_("pe", lambda e, k=k: e.transpose(out=pbf(tb)[:, k * 128:(k + 1) * 128], in_=hbp[:, k * 128:(k + 1) * 128], identity=identb),
                   r=[P("hb"), bf("identb")], w=[PB[tb]])
            o_("act", lambda e: e.copy(out=mixT[:, :, t * 128:(t + 1) * 128], in_=pbf(tb).rearrange("p (k n) -> p k n", k=8)),
               w=[PB[tb], MIXB[t]])
            yield 1

        run_pipeline([p3_tile(t) for t in range(NP3)], 4, newest_first=True)
        F3 = S.fence()

        for g in range(4, 8):
            op("pool", lambda e, g=g: e.dma_start(out=w_dn[:, g * 4:(g + 1) * 4, :],
                                                   in_=wdown_d[g * 512:(g + 1) * 512, :].rearrange("(f p) n -> p f n", p=128)),
               w=[WDBS[g]], dma_sem=dsem["ffnd%d" % g], extra=(F3 if g == 4 else None))
        HB = [bf("hid%d" % f) for f in range(32)]
        XGB = bf("xg")
        RTB = [bf("rt0"), bf("rt1")]
        fin = []
        NG = 8 if debug_phase >= 4 else 0
        ub = 0
        for g in range(NG):
            for tt in range(2):
                t = 2 * g + tt
                op("sp", lambda e, t=t, tt=tt: e.dma_start(out=xg[:, tt, :], in_=out_d[t * 128:(t + 1) * 128, :]), r=[OUTD[t]], w=[XGB],
                   dma_sem=dsem["xg"], extra=(F3 if g == 0 else None))
            for f in range(32):
                bank = 4 + ub % 4
                rt = ub % 2
                ub += 1
                for k in range(8):
                    op("pe", lambda e, bank=bank, f=f, k=k, g=g: e.matmul(psum[bank][:, 0:256], lhsT=w_up[:, k, f * 128:(f + 1) * 128],
                                                                            rhs=mixT[:, k, g * 256:(g + 1) * 256], start=(k == 0), stop=(k == 7)),
                       r=[WUB, MIXB[2 * g], MIXB[2 * g + 1]], w=[PB[bank]])
                op("act", lambda e, bank=bank, rt=rt: e.activation(out=RT[rt], in_=psum[bank][:, 0:256], func=AF.Relu),
                   w=[PB[bank], RTB[rt]], extra=(F3 if (g == 0 and f < 2) else None))
                eng = "pool" if (f % 2 == 0 and g > 0) else "dve"
                op(eng, lambda e, rt=rt, f=f: e.tensor_tensor(out=hidT[:, f, :], in0=RT[rt], in1=RT[rt], op=ALU.mult),
                   r=[RTB[rt]], w=[HB[f]], extra=(F3 if g == 0 else None))
            for tt in range(2):
                t = 2 * g + tt
                for half in range(2):
                    bank = tt * 2 + half
                    for f in range(32):
                        op("pe", lambda e, bank=bank, f=f, tt=tt, half=half: e.matmul(
                            psum[bank][:], lhsT=hidT[:, f, tt * 128:(tt + 1) * 128], rhs=w_dn[:, f, half * 512:(half + 1) * 512],
                            start=(f == 0), stop=(f == 31)), r=[HB[f], WDBS[f // 4]], w=[PB[bank]])
                for half in range(2):
                    bank = tt * 2 + half
                    op("act", lambda e, bank=bank, half=half: e.activation(out=p4junk, in_=psum[bank][:], func=AF.Square,
                                                                           accum_out=ssq4[:, half:half + 1]),
                       w=[PB[bank], bf("p4junk"), bf("ssq4")], extra=(F3 if g == 0 else None))
                op("dve", lambda e: e.tensor_tensor(out=ssum[:, 0:1], in0=ssq4[:, 0:1], in1=ssq4[:, 1:2], op=ALU.add), r=[bf("ssq4")], w=[bf("ssum")])
                rstd_from(ssum[:, 0:1], rstd[:, 0:1], 1, bf("ssum"), bf("rstd"), 1.0 / D)
                for half in range(2):
                    bank = tt * 2 + half
                    cs = slice(half * 512, (half + 1) * 512)
                    op("dve", lambda e, bank=bank, cs=cs: e.scalar_tensor_tensor(out=p4u[:, cs], in0=psum[bank][:], scalar=rstd[:, 0:1],
                                                                                  in1=gpost_m[:, cs], op0=ALU.mult, op1=ALU.mult),
                       r=[bf("rstd"), BCB["gpost_m"]], w=[PB[bank], bf("p4u")], extra=(F3 if g == 0 else None))
                op("pool", lambda e, tt=tt: e.tensor_tensor(out=xg[:, tt, :], in0=xg[:, tt, :], in1=p4u, op=ALU.add), r=[bf("p4u")], w=[XGB])
                fin.append(op("sp", lambda e, t=t, tt=tt: e.dma_start(out=out_d[t * 128:(t + 1) * 128, :], in_=xg[:, tt, :]), r=[XGB], w=[OUTD[t]],
                              dma_sem=dsem["o%d" % tt]))
        S.wait_all("sp", list(S.fence()))
        stats = S.emit(nc, es)
        build_program.stats = stats
    return nc


_CACHE = {}


def _consts():
    ident = np.eye(128, dtype=np.float32)
    j = np.arange(128)[:, None]
    i = np.arange(128)[None, :]
    tri = (j <= i).astype(np.float32)
    negm = np.where(j > i, -30000.0, 0.0).astype(np.float32)
    inv = (np.float32(1.0) / (np.float32(10000.0) ** (np.arange(0, 64, 2, dtype=np.float32) / np.float32(64)))).astype(np.float32)
    invf = np.concatenate([inv, inv])[None, :].astype(np.float32)
    return ident, tri, negm, invf


def kernel(x, c, positions, ada_w, ada_b, pre_norm_mix, post_norm_mix, w_in, gla_gate_w, gla_gate_b, gla_norm,
           lambda_q1, lambda_k1, lambda_q2, lambda_k2, diff_norm, w_out, pre_norm_mlp, post_norm_mlp, w_up, w_down):
    f = lambda a: np.ascontiguousarray(np.asarray(a, dtype=np.float32))
    x = f(x); c = f(c)
    positions = np.asarray(positions).astype(np.int32)
    nb = x.shape[0]
    if "nc" not in _CACHE:
        _CACHE["nc"] = build_program()
    nc = _CACHE["nc"]
    ident, tri, negm, invf = _consts()
    shared = {
        "ada_w": f(ada_w[0]), "ada_b": f(ada_b[0])[None, :],
        "pre_norm_mix": f(pre_norm_mix[0])[None, :], "post_norm_mix": f(post_norm_mix[0])[None, :],
        "pre_norm_mlp": f(pre_norm_mlp[0])[None, :], "post_norm_mlp": f(post_norm_mlp[0])[None, :],
        "w_in": f(w_in[0]), "gla_gate_w": f(gla_gate_w[0]), "gla_gate_b": f(gla_gate_b[0])[None, :],
        "gla_norm": f(gla_norm[0])[None, :],
        "lambda_q1": f(lambda_q1[0])[None, :], "lambda_k1": f(lambda_k1[0])[None, :],
        "lambda_q2": f(lambda_q2[0])[None, :], "lambda_k2": f(lambda_k2[0])[None, :],
        "diff_norm": f(diff_norm[0])[None, :], "w_out": f(w_out[0]), "w_up": f(w_up[0]), "w_down": f(w_down[0]),
        "k_ident": ident, "k_tri": tri, "k_negmask": negm, "k_invf": invf,
    }
    in_maps = []
    for b in range(nb):
        m = dict(shared)
        m["x"] = x[b]
        m["c8"] = np.ascontiguousarray(c[b].reshape(8, 128).T)
        m["pos"] = np.ascontiguousarray(positions[b].reshape(NT, 128).T)
        in_maps.append(m)
    res = run_bass_kernel_spmd(nc, in_maps, core_ids=list(range(nb)))
    return np.stack([np.asarray(r["out"], dtype=np.float32) for r in res.results], axis=0)
```

```python
import math
import numpy as np
from contextlib import ExitStack
import concourse.bass as bass
import concourse.mybir as mybir
from concourse.bass_utils import run_bass_kernel_spmd

F32 = mybir.dt.float32
BF16 = mybir.dt.bfloat16
I32 = mybir.dt.int32
ALU = mybir.AluOpType
AF = mybir.ActivationFunctionType
AX = mybir.AxisListType

ENGS = ("pe", "act", "dve", "pool", "sp")

S_LEN = 2048
D = 1024
NT = 16
DFF = 4096
PW = 3088
EPS = 1e-6
LAMBDA_INIT = 0.8 - 0.6 * math.exp(0.0)


class Buf:
    __slots__ = ("name", "w", "r")

    def __init__(self, name):
        self.name = name
        self.w = None
        self.r = {}


class Sched:
    def __init__(self, same_engine_sync=True):
        self.ops = {e: [] for e in ENGS}
        self.dma_vals = []
        self.same_engine_sync = same_engine_sync

    def new_dma_sem(self):
        self.dma_vals.append(0)
        return len(self.dma_vals) - 1

    def fence(self):
        evs = set()
        for e in ENGS:
            for i in range(len(self.ops[e]) - 1, -1, -1):
                o = self.ops[e][i]
                if o["fn"] is not None and o["dma"] is None:
                    evs.add(("e", e, i))
                    break
        for i, v in enumerate(self.dma_vals):
            if v:
                evs.add(("d", i, v))
        return evs

    def op(self, eng, fn, reads=(), writes=(), dma_sem=None, extra=None):
        deps = set()
        for b in reads:
            if b.w is not None:
                deps.add(b.w)
        for b in writes:
            if b.w is not None:
                deps.add(b.w)
            for ev in b.r.values():
                if not (ev[0] == "e" and ev[1] == eng and dma_sem is None and (eng == "pe" or not self.same_engine_sync)):
                    deps.add(ev)
        idx = len(self.ops[eng])
        if dma_sem is None:
            ev = ("e", eng, idx)
        else:
            self.dma_vals[dma_sem] += 16
            ev = ("d", dma_sem, self.dma_vals[dma_sem])
        for b in reads:
            b.r[(ev[0], ev[1])] = ev
        for b in writes:
            b.w = ev
            b.r = {}
        if dma_sem is None:
            if eng == "pe" or not self.same_engine_sync:
                deps = {d for d in deps if not (d[0] == "e" and d[1] == eng)}
        if extra:
            for d in extra:
                if d[0] == "e" and d[1] == eng and (eng == "pe" or d[2] >= idx):
                    continue
                deps.add(d)
        self.ops[eng].append(dict(fn=fn, deps=deps, dma=dma_sem))
        return ev

    def wait_all(self, eng, events):
        self.ops[eng].append(dict(fn=None, deps=set(e for e in events if e is not None), dma=None))

    def emit(self, nc, es):
        esems = {e: es.enter_context(nc.semaphore("s_" + e)) for e in ENGS}
        dsems = [es.enter_context(nc.semaphore("d%d" % i)) for i in range(len(self.dma_vals))]
        sig = {e: set() for e in ENGS}
        for e in ENGS:
            for o in self.ops[e]:
                for d in o["deps"]:
                    if d[0] == "e":
                        sig[d[1]].add(d[2])
        cnt = {}
        for e in ENGS:
            c = 0
            arr = []
            for i in range(len(self.ops[e])):
                if i in sig[e]:
                    c += 1
                arr.append(c)
            cnt[e] = arr
        stats = {e: [len(self.ops[e]), len(sig[e]), 0] for e in ENGS}
        block = es.enter_context(nc.Block())

        def run(ename, h):
            waited = {}
            for i, o in enumerate(self.ops[ename]):
                for d in sorted(o["deps"]):
                    if d[0] == "e":
                        key, val, sem = ("e", d[1]), cnt[d[1]][d[2]], esems[d[1]]
                    else:
                        key, val, sem = ("d", d[1]), d[2], dsems[d[1]]
                    if waited.get(key, 0) >= val:
                        continue
                    waited[key] = val
                    h.wait_ge(sem, val)
                    stats[ename][2] += 1
                if o["fn"] is None:
                    continue
                ins = o["fn"](h)
                if o["dma"] is not None:
                    ins.then_inc(dsems[o["dma"]], 16)
                elif i in sig[ename]:
                    ins.then_inc(esems[ename], 1)

        @block.tensor
        def _(h):
            run("pe", h)

        @block.scalar
        def _(h):
            run("act", h)

        @block.vector
        def _(h):
            run("dve", h)

        @block.gpsimd
        def _(h):
            run("pool", h)

        @block.sync
        def _(h):
            run("sp", h)

        return stats


class _Stop(Exception):
    pass


def build_program(debug_phase=99, n_p1=NT, p1_stop=999):
    nc = bass.Bass("TRN2", target_bir_lowering=False)

    def din(name, shape, dt=F32):
        return nc.dram_tensor(name, list(shape), dt, kind="ExternalInput").ap()

    x_d = din("x", [S_LEN, D])
    c8_d = din("c8", [128, 8])
    pos_d = din("pos", [128, NT], I32)
    adaw_d = din("ada_w", [D, 6 * D])
    adab_d = din("ada_b", [1, 6 * D])
    gpre_mix_d = din("pre_norm_mix", [1, D])
    gpost_mix_d = din("post_norm_mix", [1, D])
    gpre_mlp_d = din("pre_norm_mlp", [1, D])
    gpost_mlp_d = din("post_norm_mlp", [1, D])
    win_d = din("w_in", [D, PW])
    gatew_d = din("gla_gate_w", [16, 256])
    gateb_d = din("gla_gate_b", [1, 256])
    glanorm_d = din("gla_norm", [1, 128])
    lq1_d = din("lambda_q1", [1, 64])
    lk1_d = din("lambda_k1", [1, 64])
    lq2_d = din("lambda_q2", [1, 64])
    lk2_d = din("lambda_k2", [1, 64])
    dnorm_d = din("diff_norm", [1, 128])
    wout_d = din("w_out", [D, D])
    wup_d = din("w_up", [D, DFF])
    wdown_d = din("w_down", [DFF, D])
    ident_d = din("k_ident", [128, 128])
    tri_d = din("k_tri", [128, 128])
    negm_d = din("k_negmask", [128, 128])
    invf_d = din("k_invf", [1, 64])
    out_d = nc.dram_tensor("out", [S_LEN, D], F32, kind="ExternalOutput").ap()

    S = Sched()
    es = ExitStack()
    with es:
        AW = 53200
        arena = es.enter_context(nc.sbuf_tensor("arena", [128, AW], F32))
        A = arena[:]
        psum = [es.enter_context(nc.psum_tensor("ps%d" % i, [128, 512], F32)) for i in range(8)]
        PB = [Buf("pb%d" % i) for i in range(8)]

        def f32v(off, cols, rows=None):
            assert off % 4 == 0 and off + cols * 4 <= AW * 4, (off, cols)
            v = A[:, off // 4: off // 4 + cols]
            return v if rows is None else v[rows[0]:rows[1]]

        def bf16v(off, cols):
            assert off % 4 == 0 and cols % 2 == 0 and off + cols * 2 <= AW * 4, (off, cols)
            return A[:, off // 4: off // 4 + cols // 2].bitcast(BF16)

        def pbf(i):
            return psum[i][:].bitcast(BF16)

        o = 0
        identb = bf16v(0, 128)
        identf = f32v(256, 128)
        tri = f32v(768, 128)
        onesf = f32v(1280, 128)
        negmb = bf16v(1792, 128)
        gnorm4 = f32v(2048, 512)
        dnormbc = f32v(4096, 128)
        sincos = f32v(4608, 1024).rearrange("p (t j) -> p t j", t=NT)
        gw = f32v(8704, 256)
        SM = 9728
        c8 = f32v(SM, 8)
        cact = f32v(SM + 32, 8)
        neglam = f32v(SM + 64, 1)
        posf = f32v(SM + 128, 16)
        posi = f32v(SM + 192, 16).bitcast(I32)
        lamt = f32v(SM + 256, 8)
        epsb = f32v(SM + 288, 1)
        gncol = f32v(SM + 296, 1)
        R_MIX = 10240
        mixT = bf16v(R_MIX, 8 * S_LEN).rearrange("p (k n) -> p k n", k=8)
        R_BC = 43008
        gmod_a = f32v(R_BC, D)
        sh_a = f32v(R_BC + 4096, D)
        gpost_a = f32v(R_BC + 8192, D)
        gmod_m = f32v(R_BC + 12288, D)
        sh_m = f32v(R_BC + 16384, D)
        gpost_m = f32v(R_BC + 20480, D)
        R_A = 67584
        w_in = bf16v(R_A, 8 * PW).rearrange("p (k n) -> p k n", k=8)
        PT = [bf16v(R_A + i * 16384, 16 * 512).rearrange("p (j n) -> p j n", j=16) for i in range(2)]
        R_B = 116992
        qT = bf16v(R_B, 4 * S_LEN).rearrange("p (h n) -> p h n", h=4)
        kT = bf16v(R_B + 16384, 4 * S_LEN).rearrange("p (h n) -> p h n", h=4)
        vaug = bf16v(R_B + 32768, NT * 4 * 130).rearrange("p (t h e) -> p t h e", t=NT, h=4)
        R_C = R_B + 32768 + NT * 4 * 130 * 2
        assert R_C == 166400
        ada_ring = [f32v(R_B + i * 16384, 8 * 512).rearrange("p (k n) -> p k n", k=8) for i in range(2)]
        cb = f32v(R_B + 32768, 8 * 128).rearrange("p (k n) -> p k n", k=8)
        adab = f32v(R_MIX, 6 * D)
        angs = f32v(R_MIX + 24576, 1024).rearrange("p (t j) -> p t j", t=NT)
        angk = f32v(R_MIX + 28672, 1024).rearrange("p (t j) -> p t j", t=NT)
        angi = f32v(R_C, 1024).bitcast(I32).rearrange("p (t j) -> p t j", t=NT)
        invfbc = f32v(R_C + 4096, 64)
        lamv = f32v(R_C + 4608, 256)
        o = R_BC + 8192
        xt = [f32v(o, D), f32v(o + 4096, D)]; o += 8192
        h1 = f32v(o, D); o += 4096
        hb2 = [bf16v(o, D), None]; o += 2048
        junk = bf16v(o, D); o += 2048
        assert o == R_BC + 24576
        o = R_C
        hTt2 = [bf16v(o + i * 2048, D).rearrange("p (k n) -> p k n", k=8) for i in range(2)]; o += 4096
        glrT = f32v(o, 128); o += 512
        ez = f32v(o, 256); o += 1024
        spz = f32v(o, 256); o += 1024
        eb2 = [f32v(o + i * 1024, 256) for i in range(2)]; o += 2048
        enb2 = [f32v(o + i * 1024, 256) for i in range(2)]; o += 2048
        gqk2 = [f32v(o + i * 2048, 512) for i in range(2)]; o += 4096
        qg = bf16v(o, 256); o += 512
        kg = bf16v(o, 256); o += 512
        qkT = bf16v(o, 512).rearrange("p (a n) -> p a n", a=4); o += 1024
        sT = bf16v(o, 512).rearrange("p (a n) -> p a n", a=4); o += 1024
        vg2 = [bf16v(o + i * 1024, 512) for i in range(2)]; o += 2048
        gate2 = [f32v(o + i * 2048, 512) for i in range(2)]; o += 4096
        mixg = bf16v(o, 512); o += 1024
        Sf = f32v(o, 256).rearrange("p (a n) -> p a n", a=2); o += 1024
        Sb = bf16v(o, 256).rearrange("p (a n) -> p a n", a=2); o += 512
        Stmp = f32v(o, 256).rearrange("p (a n) -> p a n", a=2); o += 1024
        ropeA = f32v(o, 512); o += 2048
        ropeB = f32v(o, 512); o += 2048
        ropeAk = f32v(o, 512); o += 2048
        ropeBk = f32v(o, 512); o += 2048
        qr2 = [bf16v(o + i * 1024, 512) for i in range(2)]; o += 2048
        kr2 = [bf16v(o + i * 1024, 512) for i in range(2)]; o += 2048
        hb2[1] = bf16v(o, D); o += 2048
        assert o <= AW * 4, o
        dec = f32v(SM + 320, 2)
        ssq4 = f32v(SM + 336, 4)
        rstd4 = f32v(SM + 352, 4)
        ssum = f32v(SM + 368, 2)
        rstd = f32v(SM + 376, 2)
        rden = f32v(SM + 384, 4)
        rden2 = f32v(SM + 400, 4)
        ssum2 = f32v(SM + 416, 4)
        rstd2 = f32v(SM + 432, 4)
        ssumA = f32v(SM + 448, 2)
        rstdA = f32v(SM + 456, 2)
        dec2 = [f32v(SM + 464 + i * 8, 2) for i in range(2)]
        ada_ring2 = [bf16v(100352 + i * 4096, 8 * 256).rearrange("p (k n) -> p k n", k=8) for i in range(2)]
        adab2 = [bf16v(R_C + i * 512, 256) for i in range(2)]
        cb2 = bf16v(R_C + 2048, 8 * 128).rearrange("p (k n) -> p k n", k=8)
        onesb = bf16v(R_C + 1024, 128)
        W_OUT = 182272
        w_out = bf16v(W_OUT, 8 * D).rearrange("p (k n) -> p k n", k=8)
        o = 198656
        o1 = f32v(o, 512).rearrange("p (r n) -> p r n", r=4); o += 2048
        od = f32v(o, 128); o += 512
        mixd = bf16v(o, 128); o += 256
        p2junk = bf16v(o, 128); o += 256
        p2sq = f32v(o, 128); o += 512
        od4 = [od] + [f32v(o + i * 512, 128) for i in range(3)]; o += 1536
        mixd4 = [mixd] + [bf16v(o + i * 256, 128) for i in range(3)]; o += 768
        p3x = [f32v(o, D), f32v(o + 4096, D)]; o += 8192
        assert o <= AW * 4, o
        p3u = f32v(166400, D)
        p3h1 = f32v(166400 + 4096, D)
        p3hb = bf16v(166400 + 8192, D)
        p3junk = bf16v(166400 + 10240, D)
        p3hb2 = [p3hb, bf16v(166400 + 12288, D)]
        W_UP = 67584
        w_up = bf16v(W_UP, 8 * DFF).rearrange("p (k n) -> p k n", k=8)
        W_DN = W_UP + 65536
        w_dn = bf16v(W_DN, 32 * D).rearrange("p (f n) -> p f n", f=32)
        assert W_DN + 65536 == 198656
        hidT = bf16v(R_BC, 32 * 256).rearrange("p (f n) -> p f n", f=32)
        xg = f32v(198656, 2 * D).rearrange("p (a n) -> p a n", a=2)
        p4u = f32v(198656 + 8192, D)
        assert 198656 + 8192 + 4096 <= AW * 4
        RT = [f32v(R_BC + 16384 + i * 1024, 256) for i in range(2)]
        p4junk = bf16v(R_BC + 16384 + 2048, 512)

        def op(eng, fn, r=(), w=(), **kw):
            return S.op(eng, fn, reads=r, writes=w, **kw)

        dsem = {k: S.new_dma_sem() for k in ["const", "consta", "constp"] + ["win%d" % i for i in range(7)] + [ "ada0", "ada1", "x0", "x1", "wout", "ffnu", "o0", "o1",
                                             "xm0", "xm1", "xg", "g2", "ada2_0", "ada2_1", "adb2_0", "adb2_1", "xm2", "os0", "os1", "os2"] + ["ffnd%d" % g for g in range(8)]}
        B = {}

        def bf(name):
            if name not in B:
                B[name] = Buf(name)
            return B[name]

        def rstd_from(sum_ap, out_ap, n, sbuf, wbuf, scale):
            op("dve", lambda e: e.tensor_scalar(out=out_ap, in0=sum_ap, scalar1=scale, scalar2=EPS, op0=ALU.mult, op1=ALU.add),
               r=[sbuf], w=[wbuf])
            op("act", lambda e: e.activation(out=out_ap, in_=out_ap, func=AF.Ln), r=[wbuf], w=[wbuf])
            op("act", lambda e: e.activation(out=out_ap, in_=out_ap, func=AF.Exp, scale=-0.5), r=[wbuf], w=[wbuf])

        CB = bf("consts")
        CBa = bf("consts_a")
        for dst, src in [(identf, ident_d), (tri, tri_d)]:
            op("sp", lambda e, dst=dst, src=src: e.dma_start(out=dst, in_=src), w=[CB], dma_sem=dsem["const"])
        op("pool", lambda e: e.dma_start(out=negmb, in_=negm_d), w=[bf("negmb")], dma_sem=dsem["constp"])
        op("sp", lambda e: e.dma_start(out=c8, in_=c8_d), w=[CB], dma_sem=dsem["const"])
        op("act", lambda e: e.dma_start(out=posi, in_=pos_d), w=[CBa], dma_sem=dsem["consta"])
        op("sp", lambda e: e.dma_start(out=adab[0:1, :], in_=adab_d), w=[CB], dma_sem=dsem["const"])
        op("sp", lambda e: e.dma_start(out=gmod_a, in_=gpre_mix_d.partition_broadcast(128)), w=[CB], dma_sem=dsem["const"])
        op("act", lambda e: e.dma_start(out=dnormbc, in_=dnorm_d.partition_broadcast(128)), w=[CBa], dma_sem=dsem["consta"])
        op("act", lambda e: e.dma_start(out=gncol, in_=glanorm_d.rearrange("o d -> d o")), w=[CBa], dma_sem=dsem["consta"])
        op("act", lambda e: e.dma_start(out=invfbc, in_=invf_d.partition_broadcast(128)), w=[CBa], dma_sem=dsem["consta"])
        op("sp", lambda e: e.dma_start(out=gw[0:16, :], in_=gatew_d), w=[CB], dma_sem=dsem["const"])
        op("sp", lambda e: e.dma_start(out=gw[16:17, :], in_=gateb_d), w=[CB], dma_sem=dsem["const"])
        for i, src in enumerate([lq1_d, lk1_d, lq2_d, lk2_d]):
            op("act", lambda e, i=i, src=src: e.dma_start(out=lamv[0:1, i * 64:(i + 1) * 64], in_=src), w=[CBa], dma_sem=dsem["consta"])
        WIN_BLOCKS = [(1024, 16), (0, 512), (1552, 512), (2064, 512), (512, 512), (1040, 512), (2576, 512)]
        WINB = {}
        for i, (c0, ncols) in enumerate(WIN_BLOCKS):
            WINB[c0] = bf("w_in_%d" % c0)
            op("pool", lambda e, c0=c0, ncols=ncols: e.dma_start(out=w_in[:, :, c0:c0 + ncols],
                                                                  in_=win_d[:, c0:c0 + ncols].rearrange("(k p) n -> p k n", p=128)),
               r=[CB, CBa], w=[WINB[c0]], dma_sem=dsem["win%d" % i])

        op("dve", lambda e: e.memset(onesf, 1.0), w=[bf("ones")])
        op("dve", lambda e: e.memset(epsb, EPS), w=[bf("epsb")])
        op("dve", lambda e: e.tensor_copy(out=identb, in_=identf), r=[CB, CBa], w=[bf("identb")])
        CA = bf("cact")
        op("act", lambda e: e.activation(out=cact, in_=c8, func=AF.Exp, scale=-1.0), r=[CB, CBa], w=[CA])
        op("dve", lambda e: e.tensor_scalar(out=cact, in0=cact, scalar1=1.0, scalar2=None, op0=ALU.add), r=[CA], w=[CA])
        op("dve", lambda e: e.reciprocal(out=cact, in_=cact), r=[CA], w=[CA])
        op("dve", lambda e: e.tensor_tensor(out=cact, in0=cact, in1=c8, op=ALU.mult), r=[CA, CB, CBa], w=[CA])
        CBB = bf("cb")
        for k in range(8):
            op("dve", lambda e, k=k: e.tensor_scalar(out=cb[:, k, :], in0=onesf, scalar1=cact[:, k:k + 1], scalar2=None, op0=ALU.mult),
               r=[CA, bf("ones")], w=[CBB])
        LB = bf("lam")
        op("dve", lambda e: e.tensor_tensor(out=lamv[0:1, 0:64], in0=lamv[0:1, 0:64], in1=lamv[0:1, 64:128], op=ALU.mult), r=[CB, CBa], w=[LB])
        op("dve", lambda e: e.tensor_tensor(out=lamv[0:1, 128:192], in0=lamv[0:1, 128:192], in1=lamv[0:1, 192:256], op=ALU.mult), r=[LB], w=[LB])
        op("dve", lambda e: e.reduce_sum(out=lamt[0:1, 0:1], in_=lamv[0:1, 0:64], axis=AX.X), r=[LB], w=[LB])
        op("dve", lambda e: e.reduce_sum(out=lamt[0:1, 1:2], in_=lamv[0:1, 128:192], axis=AX.X), r=[LB], w=[LB])
        op("act", lambda e: e.activation(out=lamt[0:1, 0:2], in_=lamt[0:1, 0:2], func=AF.Exp), r=[LB], w=[LB])
        op("dve", lambda e: e.scalar_tensor_tensor(out=lamt[0:1, 2:3], in0=lamt[0:1, 1:2], scalar=-LAMBDA_INIT, in1=lamt[0:1, 0:1],
                                                   op0=ALU.add, op1=ALU.subtract), r=[LB], w=[LB])
        op("pe", lambda e: e.matmul(psum[7][:, 0:1], lhsT=onesf[0:1, :], rhs=lamt[0:1, 2:3], start=True, stop=True),
           r=[LB, bf("ones")], w=[PB[7]])
        op("dve", lambda e: e.tensor_copy(out=neglam, in_=psum[7][:, 0:1]), w=[PB[7], bf("neglam")])
        op("dve", lambda e: e.tensor_scalar(out=dnormbc, in0=dnormbc, scalar1=1.0 - LAMBDA_INIT, scalar2=None, op0=ALU.mult), r=[CB, CBa], w=[bf("dnorm")])

        RB = bf("rope")
        op("dve", lambda e: e.tensor_copy(out=posf, in_=posi), r=[CB, CBa], w=[RB])
        for t in range(NT):
            op("dve", lambda e, t=t: e.tensor_scalar(out=angs[:, t, :], in0=invfbc, scalar1=posf[:, t:t + 1], scalar2=None, op0=ALU.mult),
               r=[RB, CB, CBa], w=[RB])
        op("dve", lambda e: e.tensor_scalar(out=angs[:, :, 32:64], in0=angs[:, :, 32:64], scalar1=math.pi / 2, scalar2=None, op0=ALU.add), r=[RB], w=[RB])
        op("dve", lambda e: e.tensor_scalar(out=angk, in0=angs, scalar1=1.0 / (2 * math.pi), scalar2=None, op0=ALU.mult), r=[RB], w=[RB])
        op("dve", lambda e: e.tensor_copy(out=angi, in_=angk), r=[RB], w=[RB])
        op("dve", lambda e: e.tensor_copy(out=angk, in_=angi), r=[RB], w=[RB])
        C1 = 6.28125
        C2 = 2 * math.pi - C1
        op("dve", lambda e: e.scalar_tensor_tensor(out=angs, in0=angk, scalar=-C1, in1=angs, op0=ALU.mult, op1=ALU.add), r=[RB], w=[RB])
        op("dve", lambda e: e.scalar_tensor_tensor(out=angs, in0=angk, scalar=-C2, in1=angs, op0=ALU.mult, op1=ALU.add), r=[RB], w=[RB])
        op("dve", lambda e: e.tensor_scalar(out=angk, in0=angs, scalar1=math.pi, scalar2=-2 * math.pi, op0=ALU.is_gt, op1=ALU.mult), r=[RB], w=[RB])
        op("dve", lambda e: e.tensor_tensor(out=angs, in0=angs, in1=angk, op=ALU.add), r=[RB], w=[RB])
        op("dve", lambda e: e.tensor_scalar(out=angk, in0=angs, scalar1=-math.pi, scalar2=2 * math.pi, op0=ALU.is_lt, op1=ALU.mult), r=[RB], w=[RB])
        op("dve", lambda e: e.tensor_tensor(out=angs, in0=angs, in1=angk, op=ALU.add), r=[RB], w=[RB])
        op("dve", lambda e: e.tensor_scalar(out=angs, in0=angs, scalar1=3.1415925, scalar2=-3.1415925, op0=ALU.min, op1=ALU.max), r=[RB], w=[RB])
        op("act", lambda e: e.activation(out=sincos, in_=angs, func=AF.Sin), r=[RB], w=[bf("sincos")])

        F0a = S.fence()
        ADS = [bf("adaslot0"), bf("adaslot1")]
        BCB = {n: bf("bc_" + n) for n in ["gmod_a", "sh_a", "gpost_a", "gmod_m", "sh_m", "gpost_m"]}
        targets = [("sh_a", sh_a, "copy"), ("gmod_a", gmod_a, "mod"), ("gpost_a", gpost_a, "mul"),
                   ("sh_m", sh_m, "copy"), ("gmod_m", gmod_m, "mod"), ("gpost_m", gpost_m, "mul")]
        for n in range(4):
            slot = n % 2
            op("sp", lambda e, n=n, slot=slot: e.dma_start(out=ada_ring[slot],
                                                            in_=adaw_d[:, n * 512:(n + 1) * 512].rearrange("(k p) n -> p k n", p=128)),
               w=[ADS[slot]], dma_sem=dsem["ada%d" % slot])
            bank = 5 + slot
            for k in range(8):
                op("pe", lambda e, k=k, slot=slot, bank=bank: e.matmul(psum[bank][:], lhsT=cb[:, k, :], rhs=ada_ring[slot][:, k, :],
                                                                        start=(k == 0), stop=False),
                   r=[CBB, ADS[slot]], w=[PB[bank]])
            op("pe", lambda e, n=n, bank=bank: e.matmul(psum[bank][:], lhsT=onesf[0:1, :], rhs=adab[0:1, n * 512:(n + 1) * 512],
                                                        start=False, stop=True), r=[CB, CBa, bf("ones")], w=[PB[bank]])
            name, tile, kind = targets[n // 2]
            dst = tile[:, (n % 2) * 512:(n % 2 + 1) * 512]
            if kind == "copy":
                op("dve", lambda e, dst=dst, bank=bank: e.tensor_copy(out=dst, in_=psum[bank][:]), w=[PB[bank], BCB[name]])
            elif kind == "mod":
                op("dve", lambda e, dst=dst, bank=bank: e.scalar_tensor_tensor(out=dst, in0=psum[bank][:], scalar=1.0, in1=dst,
                                                                                op0=ALU.add, op1=ALU.mult), r=[CB, CBa], w=[PB[bank], BCB[name]])
            else:
                op("dve", lambda e, dst=dst, bank=bank: e.tensor_tensor(out=dst, in0=psum[bank][:], in1=dst, op=ALU.mult),
                   r=[CB, CBa], w=[PB[bank], BCB[name]])
        F0 = S.fence()

        XB = [bf("xt0"), bf("xt1")]
        QTB = bf("qT"); KTB = bf("kT"); VAB = bf("vaug")
        MIXB = [bf("mix%d" % t) for t in range(NT)]
        STB = bf("state")
        op("pool", lambda e: e.memset(Sf, 0.0), w=[STB], extra=F0a)
        op("pool", lambda e: e.memset(Sb, 0.0), w=[STB])
        op("pool", lambda e: e.memset(glrT[0:32, :], 1.0), w=[bf("glrT")], extra=F0a)

        NP1 = n_p1 if debug_phase >= 1 else 0

        def p1_tile(t):
            par = t % 2
            xst = {"x": F0a if t < 2 else None}

            def o_(eng, fn, r=(), w=()):
                return S.op(eng, fn, reads=r, writes=w, extra=xst["x"])

            P = lambda n: bf("%s_%d" % (n, par))
            hTt = hTt2[par]; eb = eb2[par]; enb = enb2[par]; gqk = gqk2[par]; vg = vg2[par]; gate = gate2[par]; dec = dec2[par]
            hb = hb2[par]; qr = qr2[par]; kr = kr2[par]
            S.op("sp", lambda e: e.dma_start(out=xt[par], in_=x_d[t * 128:(t + 1) * 128, :]), writes=[XB[par]],
                 dma_sem=dsem["x%d" % par])
            o_("act", lambda e: e.activation(out=junk, in_=xt[par], func=AF.Square, accum_out=ssumA[:, par:par + 1]),
               r=[XB[par]], w=[bf("junk"), P("ssumA")])
            o_("act", lambda e: e.activation(out=rstdA[:, par:par + 1], in_=ssumA[:, par:par + 1], func=AF.Ln, scale=1.0 / D, bias=epsb),
               r=[P("ssumA"), bf("epsb")], w=[P("rstdA")])
            o_("act", lambda e: e.activation(out=rstdA[:, par:par + 1], in_=rstdA[:, par:par + 1], func=AF.Exp, scale=-0.5),
               r=[P("rstdA")], w=[P("rstdA")])
            o_("dve", lambda e: e.scalar_tensor_tensor(out=h1, in0=xt[par], scalar=rstdA[:, par:par + 1], in1=gmod_a, op0=ALU.mult, op1=ALU.mult),
               r=[XB[par], P("rstdA"), BCB["gmod_a"]], w=[bf("h1")])
            o_("pool", lambda e: e.tensor_tensor(out=hb, in0=h1, in1=sh_a, op=ALU.add), r=[bf("h1"), BCB["sh_a"]], w=[P("hb")])
            yield 1
            for k in range(8):
                o_("pe", lambda e, k=k: e.transpose(out=pbf(0)[:, k * 128:(k + 1) * 128], in_=hb[:, k * 128:(k + 1) * 128], identity=identb),
                   r=[P("hb"), bf("identb")], w=[PB[0]])
            o_("act", lambda e: e.copy(out=hTt.rearrange("p k n -> p (k n)"), in_=pbf(0)), w=[PB[0], P("hTt")])
            yield 1

            xst["x"] = F0 if t < 2 else None
            if t == 0:
                o_("pool", lambda e: e.memset(vaug[:, :, :, 128:130], 1.0), w=[VAB])

            def inproj(bank, c0, ncols):
                for k in range(8):
                    o_("pe", lambda e, k=k: e.matmul(psum[bank][:, 0:ncols], lhsT=hTt[:, k, :], rhs=w_in[:, k, c0:c0 + ncols],
                                                     start=(k == 0), stop=(k == 7)), r=[P("hTt"), WINB[c0]], w=[PB[bank]])

            def rope(bank, dst, dname, RA, RBf, aname, bname):
                cos2 = sincos[:, t, 32:64].unsqueeze(1).unsqueeze(1).to_broadcast([128, 8, 2, 32])
                sinb = sincos[:, t, 0:32].unsqueeze(1).to_broadcast([128, 8, 32])
                SC = bf("sincos")
                src4 = psum[bank][:].rearrange("p (g two j) -> p g two j", g=8, two=2)
                A4 = RA.rearrange("p (g two j) -> p g two j", g=8, two=2)
                B4 = RBf.rearrange("p (g two j) -> p g two j", g=8, two=2)
                D4 = dst.rearrange("p (g two j) -> p g two j", g=8, two=2)
                o_("dve", lambda e: e.tensor_tensor(out=A4, in0=src4, in1=cos2, op=ALU.mult), r=[SC], w=[PB[bank], bf(aname)])
                o_("dve", lambda e: e.tensor_tensor(out=B4[:, :, 0, :], in0=src4[:, :, 1, :], in1=sinb, op=ALU.mult),
                   r=[SC], w=[PB[bank], bf(bname)])
                o_("dve", lambda e: e.tensor_tensor(out=B4[:, :, 1, :], in0=src4[:, :, 0, :], in1=sinb, op=ALU.mult),
                   r=[SC], w=[PB[bank], bf(bname)])
                o_("pool", lambda e: e.tensor_tensor(out=D4[:, :, 0, :], in0=A4[:, :, 0, :], in1=B4[:, :, 0, :], op=ALU.subtract),
                   r=[bf(aname), bf(bname)], w=[bf(dname)])
                o_("pool", lambda e: e.tensor_tensor(out=D4[:, :, 1, :], in0=A4[:, :, 1, :], in1=B4[:, :, 1, :], op=ALU.add),
                   r=[bf(aname), bf(bname)], w=[bf(dname)])

            for k in range(8):
                o_("pe", lambda e, k=k: e.matmul(psum[5][0:16, 0:128], lhsT=w_in[:, k, 1024:1040], rhs=hTt[:, k, :],
                                                 start=(k == 0), stop=(k == 7)), r=[P("hTt"), WINB[1024]], w=[PB[5]])
            o_("act", lambda e: e.copy(out=glrT[0:16, :], in_=psum[5][0:16, 0:128]), w=[PB[5], bf("glrT")])
            inproj(1, 0, 512)
            yield 0
            o_("pe", lambda e: e.matmul(psum[4][:, 0:256], lhsT=glrT[0:17, :], rhs=gw[0:17, :], start=True, stop=True),
               r=[bf("glrT"), CB, CBa], w=[PB[4]])
            o_("act", lambda e: e.activation(out=ez, in_=psum[4][:, 0:256], func=AF.Exp, scale=-1.0), w=[PB[4], bf("ez")])
            o_("act", lambda e: e.activation(out=spz, in_=ez, func=AF.Ln, bias=1.0), r=[bf("ez")], w=[bf("spz")])
            inproj(2, 1552, 512)
            yield 0
            o_("act", lambda e: e.copy(out=gqk, in_=psum[1][:]), w=[PB[1], P("gqk")])
            rope(2, qr, "qr_%d" % par, ropeA, ropeB, "ropeAq", "ropeBq")
            inproj(5, 2064, 512)
            yield 0
            rope(5, kr, "kr_%d" % par, ropeAk, ropeBk, "ropeAk", "ropeBk")
            o_("pe", lambda e: e.matmul(psum[4][:, 256:512], lhsT=tri, rhs=spz, start=True, stop=True), r=[CB, CBa, bf("spz")], w=[PB[4]])
            o_("act", lambda e: e.activation(out=eb, in_=psum[4][:, 256:512], func=AF.Exp, scale=-1.0 / 16), w=[PB[4], P("eb")])
            o_("act", lambda e: e.activation(out=enb, in_=psum[4][:, 256:512], func=AF.Exp, scale=1.0 / 16), w=[PB[4], P("enb")])
            for pr in range(2):
                o_("pe", lambda e, pr=pr: e.matmul(psum[4][:, pr:pr + 1], lhsT=spz[:, pr * 128:(pr + 1) * 128], rhs=onesf[:, 0:1],
                                                   start=True, stop=True), r=[bf("spz"), bf("ones")], w=[PB[4]])
            o_("act", lambda e: e.activation(out=dec, in_=psum[4][:, 0:2], func=AF.Exp, scale=-1.0 / 16), w=[PB[4], P("dec")])
            inproj(1, 512, 512)
            yield 0
            o_("act", lambda e: e.copy(out=vg, in_=psum[1][:]), w=[PB[1], P("vg")])
            inproj(2, 1040, 512)
            yield 0
            o_("act", lambda e: e.activation(out=gate, in_=psum[2][:], func=AF.Exp, scale=-1.0), w=[PB[2], P("gate")])
            o_("act", lambda e: e.activation(out=gate, in_=gate, func=AF.Ln, bias=1.0), r=[P("gate")], w=[P("gate")])
            o_("act", lambda e: e.activation(out=gate, in_=gate, func=AF.Exp, scale=-1.0), r=[P("gate")], w=[P("gate")])
            o_("dve", lambda e: e.tensor_tensor(out=gate, in0=psum[2][:], in1=gate, op=ALU.mult), r=[P("gate")], w=[PB[2], P("gate")])
            inproj(5, 2576, 512)
            yield 0
            o_("act", lambda e: e.copy(out=vaug[:, t, :, 0:128], in_=psum[5][:].rearrange("p (h n) -> p h n", h=4)), w=[PB[5], VAB])
            yield 1

            o_("dve", lambda e: e.scalar_tensor_tensor(out=qg, in0=gqk[:, 0:256], scalar=0.125, in1=eb, op0=ALU.mult, op1=ALU.mult),
               r=[P("gqk"), P("eb")], w=[bf("qg")])
            o_("dve", lambda e: e.tensor_tensor(out=kg, in0=gqk[:, 256:512], in1=enb, op=ALU.mult), r=[P("gqk"), P("enb")], w=[bf("kg")])
            for a in range(4):
                src = qg if a < 2 else kg
                pr = a % 2
                o_("pe", lambda e, a=a, src=src, pr=pr: e.transpose(out=pbf(6)[:, a * 128:(a + 1) * 128], in_=src[:, pr * 128:(pr + 1) * 128],
                                                                      identity=identb), r=[bf("qg"), bf("kg"), bf("identb")], w=[PB[6]])
            o_("act", lambda e: e.copy(out=qkT.rearrange("p a n -> p (a n)"), in_=pbf(6)[:, 0:512]), w=[PB[6], bf("qkT")])
            yield 0
            for hh in range(4):
                pr, hf = hh // 2, hh % 2
                sbk = 6 if hf == 0 else 7
                o_("pe", lambda e, pr=pr, hf=hf, sbk=sbk: e.matmul(psum[sbk][:, pr * 128:(pr + 1) * 128],
                                                                   lhsT=qkT[hf * 64:(hf + 1) * 64, 2 + pr, :], rhs=qkT[hf * 64:(hf + 1) * 64, pr, :],
                                                                   start=True, stop=True), r=[bf("qkT")], w=[PB[sbk]])
            for hh in range(4):
                pr, hf = hh // 2, hh % 2
                sbk = 6 if hf == 0 else 7
                o_("dve", lambda e, hh=hh, pr=pr, sbk=sbk: e.tensor_tensor(out=sT[:, hh, :], in0=psum[sbk][:, pr * 128:(pr + 1) * 128], in1=tri, op=ALU.mult),
                   r=[CB, CBa], w=[PB[sbk], bf("sT")])
            yield 0
            for hh in range(4):
                pr, hf = hh // 2, hh % 2
                o_("pe", lambda e, hh=hh: e.matmul(psum[7][:, hh * 128:(hh + 1) * 128], lhsT=sT[:, hh, :], rhs=vg[:, hh * 128:(hh + 1) * 128],
                                                   start=True, stop=False), r=[bf("sT"), P("vg")], w=[PB[7]])
                o_("pe", lambda e, hh=hh, pr=pr, hf=hf: e.matmul(psum[7][:, hh * 128:(hh + 1) * 128], lhsT=qkT[hf * 64:(hf + 1) * 64, pr, :],
                                                                   rhs=Sb[hf * 64:(hf + 1) * 64, pr, :], start=False, stop=True),
                   r=[bf("qkT"), STB], w=[PB[7]])
            for pr in range(2):
                o_("pe", lambda e, pr=pr: e.matmul(psum[6][:, pr * 256:(pr + 1) * 256], lhsT=kg[:, pr * 128:(pr + 1) * 128],
                                                   rhs=vg[:, pr * 256:(pr + 1) * 256], start=True, stop=True), r=[bf("kg"), P("vg")], w=[PB[6]])
            for pr in range(2):
                for hf in range(2):
                    rows = slice(hf * 64, (hf + 1) * 64)
                    o_("dve", lambda e, pr=pr, hf=hf, rows=rows: e.tensor_scalar(
                        out=Stmp[rows, pr, :], in0=psum[6][rows, pr * 256 + hf * 128: pr * 256 + (hf + 1) * 128],
                        scalar1=dec[rows, pr:pr + 1], scalar2=None, op0=ALU.mult), r=[P("dec")], w=[PB[6], bf("Stmp")])
            for pr in range(2):
                o_("dve", lambda e, pr=pr: e.scalar_tensor_tensor(out=Sf[:, pr, :], in0=Sf[:, pr, :], scalar=dec[:, pr:pr + 1], in1=Stmp[:, pr, :],
                                                                   op0=ALU.mult, op1=ALU.add), r=[P("dec"), bf("Stmp")], w=[STB])
                o_("pool", lambda e, pr=pr: e.tensor_copy(out=Sb[:, pr, :], in_=Sf[:, pr, :]), w=[STB])
            yield 0
            for hh in range(4):
                o_("act", lambda e, hh=hh: e.activation(out=junk[:, 0:128], in_=psum[7][:, hh * 128:(hh + 1) * 128], func=AF.Square,
                                                        accum_out=ssq4[:, hh:hh + 1]), w=[PB[7], bf("junk"), bf("ssq4")])
            o_("act", lambda e: e.activation(out=rstd4, in_=ssq4, func=AF.Ln, scale=1.0 / 128, bias=epsb), r=[bf("ssq4"), bf("epsb")], w=[bf("rstd4")])
            o_("act", lambda e: e.activation(out=rstd4, in_=rstd4, func=AF.Exp, scale=-0.5), r=[bf("rstd4")], w=[bf("rstd4")])
            for hh in range(4):
                o_("dve", lambda e, hh=hh: e.scalar_tensor_tensor(out=mixg[:, hh * 128:(hh + 1) * 128], in0=psum[7][:, hh * 128:(hh + 1) * 128],
                                                                   scalar=rstd4[:, hh:hh + 1], in1=gate[:, hh * 128:(hh + 1) * 128],
                                                                   op0=ALU.mult, op1=ALU.mult), r=[bf("rstd4"), P("gate")], w=[PB[7], bf("mixg")])
            yield 0
            for hh in range(4):
                o_("pe", lambda e, hh=hh: e.transpose(out=pbf(6)[:, hh * 128:(hh + 1) * 128], in_=mixg[:, hh * 128:(hh + 1) * 128], identity=identb),
                   r=[bf("mixg"), bf("identb")], w=[PB[6]])
            o_("act", lambda e: e.activation(out=mixT[:, 0:4, t * 128:(t + 1) * 128], in_=pbf(6)[:, 0:512].rearrange("p (a n) -> p a n", a=4),
                                             func=AF.Copy, scale=gncol[:, 0:1]), r=[CB, CBa], w=[PB[6], MIXB[t]])
            yield 0
            for a in range(8):
                src = qr if a < 4 else kr
                hh = a % 4
                o_("pe", lambda e, a=a, src=src, hh=hh: e.transpose(out=pbf(3)[:, a * 128:(a + 1) * 128], in_=src[:, hh * 128:(hh + 1) * 128],
                                                                      identity=identb), r=[P("qr"), P("kr"), bf("identb")], w=[PB[3]])
            o_("dve", lambda e: e.tensor_copy(out=qT[:, :, t * 128:(t + 1) * 128], in_=pbf(3)[:, 0:512].rearrange("p (a n) -> p a n", a=4)),
               w=[PB[3], QTB])
            o_("dve", lambda e: e.tensor_copy(out=kT[:, :, t * 128:(t + 1) * 128], in_=pbf(3)[:, 512:1024].rearrange("p (a n) -> p a n", a=4)),
               w=[PB[3], KTB])
            yield 1

        def run_pipeline(gens, nstages, newest_first=False):
            n = len(gens)
            done = [False] * n
            for step in range(n + nstages - 1):
                act = [t for t in range(n) if t <= step < t + nstages and not done[t]]
                if newest_first:
                    act = act[::-1]
                fin = {t: False for t in act}
                while not all(fin.values()):
                    for t in act:
                        if fin[t]:
                            continue
                        try:
                            v = next(gens[t])
                        except StopIteration:
                            done[t] = True
                            v = 1
                        if v == 1:
                            fin[t] = True

        run_pipeline([p1_tile(t) for t in range(NP1)], 4, newest_first=True)
        F1 = S.fence()

        WOB = bf("w_out")

        def load_w_out():
            for k in range(8):
                op("pool", lambda e, k=k: e.dma_start(out=w_out[:, k, :], in_=wout_d[k * 128:(k + 1) * 128, :]), w=[WOB], dma_sem=dsem["wout"],
                   extra=(F1 if k == 0 else None))
        PTB = [[bf("pt%d_%d" % (m, j)) for j in range(16)] for m in range(2)]
        GB = bf("gains2")
        op("sp", lambda e: e.dma_start(out=gpost_a, in_=gpost_mix_d.partition_broadcast(128)), w=[GB], dma_sem=dsem["g2"], extra=F1)
        op("sp", lambda e: e.dma_start(out=gmod_m, in_=gpre_mlp_d.partition_broadcast(128)), w=[GB], dma_sem=dsem["g2"])
        op("sp", lambda e: e.dma_start(out=gpost_m, in_=gpost_mlp_d.partition_broadcast(128)), w=[GB], dma_sem=dsem["g2"])
        CB2 = bf("cb2")
        op("dve", lambda e: e.tensor_copy(out=onesb, in_=onesf), r=[bf("ones")], w=[bf("onesb")], extra=F1)
        for k in range(8):
            op("dve", lambda e, k=k: e.tensor_scalar(out=cb2[:, k, :], in0=onesf, scalar1=cact[:, k:k + 1], scalar2=None, op0=ALU.mult),
               r=[CA, bf("ones")], w=[CB2], extra=(F1 if k == 0 else None))
        ADS2 = [bf("ada2slot0"), bf("ada2slot1")]
        ADB2 = [bf("adab2_0"), bf("adab2_1")]

        def ada_chunk2(n2):
            slot = n2 % 2
            c0 = 2048 + n2 * 256
            op("pool", lambda e: e.dma_start(out=ada_ring2[slot], in_=adaw_d[:, c0:c0 + 256].rearrange("(k p) n -> p k n", p=128)),
               w=[ADS2[slot]], dma_sem=dsem["ada2_%d" % slot], extra=(F1 if n2 < 2 else None))
            op("pool", lambda e: e.dma_start(out=adab2[slot][0:1, :], in_=adab_d[:, c0:c0 + 256]),
               w=[ADB2[slot]], dma_sem=dsem["adb2_%d" % slot], extra=(F1 if n2 < 2 else None))
            for k in range(8):
                op("pe", lambda e, k=k: e.matmul(psum[6][:, 0:256], lhsT=cb2[:, k, :], rhs=ada_ring2[slot][:, k, :],
                                                 start=(k == 0), stop=False), r=[CB2, ADS2[slot]], w=[PB[6]])
            op("pe", lambda e: e.matmul(psum[6][:, 0:256], lhsT=onesb[0:1, :], rhs=adab2[slot][0:1, :], start=False, stop=True),
               r=[ADB2[slot], bf("onesb")], w=[PB[6]])
            name, tile, kind = targets[c0 // 1024]
            dst = tile[:, c0 % 1024: c0 % 1024 + 256]
            if kind == "copy":
                op("dve", lambda e: e.tensor_copy(out=dst, in_=psum[6][:, 0:256]), w=[PB[6], BCB[name]], extra=(F1 if n2 < 2 else None))
            elif kind == "mod":
                op("dve", lambda e: e.scalar_tensor_tensor(out=dst, in0=psum[6][:, 0:256], scalar=1.0, in1=dst, op0=ALU.add, op1=ALU.mult),
                   r=[GB], w=[PB[6], BCB[name]])
            else:
                op("dve", lambda e: e.tensor_tensor(out=dst, in0=psum[6][:, 0:256], in1=dst, op=ALU.mult), r=[GB], w=[PB[6], BCB[name]])

        st2 = dict(sbank=0, abank=0, first=True)

        def p2_qk(hh, c, m):
            rows = slice(m * 64, (m + 1) * 64)
            for j in range(4 * c + 4):
                r_ = j - 4 * c
                off = max(r_, 0) * 128
                bank = (0, 1, 2, 3, 7)[st2["sbank"] % 5]
                st2["sbank"] += 1
                op("pe", lambda e, bank=bank, off=off, j=j, r_=r_: e.matmul(
                    psum[bank][:, off:512], lhsT=kT[rows, hh, j * 128:(j + 1) * 128], rhs=qT[rows, hh, c * 512 + off:(c + 1) * 512],
                    start=True, stop=(r_ < 0)), r=[KTB, QTB], w=[PB[bank]])
                if r_ >= 0:
                    op("pe", lambda e, bank=bank, off=off: e.matmul(psum[bank][:, off:off + 128], lhsT=identb, rhs=negmb,
                                                                     start=False, stop=True), r=[bf("identb"), bf("negmb")], w=[PB[bank]])
                op("act", lambda e, bank=bank, off=off, j=j: e.activation(out=PT[m][:, j, off:512], in_=psum[bank][:, off:512],
                                                                          func=AF.Exp, scale=0.125),
                   w=[PB[bank], PTB[m][j]] + (list(WINB.values()) if st2["first"] else []))
                st2["first"] = False

        def p2_pv(hh, c, m):
            for r_ in range(4):
                i = 4 * c + r_
                bank = 4 + st2["abank"] % 2
                st2["abank"] += 1
                for j in range(i + 1):
                    op("pe", lambda e, bank=bank, j=j, r_=r_, i=i: e.matmul(
                        psum[bank][:, 0:129], lhsT=PT[m][:, j, r_ * 128:(r_ + 1) * 128], rhs=vaug[:, j, hh, 0:129],
                        start=(j == 0), stop=(j == i)), r=[PTB[m][j], VAB], w=[PB[bank]])
                if m == 0:
                    op("dve", lambda e, bank=bank, r_=r_: e.reciprocal(out=rden[:, r_:r_ + 1], in_=psum[bank][:, 128:129]),
                       w=[PB[bank], bf("rden%d" % r_)])
                    op("dve", lambda e, bank=bank, r_=r_: e.tensor_scalar(out=o1[:, r_, :], in0=psum[bank][:, 0:128],
                                                                           scalar1=rden[:, r_:r_ + 1], scalar2=None, op0=ALU.mult),
                       r=[bf("rden%d" % r_)], w=[PB[bank], bf("o1_%d" % r_)])
                else:
                    q = r_
                    odq, mixdq = od4[q], mixd4[q]
                    ODB, MXB, RDB, SSB, RSB = bf("od%d" % q), bf("mixd%d" % q), bf("rdenb%d" % q), bf("ssumb%d" % q), bf("rstdb%d" % q)
                    rd = rden2[:, q:q + 1]
                    ss = ssum2[:, q:q + 1]
                    rs = rstd2[:, q:q + 1]
                    op("dve", lambda e, bank=bank, rd=rd: e.reciprocal(out=rd, in_=psum[bank][:, 128:129]), w=[PB[bank], RDB])
                    op("dve", lambda e, rd=rd: e.tensor_tensor(out=rd, in0=rd, in1=neglam, op=ALU.mult), r=[bf("neglam")], w=[RDB])
                    op("dve", lambda e, bank=bank, r_=r_, rd=rd, odq=odq: e.scalar_tensor_tensor(out=odq, in0=psum[bank][:, 0:128], scalar=rd,
                                                                                              in1=o1[:, r_, :], op0=ALU.mult, op1=ALU.add),
                       r=[RDB, bf("o1_%d" % r_)], w=[PB[bank], ODB])
                    op("dve", lambda e, odq=odq: e.tensor_tensor(out=p2sq, in0=odq, in1=odq, op=ALU.mult), r=[ODB], w=[bf("p2sq")])
                    op("dve", lambda e, ss=ss: e.reduce_sum(out=ss, in_=p2sq, axis=AX.X), r=[bf("p2sq")], w=[bf("ssum2all")])

        def p2_c2(hh, c, m):
            if m == 0:
                return
            op("act", lambda e: e.activation(out=rstd2, in_=ssum2, func=AF.Ln, scale=1.0 / 128, bias=epsb), r=[bf("ssum2all"), bf("epsb")], w=[bf("rstd2all")])
            op("act", lambda e: e.activation(out=rstd2, in_=rstd2, func=AF.Exp, scale=-0.5), r=[bf("rstd2all")], w=[bf("rstd2all")])
            for r_ in range(4):
                op("dve", lambda e, r_=r_: e.scalar_tensor_tensor(out=mixd4[r_], in0=od4[r_], scalar=rstd2[:, r_:r_ + 1], in1=dnormbc,
                                                                   op0=ALU.mult, op1=ALU.mult),
                   r=[bf("od%d" % r_), bf("rstd2all"), bf("dnorm")], w=[bf("mixd%d" % r_)])

        def p2_tr(hh, c, m):
            if m == 0:
                return
            for r_ in range(4):
                op("pe", lambda e, r_=r_: e.transpose(out=pbf(6)[:, r_ * 128:(r_ + 1) * 128], in_=mixd4[r_], identity=identb),
                   r=[bf("mixd%d" % r_), bf("identb")], w=[PB[6]])
            op("dve", lambda e: e.tensor_copy(out=mixT[:, 4 + hh, c * 512:(c + 1) * 512], in_=pbf(6)[:, 0:512]),
               w=[PB[6]] + [MIXB[4 * c + r_] for r_ in range(4)])

        NH = 4 if debug_phase >= 2 else 0
        iters = [(hh, c, m) for hh in range(NH) for c in range(4) for m in range(2)]
        if iters:
            p2_qk(*iters[0])
        for n, it in enumerate(iters):
            if it[2] == 0 and n >= 4:
                ada_chunk2(n // 2 - 2)
            if n in (27, 29):
                ada_chunk2(14 + (n - 27) // 2)
            if n == 8:
                load_w_out()
                op("pool", lambda e: e.dma_start(out=w_up[:, 5, :], in_=wup_d[5 * 128:6 * 128, :]), w=[bf("w_up")], dma_sem=dsem["ffnu"], extra=F1)
                st2["pref"] = {5}
            if n >= 1:
                p2_c2(*iters[n - 1])
            if n + 1 < len(iters):
                p2_qk(*iters[n + 1])
            if n >= 1:
                p2_tr(*iters[n - 1])
            p2_pv(*it)
        if iters:
            p2_c2(*iters[-1])
            p2_tr(*iters[-1])
            op("pool", lambda e: e.dma_start(out=w_up[:, 4, :], in_=wup_d[4 * 128:5 * 128, :]), w=[bf("w_up"), ADS2[0], ADS2[1]],
               dma_sem=dsem["ffnu"])
            st2["pref"] = st2.get("pref", set()) | {4}
        else:
            load_w_out()
        F2 = S.fence()

        WUB = bf("w_up"); WDBS = [bf("w_dn%d" % g) for g in range(8)]
        for k in range(8):
            if k in st2.get("pref", set()):
                continue
            op("pool", lambda e, k=k: e.dma_start(out=w_up[:, k, :], in_=wup_d[k * 128:(k + 1) * 128, :]), w=[WUB], dma_sem=dsem["ffnu"],
               extra=(F2 if k == 0 else None))
        for g in range(4):
            op("pool", lambda e, g=g: e.dma_start(out=w_dn[:, g * 4:(g + 1) * 4, :],
                                                   in_=wdown_d[g * 512:(g + 1) * 512, :].rearrange("(f p) n -> p f n", p=128)),
               w=[WDBS[g]], dma_sem=dsem["ffnd%d" % g])
        P3X = [bf("p3x0"), bf("p3x1"), bf("p3x2")]
        OUTD = [bf("outd%d" % t) for t in range(NT)]
        NP3 = NT if debug_phase >= 3 else 0
        p3x3 = [f32v(198656 + i * 4096, D) for i in range(3)]
        ssq_y = [f32v(SM + 480, 2), f32v(SM + 488, 2)]
        sm_y = f32v(SM + 496, 4)

        def p3_tile(t):
            par = t % 2
            slot = t % 3
            X2 = F2 if t < 3 else None
            xx = p3x3[slot]
            XS = P3X[slot]
            yb = (0, 1) if par == 0 else (2, 3)
            P = lambda n: bf("p3%s_%d" % (n, par))

            def o_(eng, fn, r=(), w=(), **kw):
                return S.op(eng, fn, reads=r, writes=w, extra=X2, **kw)

            o_("sp", lambda e: e.dma_start(out=xx, in_=x_d[t * 128:(t + 1) * 128, :]), w=[XS], dma_sem=dsem["xm%d" % slot])
            for half in range(2):
                for k in range(8):
                    o_("pe", lambda e, half=half, k=k: e.matmul(psum[yb[half]][:], lhsT=mixT[:, k, t * 128:(t + 1) * 128],
                                                                  rhs=w_out[:, k, half * 512:(half + 1) * 512], start=(k == 0), stop=(k == 7)),
                       r=[MIXB[t], WOB], w=[PB[yb[half]]])
                yield 0
            for half in range(2):
                o_("act", lambda e, half=half: e.activation(out=p3junk[:, 0:512], in_=psum[yb[half]][:], func=AF.Square,
                                                            accum_out=ssq_y[par][:, half:half + 1]),
                   w=[PB[yb[half]], bf("p3junk"), P("ssqy")])
            yield 1
            o_("dve", lambda e: e.tensor_tensor(out=sm_y[:, par:par + 1], in0=ssq_y[par][:, 0:1], in1=ssq_y[par][:, 1:2], op=ALU.add),
               r=[P("ssqy")], w=[P("smy")])
            o_("act", lambda e: e.activation(out=sm_y[:, par:par + 1], in_=sm_y[:, par:par + 1], func=AF.Ln, scale=1.0 / D, bias=epsb),
               r=[P("smy"), bf("epsb")], w=[P("smy")])
            o_("act", lambda e: e.activation(out=sm_y[:, par:par + 1], in_=sm_y[:, par:par + 1], func=AF.Exp, scale=-0.5), r=[P("smy")], w=[P("smy")])
            yield 0
            for half in range(2):
                cs = slice(half * 512, (half + 1) * 512)
                o_("dve", lambda e, half=half, cs=cs: e.scalar_tensor_tensor(out=p3u[:, cs], in0=psum[yb[half]][:], scalar=sm_y[:, par:par + 1],
                                                                              in1=gpost_a[:, cs], op0=ALU.mult, op1=ALU.mult),
                   r=[P("smy"), BCB["gpost_a"]], w=[PB[yb[half]], bf("p3u")])
            o_("dve", lambda e: e.tensor_tensor(out=xx, in0=xx, in1=p3u, op=ALU.add), r=[bf("p3u")], w=[XS])
            yield 0
            o_("sp", lambda e: e.dma_start(out=out_d[t * 128:(t + 1) * 128, :], in_=xx), r=[XS], w=[OUTD[t]], dma_sem=dsem["os%d" % slot])
            o_("act", lambda e: e.activation(out=p3junk, in_=xx, func=AF.Square, accum_out=sm_y[:, 2 + par:3 + par]),
               r=[XS], w=[bf("p3junk"), P("ssh")])
            yield 1
            o_("act", lambda e: e.activation(out=sm_y[:, 2 + par:3 + par], in_=sm_y[:, 2 + par:3 + par], func=AF.Ln, scale=1.0 / D, bias=epsb),
               r=[P("ssh"), bf("epsb")], w=[P("ssh")])
            o_("act", lambda e: e.activation(out=sm_y[:, 2 + par:3 + par], in_=sm_y[:, 2 + par:3 + par], func=AF.Exp, scale=-0.5), r=[P("ssh")], w=[P("ssh")])
            yield 0
            o_("dve", lambda e: e.scalar_tensor_tensor(out=p3h1, in0=xx, scalar=sm_y[:, 2 + par:3 + par], in1=gmod_m, op0=ALU.mult, op1=ALU.mult),
               r=[XS, P("ssh"), BCB["gmod_m"]], w=[bf("p3h1")])
            hbp = p3hb2[par]
            o_("dve", lambda e: e.tensor_tensor(out=hbp, in0=p3h1, in1=sh_m, op=ALU.add), r=[bf("p3h1"), BCB["sh_m"]], w=[P("hb")])
            yield 1
            tb = 4 + par
            for k in range(8):
                o_("pe", lambda e, k=k: e.transpose(out=pbf(tb)[:, k * 128:(k + 1) * 128], in_=hbp[:, k * 128:(k + 1) * 128], identity=identb),
                   r=[P("hb"), bf("identb")], w=[PB[tb]])
            o_("act", lambda e: e.copy(out=mixT[:, :, t * 128:(t + 1) * 128], in_=pbf(tb).rearrange("p (k n) -> p k n", k=8)),
               w=[PB[tb], MIXB[t]])
            yield 1

        run_pipeline([p3_tile(t) for t in range(NP3)], 4, newest_first=True)
        F3 = S.fence()

        for g in range(4, 8):
            op("pool", lambda e, g=g: e.dma_start(out=w_dn[:, g * 4:(g + 1) * 4, :],
                                                   in_=wdown_d[g * 512:(g + 1) * 512, :].rearrange("(f p) n -> p f n", p=128)),
               w=[WDBS[g]], dma_sem=dsem["ffnd%d" % g], extra=(F3 if g == 4 else None))
        HB = [bf("hid%d" % f) for f in range(32)]
        XGB = bf("xg")
        RTB = [bf("rt0"), bf("rt1")]
        fin = []
        NG = 8 if debug_phase >= 4 else 0
        ub = 0
        for g in range(NG):
            for tt in range(2):
                t = 2 * g + tt
                op("sp", lambda e, t=t, tt=tt: e.dma_start(out=xg[:, tt, :], in_=out_d[t * 128:(t + 1) * 128, :]), r=[OUTD[t]], w=[XGB],
                   dma_sem=dsem["xg"], extra=(F3 if g == 0 else None))
            for f in range(32):
                bank = 4 + ub % 4
                rt = ub % 2
                ub += 1
                for k in range(8):
                    op("pe", lambda e, bank=bank, f=f, k=k, g=g: e.matmul(psum[bank][:, 0:256], lhsT=w_up[:, k, f * 128:(f + 1) * 128],
                                                                            rhs=mixT[:, k, g * 256:(g + 1) * 256], start=(k == 0), stop=(k == 7)),
                       r=[WUB, MIXB[2 * g], MIXB[2 * g + 1]], w=[PB[bank]])
                op("act", lambda e, bank=bank, rt=rt: e.activation(out=RT[rt], in_=psum[bank][:, 0:256], func=AF.Relu),
                   w=[PB[bank], RTB[rt]], extra=(F3 if (g == 0 and f < 2) else None))
                op("act", lambda e, rt=rt, f=f: e.activation(out=hidT[:, f, :], in_=RT[rt], func=AF.Square),
                   r=[RTB[rt]], w=[HB[f]], extra=(F3 if g == 0 else None))
            for tt in range(2):
                t = 2 * g + tt
                for half in range(2):
                    bank = tt * 2 + half
                    for f in range(32):
                        op("pe", lambda e, bank=bank, f=f, tt=tt, half=half: e.matmul(
                            psum[bank][:], lhsT=hidT[:, f, tt * 128:(tt + 1) * 128], rhs=w_dn[:, f, half * 512:(half + 1) * 512],
                            start=(f == 0), stop=(f == 31)), r=[HB[f], WDBS[f // 4]], w=[PB[bank]])
                for half in range(2):
                    bank = tt * 2 + half
                    op("act", lambda e, bank=bank, half=half: e.activation(out=p4junk, in_=psum[bank][:], func=AF.Square,
                                                                           accum_out=ssq4[:, half:half + 1]),
                       w=[PB[bank], bf("p4junk"), bf("ssq4")], extra=(F3 if g == 0 else None))
                op("dve", lambda e: e.tensor_tensor(out=ssum[:, 0:1], in0=ssq4[:, 0:1], in1=ssq4[:, 1:2], op=ALU.add), r=[bf("ssq4")], w=[bf("ssum")])
                rstd_from(ssum[:, 0:1], rstd[:, 0:1], 1, bf("ssum"), bf("rstd"), 1.0 / D)
                for half in range(2):
                    bank = tt * 2 + half
                    cs = slice(half * 512, (half + 1) * 512)
                    op("dve", lambda e, bank=bank, cs=cs: e.scalar_tensor_tensor(out=p4u[:, cs], in0=psum[bank][:], scalar=rstd[:, 0:1],
                                                                                  in1=gpost_m[:, cs], op0=ALU.mult, op1=ALU.mult),
                       r=[bf("rstd"), BCB["gpost_m"]], w=[PB[bank], bf("p4u")], extra=(F3 if g == 0 else None))
                op("pool", lambda e, tt=tt: e.tensor_tensor(out=xg[:, tt, :], in0=xg[:, tt, :], in1=p4u, op=ALU.add), r=[bf("p4u")], w=[XGB])
                fin.append(op("sp", lambda e, t=t, tt=tt: e.dma_start(out=out_d[t * 128:(t + 1) * 128, :], in_=xg[:, tt, :]), r=[XGB], w=[OUTD[t]],
                              dma_sem=dsem["o%d" % tt]))
        S.wait_all("sp", list(S.fence()))
        stats = S.emit(nc, es)
        build_program.stats = stats
    return nc


_CACHE = {}


def _consts():
    ident = np.eye(128, dtype=np.float32)
    j = np.arange(128)[:, None]
    i = np.arange(128)[None, :]
    tri = (j <= i).astype(np.float32)
    negm = np.where(j > i, -30000.0, 0.0).astype(np.float32)
    inv = (np.float32(1.0) / (np.float32(10000.0) ** (np.arange(0, 64, 2, dtype=np.float32) / np.float32(64)))).astype(np.float32)
    invf = np.concatenate([inv, inv])[None, :].astype(np.float32)
    return ident, tri, negm, invf


def kernel(x, c, positions, ada_w, ada_b, pre_norm_mix, post_norm_mix, w_in, gla_gate_w, gla_gate_b, gla_norm,
           lambda_q1, lambda_k1, lambda_q2, lambda_k2, diff_norm, w_out, pre_norm_mlp, post_norm_mlp, w_up, w_down):
    f = lambda a: np.ascontiguousarray(np.asarray(a, dtype=np.float32))
    x = f(x); c = f(c)
    positions = np.asarray(positions).astype(np.int32)
    nb = x.shape[0]
    if "nc" not in _CACHE:
        _CACHE["nc"] = build_program()
    nc = _CACHE["nc"]
    ident, tri, negm, invf = _consts()
    shared = {
        "ada_w": f(ada_w[0]), "ada_b": f(ada_b[0])[None, :],
        "pre_norm_mix": f(pre_norm_mix[0])[None, :], "post_norm_mix": f(post_norm_mix[0])[None, :],
        "pre_norm_mlp": f(pre_norm_mlp[0])[None, :], "post_norm_mlp": f(post_norm_mlp[0])[None, :],
        "w_in": f(w_in[0]), "gla_gate_w": f(gla_gate_w[0]), "gla_gate_b": f(gla_gate_b[0])[None, :],
        "gla_norm": f(gla_norm[0])[None, :],
        "lambda_q1": f(lambda_q1[0])[None, :], "lambda_k1": f(lambda_k1[0])[None, :],
        "lambda_q2": f(lambda_q2[0])[None, :], "lambda_k2": f(lambda_k2[0])[None, :],
        "diff_norm": f(diff_norm[0])[None, :], "w_out": f(w_out[0]), "w_up": f(w_up[0]), "w_down": f(w_down[0]),
        "k_ident": ident, "k_tri": tri, "k_negmask": negm, "k_invf": invf,
    }
    in_maps = []
    for b in range(nb):
        m = dict(shared)
        m["x"] = x[b]
        m["c8"] = np.ascontiguousarray(c[b].reshape(8, 128).T)
        m["pos"] = np.ascontiguousarray(positions[b].reshape(NT, 128).T)
        in_maps.append(m)
    res = run_bass_kernel_spmd(nc, in_maps, core_ids=list(range(nb)))
    return np.stack([np.asarray(r["out"], dtype=np.float32) for r in res.results], axis=0)
```
